# Optimizing a Trainium2 kernel written in Bass

```python
import jax, jax.numpy as jnp
from jax import lax
import numpy as np

D_MODEL = 4096
BATCH = 8
SEQ = 2048
DEPTH = 2

HEAD_DIM = 128
BLOCK = 128
NEG = -1e30
ATTN_SCALE = HEAD_DIM ** -0.5

A_PAIRS = ((128, 1), (512, 4), (2048, 16))
A_HEADS_PER_PAIR = 4
A_HEADS = A_HEADS_PER_PAIR * len(A_PAIRS)

B_HEADS = 8
B_KV_GROUPS = 2
B_GROUP_SIZE = B_HEADS // B_KV_GROUPS
B_CMP_LEN = 32
B_CMP_STRIDE = 16
B_SEL_LEN = 64
B_SEL_RATIO = B_SEL_LEN // B_CMP_STRIDE
B_SEL_OVERLAP = (1.0, 2.0, 2.0, 2.0, 1.0)
B_N_SEL = 8
B_WINDOW = 512
B_CMP_HIDDEN = 512
B_SEL_QBLOCK = 64

C_HEADS = 8

D_HEADS = 8
D_IDX_HEADS = 8
D_IDX_DIM = 64
D_TOPK = 256

D_FF = 8192
N_BRANCH = 4
ALPHA = (2 * DEPTH) ** 0.25
BETA = (8 * DEPTH) ** -0.25
LN_EPS = 1e-5

N_ALIBI = A_HEADS + B_HEADS + D_HEADS
A_SLOPE_IDX = (0, 1, 2, 3, 12, 13, 14, 15, 24, 25, 26, 27)
B_SLOPE_IDX = (4, 5, 6, 7, 8, 9, 10, 11)
D_SLOPE_IDX = (16, 17, 18, 19, 20, 21, 22, 23)

IN_SIZES = (
    A_HEADS * HEAD_DIM, A_HEADS * HEAD_DIM, A_HEADS * HEAD_DIM,
    B_HEADS * HEAD_DIM, 3 * 2 * B_KV_GROUPS * HEAD_DIM, 3 * B_HEADS,
    C_HEADS * HEAD_DIM, C_HEADS * HEAD_DIM, C_HEADS * HEAD_DIM, C_HEADS,
    D_HEADS * HEAD_DIM, HEAD_DIM, HEAD_DIM,
    D_IDX_HEADS * D_IDX_DIM, D_IDX_DIM, D_IDX_HEADS,
    N_BRANCH * D_MODEL,
)
D_IN = sum(IN_SIZES)
BRANCH_SIZES = (A_HEADS_PER_PAIR * HEAD_DIM, B_HEADS * HEAD_DIM, C_HEADS * HEAD_DIM, D_HEADS * HEAD_DIM)
D_BRANCH = sum(BRANCH_SIZES)

kernel_name = 'hybrid_gated_sparse_attention_trunk'


def _split(t, sizes, axis):
    return jnp.split(t, np.cumsum(sizes)[:-1].tolist(), axis=axis)


def alibi_slopes():
    return jnp.exp2(-8.0 * jnp.arange(1, N_ALIBI + 1, dtype=jnp.float32) / N_ALIBI)


def layer_norm(x, g, b):
    xf = x.astype(jnp.float32)
    mu = jnp.mean(xf, axis=-1, keepdims=True)
    var = jnp.mean(jnp.square(xf - mu), axis=-1, keepdims=True)
    y = (xf - mu) * lax.rsqrt(var + LN_EPS)
    return (y * g.astype(jnp.float32) + b.astype(jnp.float32)).astype(x.dtype)


def swiglu(x, w_gate, w_up, w_down):
    return (jax.nn.silu(x @ w_gate) * (x @ w_up)) @ w_down


def masked_softmax(s, mask):
    s = jnp.where(mask, s.astype(jnp.float32), NEG)
    m = jnp.max(s, axis=-1, keepdims=True)
    e = jnp.where(mask, jnp.exp(s - m), 0.0)
    den = jnp.maximum(jnp.sum(e, axis=-1, keepdims=True), 1e-30)
    return e / den, (m + jnp.log(den))[..., 0]


def banded_attention(q, k, v, window, n_prev, slopes, step):
    B_, N, G, R, Dh = q.shape
    nb = N // BLOCK
    kw = (n_prev + 1) * BLOCK
    qb = q.reshape(B_, nb, BLOCK, G, R, Dh)

    def bands(t):
        tp = jnp.pad(t, ((0, 0), (n_prev * BLOCK, 0), (0, 0), (0, 0)))
        tp = tp.reshape(B_, nb + n_prev, BLOCK, G, Dh)
        return jnp.concatenate([tp[:, i:i + nb] for i in range(n_prev + 1)], axis=2)

    kb, vb = bands(k), bands(v)
    dist = jnp.arange(BLOCK)[:, None] + n_prev * BLOCK - jnp.arange(kw)[None, :]
    key_abs = jnp.arange(nb)[:, None] * BLOCK - n_prev * BLOCK + jnp.arange(kw)[None, :]
    mask = ((dist >= 0) & (dist <= window))[None] & (key_abs >= 0)[:, None, :]
    s = jnp.einsum('bnqgrd,bnkgd->bngrqk', qb, kb).astype(jnp.float32) * ATTN_SCALE
    s = s - slopes[None, None, :, :, None, None] * (step * dist).astype(jnp.float32)
    p, lse = masked_softmax(s, mask[None, :, None, None])
    o = jnp.einsum('bngrqk,bnkgd->bnqgrd', p.astype(v.dtype), vb).reshape(B_, N, G, R, Dh)
    lse = jnp.moveaxis(lse, -1, 2).reshape(B_, N, G, R)
    return o, lse


def dilated_attention(q, k, v, slopes):
    B_, L = q.shape[:2]
    outs, lses = [], []
    for g, (window, dil) in enumerate(A_PAIRS):
        hs = slice(g * A_HEADS_PER_PAIR, (g + 1) * A_HEADS_PER_PAIR)
        n = L // dil
        n_pad = -(-n // BLOCK) * BLOCK

        def sub(t):
            t = t[:, :, hs].reshape(B_, n, dil * A_HEADS_PER_PAIR, HEAD_DIM)
            return jnp.pad(t, ((0, 0), (0, n_pad - n), (0, 0), (0, 0)))

        steps = window // dil
        o, lse = banded_attention(sub(q)[:, :, :, None], sub(k), sub(v), steps, -(-steps // BLOCK),
                                  jnp.tile(slopes[hs], dil)[:, None], dil)
        outs.append(o[:, :n].reshape(B_, L, A_HEADS_PER_PAIR, HEAD_DIM))
        lses.append(lse[:, :n].reshape(B_, L, A_HEADS_PER_PAIR))
    w = jax.nn.softmax(jnp.stack(lses), axis=0)
    o = jnp.sum(w[..., None].astype(q.dtype) * jnp.stack(outs), axis=0)
    return o.reshape(B_, L, A_HEADS_PER_PAIR * HEAD_DIM)


def native_sparse_attention(q, kv, gate_logits, cmp_w1, cmp_w2, cmp_pos, slopes):
    B_, L = q.shape[:2]
    G, R = B_KV_GROUPS, B_GROUP_SIZE
    q = q.reshape(B_, L, G, R, HEAD_DIM)
    kv = kv.reshape(B_, L, 3, 2, G, HEAD_DIM)
    slopes = slopes.reshape(G, R)
    t_pos = jnp.arange(L)

    ratio = B_CMP_LEN // B_CMP_STRIDE
    n_chunk = L // B_CMP_STRIDE
    n_cmp = n_chunk - ratio + 1

    def compress(t, j):
        c = t.reshape(B_, n_chunk, B_CMP_STRIDE, G, HEAD_DIM)
        blk = jnp.concatenate([c[:, i:i + n_cmp] for i in range(ratio)], axis=2) + cmp_pos[j][:, None, :]
        blk = blk.transpose(0, 1, 3, 2, 4).reshape(B_, n_cmp, G, B_CMP_LEN * HEAD_DIM)
        return jax.nn.gelu(blk @ cmp_w1[j]) @ cmp_w2[j]

    k_cmp = compress(kv[:, :, 0, 0], 0)
    v_cmp = compress(kv[:, :, 0, 1], 1)
    dist_c = t_pos[:, None] - (jnp.arange(n_cmp) * B_CMP_STRIDE + B_CMP_LEN - 1)[None, :]
    s = jnp.einsum('blgrd,bngd->bgrln', q, k_cmp).astype(jnp.float32) * ATTN_SCALE
    s = s - slopes[:, :, None, None] * dist_c.astype(jnp.float32)
    p_cmp, _ = masked_softmax(s, dist_c >= 0)
    o_cmp = jnp.einsum('bgrln,bngd->blgrd', p_cmp.astype(q.dtype), v_cmp)

    imp = p_cmp.sum(axis=2)
    n_slc = L // B_SEL_LEN
    imp = jnp.pad(imp, ((0, 0), (0, 0), (0, 0), (1, B_SEL_RATIO * n_slc - n_cmp)))
    p_slc = sum(w * imp[..., o:o + B_SEL_RATIO * (n_slc - 1) + 1:B_SEL_RATIO]
                for o, w in enumerate(B_SEL_OVERLAP))
    blk_j = jnp.arange(n_slc)[None, :]
    cur = (t_pos // B_SEL_LEN)[:, None]
    forced = (blk_j == 0) | (blk_j == cur) | (blk_j == cur - 1)
    score = jnp.where(forced, 1e9, jnp.where(blk_j <= cur, p_slc, -1e9))
    n_sel = min(B_N_SEL, n_slc)
    _, sel = lax.top_k(score, n_sel)

    ks_b = kv[:, :, 1, 0].transpose(0, 2, 1, 3).reshape(B_, G, n_slc, B_SEL_LEN, HEAD_DIM)
    vs_b = kv[:, :, 1, 1].transpose(0, 2, 1, 3).reshape(B_, G, n_slc, B_SEL_LEN, HEAD_DIM)
    gather = jax.vmap(jax.vmap(lambda kk, ii: kk[ii]))
    n_keys = n_sel * B_SEL_LEN

    def sel_block(i):
        qb = lax.dynamic_slice_in_dim(q, i * B_SEL_QBLOCK, B_SEL_QBLOCK, axis=1)
        ib = lax.dynamic_slice_in_dim(sel, i * B_SEL_QBLOCK, B_SEL_QBLOCK, axis=2)
        kg = gather(ks_b, ib)
        vg = gather(vs_b, ib).reshape(B_, G, B_SEL_QBLOCK, n_keys, HEAD_DIM)
        t = i * B_SEL_QBLOCK + jnp.arange(B_SEL_QBLOCK)
        kpos = ib[..., None] * B_SEL_LEN + jnp.arange(B_SEL_LEN)
        dist = t[None, None, :, None, None] - kpos
        s = jnp.einsum('bqgrd,bgqnkd->bgrqnk', qb, kg).astype(jnp.float32) * ATTN_SCALE
        s = s - slopes[None, :, :, None, None, None] * dist[:, :, None].astype(jnp.float32)
        s = s.reshape(B_, G, R, B_SEL_QBLOCK, n_keys)
        mask = (dist >= 0)[:, :, None].reshape(B_, G, 1, B_SEL_QBLOCK, n_keys)
        p, _ = masked_softmax(s, mask)
        return jnp.einsum('bgrqk,bgqkd->bqgrd', p.astype(vg.dtype), vg)

    o_slc = lax.map(sel_block, jnp.arange(L // B_SEL_QBLOCK))
    o_slc = jnp.moveaxis(o_slc, 0, 1).reshape(B_, L, G, R, HEAD_DIM)

    w_steps = B_WINDOW - 1
    o_win, _ = banded_attention(q, kv[:, :, 2, 0], kv[:, :, 2, 1], w_steps, -(-w_steps // BLOCK), slopes, 1)

    g = jax.nn.sigmoid(gate_logits.astype(jnp.float32)).astype(q.dtype).reshape(B_, L, 3, G, R, 1)
    o = g[:, :, 0] * o_cmp + g[:, :, 1] * o_slc + g[:, :, 2] * o_win
    return o.reshape(B_, L, B_HEADS * HEAD_DIM)


def forgetting_attention(q, k, v, f_logit):
    B_, L = q.shape[:2]
    c = jnp.cumsum(jax.nn.log_sigmoid(f_logit.astype(jnp.float32)), axis=1).transpose(0, 2, 1)
    k_pos = jnp.arange(L)

    def block(i):
        qb = lax.dynamic_slice_in_dim(q, i * BLOCK, BLOCK, axis=1)
        cq = lax.dynamic_slice_in_dim(c, i * BLOCK, BLOCK, axis=2)
        s = jnp.einsum('bqhd,bkhd->bhqk', qb, k).astype(jnp.float32) * ATTN_SCALE
        s = s + cq[..., None] - c[:, :, None, :]
        mask = (i * BLOCK + jnp.arange(BLOCK))[:, None] >= k_pos[None, :]
        p, _ = masked_softmax(s, mask)
        return jnp.einsum('bhqk,bkhd->bqhd', p.astype(v.dtype), v)

    o = lax.map(block, jnp.arange(L // BLOCK))
    return jnp.moveaxis(o, 0, 1).reshape(B_, L, -1)


def indexed_sparse_attention(q, k, v, iq, ik, iw, slopes):
    B_, L = q.shape[:2]
    n_top = min(D_TOPK, L // 4)
    k_pos = jnp.arange(L)
    gather = jax.vmap(lambda kk, ii: kk[ii])

    def block(i):
        t = i * BLOCK + jnp.arange(BLOCK)
        qb = lax.dynamic_slice_in_dim(q, i * BLOCK, BLOCK, axis=1)
        iqb = lax.dynamic_slice_in_dim(iq, i * BLOCK, BLOCK, axis=1)
        iwb = lax.dynamic_slice_in_dim(iw, i * BLOCK, BLOCK, axis=1)
        rel = jax.nn.relu(jnp.einsum('bqhd,bkd->bqhk', iqb, ik).astype(jnp.float32))
        score = jnp.einsum('bqh,bqhk->bqk', iwb.astype(jnp.float32), rel)
        score = jnp.where(t[:, None] >= k_pos[None, :], score, NEG)
        _, idx = lax.top_k(score, n_top)
        kg, vg = gather(k, idx), gather(v, idx)
        dist = t[None, :, None] - idx
        s = jnp.einsum('bqhd,bqkd->bqhk', qb, kg).astype(jnp.float32) * ATTN_SCALE
        s = s - slopes[:, None] * dist[:, :, None, :].astype(jnp.float32)
        p, _ = masked_softmax(s, (dist >= 0)[:, :, None, :])
        return jnp.einsum('bqhk,bqkd->bqhd', p.astype(vg.dtype), vg)

    o = lax.map(block, jnp.arange(L // BLOCK))
    return jnp.moveaxis(o, 0, 1).reshape(B_, L, -1)


def hybrid_mixer(x, w_in, b_forget, b_gate, cmp_w1, cmp_w2, cmp_pos, w_branch, w_out):
    B_, L, _ = x.shape

    def heads(t, h):
        return t.reshape(B_, L, h, HEAD_DIM)

    (a_q, a_k, a_v, b_q, b_kv, b_g, c_q, c_k, c_v, c_f,
     d_q, d_k, d_v, d_iq, d_ik, d_iw, g) = _split(x @ w_in, IN_SIZES, -1)
    slopes = alibi_slopes()
    o_a = dilated_attention(heads(a_q, A_HEADS), heads(a_k, A_HEADS), heads(a_v, A_HEADS),
                            slopes[np.array(A_SLOPE_IDX)])
    o_b = native_sparse_attention(b_q, b_kv, b_g, cmp_w1, cmp_w2, cmp_pos, slopes[np.array(B_SLOPE_IDX)])
    o_c = forgetting_attention(heads(c_q, C_HEADS), heads(c_k, C_HEADS), heads(c_v, C_HEADS), c_f + b_forget)
    o_d = indexed_sparse_attention(heads(d_q, D_HEADS), d_k, d_v,
                                   d_iq.reshape(B_, L, D_IDX_HEADS, D_IDX_DIM), d_ik, d_iw,
                                   slopes[np.array(D_SLOPE_IDX)])
    gates = jax.nn.sigmoid((g + b_gate).astype(jnp.float32)).astype(x.dtype).reshape(B_, L, N_BRANCH, D_MODEL)
    w_a, w_b, w_c, w_d = _split(w_branch, BRANCH_SIZES, 0)
    merged = (gates[:, :, 0] * (o_a @ w_a) + gates[:, :, 1] * (o_b @ w_b)
              + gates[:, :, 2] * (o_c @ w_c) + gates[:, :, 3] * (o_d @ w_d))
    return merged @ w_out


def setup_inputs(seed: int = 0) -> dict:
    key = jax.random.key(seed)
    ks = jax.random.split(key, 18)

    def nrm(k, shape, scale):
        return jax.random.normal(k, shape, jnp.float32) * scale

    branch_scale = jnp.asarray(np.repeat(np.array([s ** -0.5 for s in BRANCH_SIZES], np.float32),
                                         BRANCH_SIZES))[:, None] * BETA
    return {
        'x': nrm(ks[0], (BATCH, SEQ, D_MODEL), 1.0),
        'ln_g': 1.0 + nrm(ks[1], (DEPTH, 3, D_MODEL), 0.05),
        'ln_b': nrm(ks[2], (DEPTH, 3, D_MODEL), 0.02),
        'ffn1_w_gate': nrm(ks[3], (DEPTH, D_MODEL, D_FF), D_MODEL ** -0.5),
        'ffn1_w_up': nrm(ks[4], (DEPTH, D_MODEL, D_FF), D_MODEL ** -0.5),
        'ffn1_w_down': nrm(ks[5], (DEPTH, D_FF, D_MODEL), BETA * D_FF ** -0.5),
        'w_in': nrm(ks[6], (DEPTH, D_MODEL, D_IN), D_MODEL ** -0.5),
        'b_forget': jax.random.uniform(ks[7], (DEPTH, C_HEADS), jnp.float32, 1.0, 6.0),
        'b_gate': nrm(ks[8], (DEPTH, N_BRANCH * D_MODEL), 0.02),
        'cmp_w1': nrm(ks[9], (DEPTH, 2, B_CMP_LEN * HEAD_DIM, B_CMP_HIDDEN), (B_CMP_LEN * HEAD_DIM) ** -0.5),
        'cmp_w2': nrm(ks[10], (DEPTH, 2, B_CMP_HIDDEN, HEAD_DIM), (2.0 / B_CMP_HIDDEN) ** 0.5),
        'cmp_pos': nrm(ks[11], (DEPTH, 2, B_CMP_LEN, HEAD_DIM), 0.1),
        'w_branch': nrm(ks[12], (DEPTH, D_BRANCH, D_MODEL), 1.0) * branch_scale,
        'w_out': nrm(ks[13], (DEPTH, D_MODEL, D_MODEL), BETA * D_MODEL ** -0.5),
        'ffn2_w_gate': nrm(ks[14], (DEPTH, D_MODEL, D_FF), D_MODEL ** -0.5),
        'ffn2_w_up': nrm(ks[15], (DEPTH, D_MODEL, D_FF), D_MODEL ** -0.5),
        'ffn2_w_down': nrm(ks[16], (DEPTH, D_FF, D_MODEL), BETA * D_FF ** -0.5),
    }


def reference(x, ln_g, ln_b, ffn1_w_gate, ffn1_w_up, ffn1_w_down, w_in, b_forget, b_gate,
              cmp_w1, cmp_w2, cmp_pos, w_branch, w_out, ffn2_w_gate, ffn2_w_up, ffn2_w_down):
    for l in range(DEPTH):
        x = layer_norm(ALPHA * x + 0.5 * swiglu(x, ffn1_w_gate[l], ffn1_w_up[l], ffn1_w_down[l]),
                       ln_g[l, 0], ln_b[l, 0])
        x = layer_norm(ALPHA * x + hybrid_mixer(x, w_in[l], b_forget[l], b_gate[l], cmp_w1[l], cmp_w2[l],
                                                cmp_pos[l], w_branch[l], w_out[l]),
                       ln_g[l, 1], ln_b[l, 1])
        x = layer_norm(ALPHA * x + 0.5 * swiglu(x, ffn2_w_gate[l], ffn2_w_up[l], ffn2_w_down[l]),
                       ln_g[l, 2], ln_b[l, 2])
    return x
```

```python
import math
from contextlib import ExitStack

import numpy as np
import concourse.bass as bass
import concourse.mybir as mybir
from concourse.bass_utils import run_bass_kernel_spmd

F32 = mybir.dt.float32
BF16 = mybir.dt.bfloat16
AF = mybir.ActivationFunctionType
ALU = mybir.AluOpType
AX = mybir.AxisListType

SB_BASE = 16640
SB_END = 229376

D = 4096
L = 2048
DFF = 8192
DEPTH = 2
KC = D // 128
NT = L // 128
ALPHA = (2 * DEPTH) ** 0.25
SCALE = 128 ** -0.5
EPS = 1e-5
D_IN = 28520
N_ALIBI = 28
SLOPES = [2.0 ** (-8.0 * i / N_ALIBI) for i in range(1, N_ALIBI + 1)]
A_SLOPE_IDX = (0, 1, 2, 3, 12, 13, 14, 15, 24, 25, 26, 27)
B_SLOPE_IDX = (4, 5, 6, 7, 8, 9, 10, 11)
D_SLOPE_IDX = (16, 17, 18, 19, 20, 21, 22, 23)
NEGBIG = -30000.0

C_AQ, C_AK, C_AV = 0, 1536, 3072
C_BQ, C_BKV, C_BG = 4608, 5632, 7168
C_CQ, C_CK, C_CV, C_CF = 7192, 8216, 9240, 10264
C_DQ, C_DK, C_DV, C_DIQ, C_DIK, C_DIW = 10272, 11296, 11424, 11552, 12064, 12128
C_G = 12136


class Buf:
    __slots__ = ("name", "w", "rd")

    def __init__(self, name=""):
        self.name = name
        self.w = None
        self.rd = []


class Op:
    __slots__ = ("eng", "fn", "deps", "dma", "dsem", "signal", "tok", "seq")
    _n = 0

    def __init__(self, eng, fn, dma, dsem):
        Op._n += 1
        self.seq = Op._n
        self.eng = eng
        self.fn = fn
        self.deps = []
        self.dma = dma
        self.dsem = dsem
        self.signal = dma
        self.tok = None


ENGS = ("pe", "act", "dve", "pool", "sp")


class Sched:
    def __init__(self, nc):
        self.nc = nc
        self.ops = {e: [] for e in ENGS}
        self.sb_off = SB_BASE
        self.n_dsem = 0
        self.cur_dsem = 0
        self.n_sw = 0
        self.cur_sw = 0
        self.all_bufs = []
        self.nalloc = 0

    def sb(self, shape, dtype, name=None):
        nbytes = 2 if dtype == BF16 else 4
        per_part = nbytes
        for s in shape[1:]:
            per_part *= s
        off = (self.sb_off + 63) // 64 * 64
        assert off + per_part <= SB_END, f"SBUF overflow {name} {off + per_part - SB_END}"
        self.sb_off = off + per_part
        self.nalloc += 1
        return self.nc.alloc_sbuf_tensor_at(f"{name or 't'}{self.nalloc}", list(shape), dtype, offset=off)

    def sb_mark(self):
        return (self.sb_off, self.cur_dsem, self.cur_sw)

    def sb_reset(self, mark):
        self.sb_off, self.cur_dsem, self.cur_sw = mark

    def buf(self, name=""):
        b = Buf(name)
        self.all_bufs.append(b)
        return b

    def dsem(self, sw=False):
        if sw:
            self.cur_sw += 1
            self.n_sw = max(self.n_sw, self.cur_sw)
            return ("s", self.cur_sw - 1)
        self.cur_dsem += 1
        self.n_dsem = max(self.n_dsem, self.cur_dsem)
        return ("h", self.cur_dsem - 1)

    def add(self, eng, fn, reads=(), writes=(), dma=False, dsem=None):
        if dma:
            assert (dsem[0] == "s") == (eng == "pool"), (eng, dsem)
        op = Op(eng, fn, dma, dsem)
        deps = op.deps
        for b in reads:
            w = b.w
            if w is not None:
                if dma or w.eng != eng or w.dma or eng != "pe":
                    deps.append(w)
            b.rd.append(op)
        for b in writes:
            w = b.w
            if w is not None and (dma or w.dma or w.eng != eng):
                deps.append(w)
            for r in b.rd:
                if r is not op and (dma or r.dma or r.eng != eng):
                    deps.append(r)
            b.w = op
            b.rd = []
        for d in deps:
            d.signal = True
        self.ops[eng].append(op)
        return op

    def barrier(self):
        pend = set()
        for b in self.all_bufs:
            if b.w is not None:
                pend.add(b.w)
            for r in b.rd:
                pend.add(r)
        pend = list(pend)
        for e in ENGS:
            op = Op(e, None, False, None)
            for d in pend:
                if d.eng != e or d.dma:
                    op.deps.append(d)
                    d.signal = True
            self.ops[e].append(op)
        for b in self.all_bufs:
            b.w = None
            b.rd = []
        if len(self.all_bufs) > 20000:
            self.all_bufs = self.all_bufs[-5000:]

    def emit(self):
        nc = self.nc
        with ExitStack() as ctx:
            esem = {e: ctx.enter_context(nc.semaphore(f"s_{e}")) for e in ENGS}
            dsems = {("h", i): ctx.enter_context(nc.semaphore(f"d_{i}")) for i in range(self.n_dsem)}
            dsems.update({("s", i): ctx.enter_context(nc.semaphore(f"w_{i}")) for i in range(self.n_sw)})
            for e in ENGS:
                cnt = 0
                for op in self.ops[e]:
                    if op.dma or not op.signal or op.fn is None:
                        continue
                    cnt += 1
                    op.tok = (esem[e], cnt)
            dcnt = {k: 0 for k in dsems}
            alld = [op for e in ENGS for op in self.ops[e] if op.dma]
            alld.sort(key=lambda o: o.seq)
            import bisect
            dhist = {k: ([], []) for k in dsems}
            for op in alld:
                dcnt[op.dsem] += 16
                op.tok = (dsems[op.dsem], dcnt[op.dsem])
                dhist[op.dsem][0].append(op.seq)
                dhist[op.dsem][1].append(dcnt[op.dsem])
            block = ctx.enter_context(nc.Block())

            def run(e, eng):
                known = {}
                for op in self.ops[e]:
                    need = {}
                    for d in op.deps:
                        if d.tok is None:
                            continue
                        s, v = d.tok
                        if d.dma:
                            seqs, vals = dhist[d.dsem]
                            v = vals[bisect.bisect_left(seqs, op.seq) - 1]
                        k = id(s)
                        if known.get(k, 0) < v and (k not in need or need[k][1] < v):
                            need[k] = (s, v)
                    for k, (s, v) in need.items():
                        eng.wait_ge(s, v)
                        known[k] = v
                    if op.fn is None:
                        continue
                    ins = op.fn(eng)
                    if op.dma:
                        ins.then_inc(op.tok[0], 16)
                    elif op.signal:
                        ins.then_inc(op.tok[0], 1)

            @block.tensor
            def _(eng):
                run("pe", eng)

            @block.scalar
            def _(eng):
                run("act", eng)

            @block.vector
            def _(eng):
                run("dve", eng)

            @block.gpsimd
            def _(eng):
                run("pool", eng)

            @block.sync
            def _(eng):
                run("sp", eng)


class Rot:
    def __init__(self, S, n, shape, dtype, name, dma=False, sw=False):
        self.slots = []
        for i in range(n):
            t = S.sb(shape, dtype, name)
            self.slots.append((t, S.buf(name), S.dsem(sw) if (dma or sw) else None))
        self.i = 0

    def next(self):
        s = self.slots[self.i % len(self.slots)]
        self.i += 1
        return s


CO_ID = 0
CO_D0 = 128
CO_CM = 640
CO_LE = 768
CO_LT = 896
CO_ONE = 1024
CO_DC = 1152
CO_D0R = 1280
CO_CMR = 1792
CO_NLE = 2304
CO_D0P = 2432
CO_N = 2944


def host_consts():
    c = np.zeros((128, CO_N), np.float32)
    s = np.arange(128)[:, None]
    c[:, CO_ID:CO_ID + 128] = np.eye(128)
    c[:, CO_D0:CO_D0 + 512] = np.arange(512)[None, :] - s
    t = np.arange(128)[None, :]
    c[:, CO_CM:CO_CM + 128] = (t >= s)
    c[:, CO_LE:CO_LE + 128] = (t <= s)
    c[:, CO_LT:CO_LT + 128] = (t < s)
    c[:, CO_ONE:CO_ONE + 128] = 1.0
    c[:, CO_DC:CO_DC + 128] = s - 16 * t - 31
    c[:, CO_D0P:CO_D0P + 512] = np.maximum(np.arange(512)[None, :] - s, 0)
    for i in range(4):
        c[:, CO_D0R + i * 128:CO_D0R + (i + 1) * 128] = np.maximum(t - s, 0)
        c[:, CO_CMR + i * 128:CO_CMR + (i + 1) * 128] = (t >= s)
    c[:, CO_NLE:CO_NLE + 128] = ((t <= s) - 1.0) * 1e30
    return c


def host_sel():
    e = np.zeros((32, L), np.float32)
    for j in range(32):
        e[j, j * 64:(j + 1) * 64] = 1.0
    return e


class Prog:
    def __init__(self, spc, dbg=None, stages=None):
        self.spc = spc
        self.dbg = dbg or ()
        self.stages = stages
        nc = bass.Bass("TRN2", target_bir_lowering=False)
        self.nc = nc
        S = Sched(nc)
        self.S = S

        self.declared = []

        def din(name, shape):
            self.declared.append(name)
            return nc.dram_tensor(name, list(shape), F32, kind="ExternalInput").ap()

        self.x = din("x", [spc, L, D])
        self.ln_g = din("ln_g", [DEPTH * 3, D])
        self.ln_b = din("ln_b", [DEPTH * 3, D])
        wshapes = dict((("ffn1_w_gate", [DEPTH, D, DFF]), ("ffn1_w_up", [DEPTH, D, DFF]), ("ffn1_w_down", [DEPTH, DFF, D]),
                        ("w_in", [DEPTH, D, D_IN]), ("b_forget", [DEPTH, 8]), ("b_gate", [DEPTH, 4 * D]),
                        ("cmp_w1", [DEPTH * 2, 4096, 512]), ("cmp_w2", [DEPTH * 2, 512, 128]), ("cmp_pos", [DEPTH * 2, 32, 128]),
                        ("w_branch", [DEPTH, 3584, D]), ("w_out", [DEPTH, D, D]),
                        ("ffn2_w_gate", [DEPTH, D, DFF]), ("ffn2_w_up", [DEPTH, D, DFF]), ("ffn2_w_down", [DEPTH, DFF, D])))

        class LazyW(dict):
            def __missing__(s2, nm):
                s2[nm] = din(nm, wshapes[nm])
                return s2[nm]
        self.w = LazyW()
        self.consts_d = din("consts", [128, CO_N])
        self.sel_d = din("sel", [32, L])
        self.selc_d = din("selc", [128, 16 * 96])
        self.out = nc.dram_tensor("out", [spc, L, D], F32, kind="ExternalOutput").ap()

        def scr(name, shape, dt):
            kind = "ExternalOutput" if name in self.dbg else "Internal"
            return nc.dram_tensor(name, list(shape), dt, kind=kind).ap()

        self.xaT = scr("xaT", [D, L], F32)
        self.zT = scr("zT", [D, L], F32)
        self.hT = scr("hT", [DFF, L], BF16)
        self.pfm = scr("pfm", [70, 128, L], BF16)
        self.bgT = scr("bgT", [24, L], F32)
        self.cfT = scr("cfT", [8, L], F32)
        self.iwtm = scr("iwtm", [L, 8], F32)
        self.ptm = scr("ptm", [L, 3200], BF16)
        self.gatesT = scr("gatesT", [4 * D, L], BF16)
        self.obT = scr("obT", [28, 128, L], BF16)
        self.mergedT = scr("mergedT", [D, L], BF16)
        self.xbT_d = nc.dram_tensor("xbT_d", [D, L], BF16, kind=("ExternalOutput" if "xb" in self.dbg else "Internal")).ap()
        self.csd = scr("csd", [8, L], F32)
        self.bmd = scr("bmd", [2, 32, L], BF16)

        self.ps = nc.alloc_psum_tensor("ps", [128, 8, 512], F32)
        self.psb = [S.buf(f"ps{i}") for i in range(8)]
        self.psi = 0
        self.build()
        S.barrier()
        S.emit()

    def bank(self):
        i = self.psi % 8
        self.psi += 1
        return i

    def dma(self, out, in_, reads, writes, dsem, q="sp"):
        self.S.add(q, lambda e: e.dma_start(out=out, in_=in_), reads=reads, writes=writes, dma=True, dsem=dsem)

    def tr_small(self, dst, src, n, reads, wbuf):
        S = self.S
        ps = self.ps
        bk = self.bank()
        ident = self.cst[:n, CO_ID:CO_ID + n]
        S.add("pe", lambda e: e.transpose(ps[:, bk, :n], src, ident), reads=list(reads) + [self.b_cst], writes=(self.psb[bk],))
        S.add("act", lambda e: e.activation(out=dst, in_=ps[:, bk, :n], func=AF.Copy), reads=(self.psb[bk],), writes=(wbuf,))

    def load_colvec(self, dst, src1d, n, wbuf, tmp, tmpb, ds):
        self.dma(tmp[:n, :], src1d.rearrange("(c p) -> c p", p=128), (), (tmpb,), ds)
        self.tr_small(dst, tmp[:n, :], n, (tmpb,), wbuf)

    def reset_bufs(self):
        S = self.S
        for b in self.persist_bufs + self.psb:
            if b not in S.all_bufs[:64]:
                S.all_bufs.insert(0, b)

    def barrier(self):
        self.S.barrier()
        self.reset_bufs()

    def build(self):
        S = self.S
        self.cst = S.sb([128, CO_N], F32, "cst")
        self.cstb = S.sb([128, CO_N], BF16, "cstb")
        self.mark_noxb = S.sb_mark()
        self.xb = S.sb([128, KC, L], BF16, "xb")
        self.b_cst = S.buf("cst")
        self.b_xb = [S.buf(f"xb{c}") for c in range(KC)]
        self.persist_bufs = [self.b_cst] + self.b_xb
        ds = S.dsem()
        self.dma(self.cst[:], self.consts_d, (), (self.b_cst,), ds)
        S.add("dve", lambda e: e.tensor_copy(out=self.cstb[:], in_=self.cst[:]), reads=(self.b_cst,), writes=(self.b_cst,))
        self.base_mark = S.sb_mark()
        st = self.stages or {}
        layers = st.get("layers", list(range(DEPTH)))
        parts = st.get("parts", ("ffn1", "mixer", "ffn2"))
        for s in range(self.spc):
            if not st.get("skip_input"):
                self.ph_input(s, st.get("in_scale", ALPHA))
            for l in layers:
                if "ffn1" in parts:
                    self.ffn(l, 1)
                if "mixer" in parts:
                    self.mixer(l)
                    if "stop" in st:
                        break
                if "ffn2" in parts:
                    self.ffn(l, 2, final=(l == DEPTH - 1))
            if "xb" in self.dbg:
                dsx = S.dsem()
                for c in range(KC):
                    self.dma(self.xbT_d[c * 128:(c + 1) * 128, :], self.xb[:, c, :], (self.b_xb[c],), (), dsx)
            if not st:
                self.ph_output(s)

    def ph_input(self, s, in_scale=ALPHA):
        S = self.S
        nc = self.nc
        mark = S.sb_mark()
        xt = Rot(S, 8, [128, 1024], F32, "xin", dma=True)
        st_a = Rot(S, 3, [128, 512], F32, "xast", dma=True)
        ident = self.cst[:, CO_ID:CO_ID + 128]
        ps = self.ps
        for tg in range(4):
            for fq in range(4):
                tiles = []
                for j in range(4):
                    tl, bf, ds = xt.next()
                    tt = tg * 4 + j
                    self.dma(tl[:], self.x[s, tt * 128:(tt + 1) * 128, fq * 1024:(fq + 1) * 1024], (), (bf,), ds)
                    tiles.append((tl, bf))
                for ci in range(8):
                    c = fq * 8 + ci
                    bk = self.bank()

                    def tr(e, tiles=tiles, ci=ci, bk=bk):
                        for j, (tl, bf) in enumerate(tiles):
                            ins = e.transpose(ps[:, bk, j * 128:(j + 1) * 128], tl[:, ci * 128:(ci + 1) * 128], ident)
                        return ins
                    S.add("pe", tr, reads=[b for _, b in tiles] + [self.b_cst], writes=(self.psb[bk],))
                    sa, sab, sads = st_a.next()
                    S.add("act", lambda e, sa=sa, bk=bk: e.activation(out=sa[:], in_=ps[:, bk, :], func=AF.Copy, scale=float(in_scale)),
                          reads=(self.psb[bk],), writes=(sab,))
                    self.dma(self.xaT[c * 128:(c + 1) * 128, tg * 512:(tg + 1) * 512], sa[:], (sab,), (), sads)
                    S.add("dve", lambda e, c=c, tg=tg, sa=sa: e.tensor_scalar(out=self.xb[:, c, tg * 512:(tg + 1) * 512], in0=sa[:], scalar1=1.0 / float(in_scale),
                                                                         scalar2=None, op0=ALU.mult),
                          reads=(sab,), writes=(self.b_xb[c],))
        self.barrier()
        S.sb_reset(mark)

    def ph_output(self, s):
        S = self.S
        mark = S.sb_mark()
        zi = Rot(S, 6, [128, L], F32, "oin", dma=True)
        so = Rot(S, 3, [128, 512], F32, "oout", dma=True)
        ident = self.cst[:, CO_ID:CO_ID + 128]
        ps = self.ps
        for cg in range(8):
            tiles = []
            for j in range(4):
                c = cg * 4 + j
                tl, bf, ds = zi.next()
                self.dma(tl[:], self.zT[c * 128:(c + 1) * 128, :], (), (bf,), ds)
                tiles.append((tl, bf))
            for tt in range(NT):
                bk = self.bank()

                def tr(e, tiles=tiles, tt=tt, bk=bk):
                    for j, (tl, bf) in enumerate(tiles):
                        ins = e.transpose(ps[:, bk, j * 128:(j + 1) * 128], tl[:, tt * 128:(tt + 1) * 128], ident)
                    return ins
                S.add("pe", tr, reads=[b for _, b in tiles] + [self.b_cst], writes=(self.psb[bk],))
                so_t, sob, sods = so.next()
                if tt % 2 == 0:
                    S.add("dve", lambda e, so_t=so_t, bk=bk: e.tensor_copy(out=so_t[:], in_=ps[:, bk, :]), reads=(self.psb[bk],), writes=(sob,))
                else:
                    S.add("act", lambda e, so_t=so_t, bk=bk: e.activation(out=so_t[:], in_=ps[:, bk, :], func=AF.Copy), reads=(self.psb[bk],), writes=(sob,))
                self.dma(self.out[s, tt * 128:(tt + 1) * 128, cg * 512:(cg + 1) * 512], so_t[:], (sob,), (), sods)
        self.barrier()
        S.sb_reset(mark)

    def ffn(self, l, which, final=False, seq=0):
        wg = self.w[f"ffn{which}_w_gate"][l]
        wu = self.w[f"ffn{which}_w_up"][l]
        wd = self.w[f"ffn{which}_w_down"][l]
        fp = (self.stages or {}).get("ffn_parts", ("up", "down", "ln"))
        if "up" in fp:
            self.ph_up(wg, wu)
        if "down" in fp:
            self.ph_down(self.hT, wd, DFF // 128, 0.5)
        if "ln" in fp:
            self.ph_ln(l * 3 + (0 if which == 1 else 2), final)

    def ph_up(self, wg, wu):
        S = self.S
        ps = self.ps
        mark = S.sb_mark()
        CG = 128
        wsl = {"g": Rot(S, 3, [128, KC, CG], BF16, "wg", sw=True), "u": Rot(S, 3, [128, KC, CG], BF16, "wu", sw=True)}
        sg = Rot(S, 2, [128, 512], F32, "sg")
        hb = Rot(S, 3, [128, 512], BF16, "hb", dma=True)
        xb = self.xb
        import os
        for cg in range(int(os.environ.get('UPN', DFF // CG))):
            cur = {}
            for nm, w in (("g", wg), ("u", wu)):
                tl, bf, ds = wsl[nm].next()
                self.dma(tl[:], w[:, cg * CG:(cg + 1) * CG].rearrange("(kc p) n -> p kc n", p=128), (), (bf,), ds, q="pool")
                cur[nm] = (tl, bf)
            for mi in range(CG // 128):
                m = cg * (CG // 128) + mi
                for th in range(2):
                    banks = [self.bank() for _ in range(4)]
                    for j, nm in enumerate(("g", "u")):
                        tl, bf = cur[nm]
                        for tt in range(2):
                            bk = banks[j * 2 + tt]
                            t = th * 2 + tt

                            def mm(e, tl=tl, bk=bk, t=t, mi=mi):
                                for k in range(KC):
                                    ins = e.matmul(ps[:, bk, :], tl[:, k, mi * 128:(mi + 1) * 128], xb[:, k, t * 512:(t + 1) * 512],
                                                   start=(k == 0), stop=(k == KC - 1))
                                return ins
                            S.add("pe", mm, reads=[bf] + self.b_xb, writes=(self.psb[bk],))
                    for tt in range(2):
                        t = th * 2 + tt
                        sgt, sgb, _ = sg.next()
                        hbt, hbb, hds = hb.next()
                        bg, bu = banks[tt], banks[2 + tt]
                        S.add("act", lambda e, sgt=sgt, bg=bg: e.activation(out=sgt[:], in_=ps[:, bg, :], func=AF.Silu),
                              reads=(self.psb[bg],), writes=(sgb,))
                        S.add("dve", lambda e, hbt=hbt, sgt=sgt, bu=bu: e.tensor_tensor(out=hbt[:], in0=sgt[:], in1=ps[:, bu, :], op=ALU.mult),
                              reads=(sgb, self.psb[bu]), writes=(hbb,))
                        self.dma(self.hT[m * 128:(m + 1) * 128, t * 512:(t + 1) * 512], hbt[:], (hbb,), (), hds)
        self.barrier()
        S.sb_reset(mark)

    def ph_down(self, srcT, wd, kc, mul):
        S = self.S
        ps = self.ps
        mark = S.sb_mark()
        S.sb_reset(self.mark_noxb)
        TH = 1024 if kc > 32 else 2048
        npass = L // TH
        ntt = TH // 512
        hs = S.sb([128, kc, TH], BF16, "hs")
        hsb = [S.buf("hs") for _ in range(kc)]
        hds = S.dsem()
        wsl = Rot(S, 2, [128, kc, 128], BF16, "wd", sw=True)
        xat = Rot(S, 3, [128, 512], F32, "xat", dma=True)
        zst = Rot(S, 3, [128, 512], F32, "zst", dma=True)
        for p in range(npass):
            for k in range(kc):
                self.dma(hs[:, k, :], srcT[k * 128:(k + 1) * 128, p * TH:(p + 1) * TH], (), (hsb[k],), hds)
            for m in range(D // 128):
                tl, bf, ds = wsl.next()
                nsplit = 4
                ks = kc // nsplit
                for q in range(nsplit):
                    self.dma(tl[:, q * ks:(q + 1) * ks, :],
                             wd[q * ks * 128:(q + 1) * ks * 128, m * 128:(m + 1) * 128].rearrange("(kc p) n -> p kc n", p=128),
                             (), (bf,), ds, q="pool")
                for tt in range(ntt):
                    t0 = p * TH + tt * 512
                    bk = self.bank()

                    def mm(e, tl=tl, bk=bk, tt=tt):
                        for k in range(kc):
                            ins = e.matmul(ps[:, bk, :], tl[:, k, :], hs[:, k, tt * 512:(tt + 1) * 512], start=(k == 0), stop=(k == kc - 1))
                        return ins
                    S.add("pe", mm, reads=[bf] + hsb, writes=(self.psb[bk],))
                    xa, xab, xads = xat.next()
                    self.dma(xa[:], self.xaT[m * 128:(m + 1) * 128, t0:t0 + 512], (), (xab,), xads)
                    z, zb, zds = zst.next()
                    S.add("dve", lambda e, z=z, xa=xa, bk=bk: e.scalar_tensor_tensor(out=z[:], in0=ps[:, bk, :], scalar=float(mul), in1=xa[:],
                                                                                  op0=ALU.mult, op1=ALU.add),
                          reads=(self.psb[bk], xab), writes=(zb,))
                    self.dma(self.zT[m * 128:(m + 1) * 128, t0:t0 + 512], z[:], (zb,), (), zds)
        self.barrier()
        S.sb_reset(mark)

    def ph_ln(self, idx, final):
        S = self.S
        ps = self.ps
        mark = S.sb_mark()
        H = L // 2
        zin = Rot(S, 2, [128, H], F32, "zin", dma=True)
        acc1 = S.sb([128, H], F32, "acc1")
        acc2 = S.sb([128, H], F32, "acc2")
        b_a1, b_a2 = S.buf("a1"), S.buf("a2")
        sq = Rot(S, 2, [128, H], F32, "sq")
        gb = S.sb([128, 4, KC], F32, "gb")
        b_gb = S.buf("gb")
        gds = S.dsem()
        cvt = S.sb([32, 2, 128], F32, "cvtmp")
        b_cvt = S.buf("cvt")
        self.load_colvec(gb[:, 0, :], self.ln_g[idx], KC, b_gb, cvt[:, 0, :], b_cvt, gds)
        self.load_colvec(gb[:, 1, :], self.ln_b[idx], KC, b_gb, cvt[:, 1, :], b_cvt, gds)
        S.add("dve", lambda e: e.tensor_scalar(out=gb[:, 2:4, :], in0=gb[:, 0:2, :], scalar1=float(ALPHA), scalar2=None, op0=ALU.mult),
              reads=(b_gb,), writes=(b_gb,))
        ones = self.cst[:, CO_ONE:CO_ONE + 128]
        mu = S.sb([128, H], F32, "mu")
        rs = S.sb([128, H], F32, "rs")
        b_mu, b_rs = S.buf("mu"), S.buf("rs")
        yt = Rot(S, 2, [128, H], F32, "yt")
        xo = Rot(S, 2, [128, H], F32, "xo", dma=True)
        for hf in range(2):
            hs_ = slice(hf * H, (hf + 1) * H)
            for c in range(KC):
                z, zb, zds = zin.next()
                self.dma(z[:], self.zT[c * 128:(c + 1) * 128, hs_], (), (zb,), zds)
                s_t, s_b, _ = sq.next()
                S.add("act", lambda e, s_t=s_t, z=z: e.activation(out=s_t[:], in_=z[:], func=AF.Square), reads=(zb,), writes=(s_b,))
                if c == 0:
                    S.add("pool", lambda e, z=z: e.tensor_copy(out=acc1[:], in_=z[:]), reads=(zb,), writes=(b_a1,))
                    S.add("dve", lambda e, s_t=s_t: e.tensor_copy(out=acc2[:], in_=s_t[:]), reads=(s_b,), writes=(b_a2,))
                else:
                    S.add("pool", lambda e, z=z: e.tensor_tensor(out=acc1[:], in0=acc1[:], in1=z[:], op=ALU.add), reads=(zb, b_a1), writes=(b_a1,))
                    S.add("dve", lambda e, s_t=s_t: e.tensor_tensor(out=acc2[:], in0=acc2[:], in1=s_t[:], op=ALU.add), reads=(s_b, b_a2), writes=(b_a2,))
            for tt in range(H // 512):
                b1, b2 = self.bank(), self.bank()
                sl = slice(tt * 512, (tt + 1) * 512)
                S.add("pe", lambda e, b1=b1, sl=sl: e.matmul(ps[:, b1, :], ones, acc1[:, sl], start=True, stop=True),
                      reads=(b_a1, self.b_cst), writes=(self.psb[b1],))
                S.add("pe", lambda e, b2=b2, sl=sl: e.matmul(ps[:, b2, :], ones, acc2[:, sl], start=True, stop=True),
                      reads=(b_a2, self.b_cst), writes=(self.psb[b2],))
                S.add("act", lambda e, b1=b1, sl=sl: e.activation(out=mu[:, sl], in_=ps[:, b1, :], func=AF.Copy, scale=1.0 / D),
                      reads=(self.psb[b1],), writes=(b_mu,))
                S.add("dve", lambda e, sl=sl: e.tensor_tensor(out=rs[:, sl], in0=mu[:, sl], in1=mu[:, sl], op=ALU.mult),
                      reads=(b_mu,), writes=(b_rs,))
                S.add("dve", lambda e, b2=b2, sl=sl: e.scalar_tensor_tensor(out=rs[:, sl], in0=ps[:, b2, :], scalar=1.0 / D, in1=rs[:, sl],
                                                                         op0=ALU.mult, op1=ALU.subtract),
                      reads=(self.psb[b2], b_rs), writes=(b_rs,))
                S.add("dve", lambda e, sl=sl: e.tensor_scalar(out=rs[:, sl], in0=rs[:, sl], scalar1=float(EPS), scalar2=None, op0=ALU.add),
                      reads=(b_rs,), writes=(b_rs,))
                S.add("act", lambda e, sl=sl: e.activation(out=rs[:, sl], in_=rs[:, sl], func=AF.Sqrt), reads=(b_rs,), writes=(b_rs,))
                S.add("dve", lambda e, sl=sl: e.reciprocal(out=rs[:, sl], in_=rs[:, sl]), reads=(b_rs,), writes=(b_rs,))
            for c in range(KC):
                z, zb, zds = zin.next()
                self.dma(z[:], self.zT[c * 128:(c + 1) * 128, hs_], (), (zb,), zds)
                y, yb, _ = yt.next()
                S.add("pool", lambda e, y=y, z=z: e.tensor_tensor(out=y[:], in0=z[:], in1=mu[:], op=ALU.subtract), reads=(zb, b_mu), writes=(yb,))
                S.add("dve", lambda e, y=y: e.tensor_tensor(out=y[:], in0=y[:], in1=rs[:], op=ALU.mult), reads=(yb, b_rs), writes=(yb,))
                S.add("act", lambda e, y=y, c=c, hs_=hs_: e.activation(out=self.xb[:, c, hs_], in_=y[:], func=AF.Identity, bias=gb[:, 1, c:c + 1], scale=gb[:, 0, c:c + 1]),
                      reads=(yb, b_gb), writes=(self.b_xb[c],))
                o, ob, ods = xo.next()
                if final:
                    S.add("act", lambda e, y=y, o=o, c=c: e.activation(out=o[:], in_=y[:], func=AF.Identity, bias=gb[:, 1, c:c + 1], scale=gb[:, 0, c:c + 1]),
                          reads=(yb, b_gb), writes=(ob,))
                    self.dma(self.zT[c * 128:(c + 1) * 128, hs_], o[:], (ob, zb), (), ods)
                else:
                    S.add("act", lambda e, y=y, o=o, c=c: e.activation(out=o[:], in_=y[:], func=AF.Identity, bias=gb[:, 3, c:c + 1], scale=gb[:, 2, c:c + 1]),
                          reads=(yb, b_gb), writes=(ob,))
                    self.dma(self.xaT[c * 128:(c + 1) * 128, hs_], o[:], (ob,), (), ods)
        self.barrier()
        S.sb_reset(mark)

    def mixer(self, l):
        self.ph_proj(l)
        st = self.stages or {}
        if st.get("stop") == ("proj", l):
            return
        self.S.sb_reset(self.mark_noxb)
        import os
        mx = os.environ.get("MIX", "acdb")
        if "a" in mx:
            self.mix_a()
        if "c" in mx:
            self.mix_c()
        if "d" in mx:
            self.mix_d()
        if "b" in mx:
            self.mix_b(l)
        self.S.sb_reset(self.base_mark)
        if st.get("stop") == ("attn", l):
            return
        self.ph_merge(l)
        self.ph_down(self.mergedT, self.w["w_out"][l], KC, 1.0)
        self.ph_ln(l * 3 + 1, False)

    def ph_proj(self, l):
        S = self.S
        ps = self.ps
        xb = self.xb
        w = self.w["w_in"][l]
        mark = S.sb_mark()
        wsl = Rot(S, 2, [128, KC, 256], BF16, "wfm", sw=True)
        stg = Rot(S, 3, [128, L], BF16, "pst", dma=True)
        bgate = S.sb([128, 128], F32, "bgate")
        b_bg = S.buf("bgate")
        ds0 = S.dsem()
        cvt = S.sb([128, 128], F32, "cvtmp")
        b_cvt = S.buf("cvt")
        self.load_colvec(bgate[:], self.w["b_gate"][l], 128, b_bg, cvt, b_cvt, ds0)
        bfor = S.sb([8, 1], F32, "bfor")
        self.dma(bfor[:], self.w["b_forget"][l].rearrange("(p o) -> p o", o=1), (b_bg,), (b_bg,), ds0)
        sm = S.sb([128, L], F32, "smallst")
        b_sm = S.buf("sm")
        dsm = S.dsem()

        def fm_block(tl, bf, bi, ncol, kind, dst, arg=None):
            st_t, st_b, st_ds = (None, None, None)
            if kind in ("q", "k", "gate"):
                st_t, st_b, st_ds = stg.next()
            for t in range(4):
                bk = self.bank()

                def mm(e, tl=tl, bk=bk, t=t, bi=bi, ncol=ncol):
                    for k in range(KC):
                        ins = e.matmul(ps[:ncol, bk, :], tl[:, k, bi * 128:bi * 128 + ncol], xb[:, k, t * 512:(t + 1) * 512],
                                       start=(k == 0), stop=(k == KC - 1))
                    return ins
                S.add("pe", mm, reads=[bf] + self.b_xb, writes=(self.psb[bk],))
                sl = slice(t * 512, (t + 1) * 512)
                if kind == "q":
                    S.add("act", lambda e, st_t=st_t, bk=bk, sl=sl: e.activation(out=st_t[:, sl], in_=ps[:, bk, :], func=AF.Copy, scale=SCALE),
                          reads=(self.psb[bk],), writes=(st_b,))
                elif kind == "k":
                    S.add("dve", lambda e, st_t=st_t, bk=bk, sl=sl: e.tensor_copy(out=st_t[:, sl], in_=ps[:, bk, :]),
                          reads=(self.psb[bk],), writes=(st_b,))
                elif kind == "gate":
                    S.add("act", lambda e, st_t=st_t, bk=bk, sl=sl, arg=arg: e.activation(out=st_t[:, sl], in_=ps[:, bk, :], func=AF.Sigmoid,
                                                                                       bias=bgate[:, arg:arg + 1]),
                          reads=(self.psb[bk], b_bg), writes=(st_b,))
                elif kind == "bg":
                    S.add("act", lambda e, bk=bk, sl=sl: e.activation(out=sm[:24, sl], in_=ps[:24, bk, :], func=AF.Sigmoid),
                          reads=(self.psb[bk],), writes=(b_sm,))
                elif kind == "cf":
                    S.add("act", lambda e, bk=bk, sl=sl: e.activation(out=sm[:8, sl], in_=ps[:8, bk, :], func=AF.Identity, bias=bfor[:, 0:1]),
                          reads=(self.psb[bk], b_bg), writes=(b_sm,))
            if kind in ("q", "k", "gate"):
                self.dma(dst, st_t[:], (st_b,), (), st_ds)
            elif kind == "bg":
                self.dma(self.bgT, sm[:24, :], (b_sm,), (), dsm)
            elif kind == "cf":
                self.dma(self.cfT, sm[:8, :], (b_sm,), (), dsm)

        def load(col0, ncols, dup=False):
            tl, bf, ds = wsl.next()
            if dup:
                for h in range(2):
                    self.dma(tl[:, :, h * 64:(h + 1) * 64], w[:, col0:col0 + 64].rearrange("(kc p) n -> p kc n", p=128), (), (bf,), ds, q="pool")
            else:
                self.dma(tl[:, :, :ncols], w[:, col0:col0 + ncols].rearrange("(kc p) n -> p kc n", p=128), (), (bf,), ds, q="pool")
            return tl, bf

        def fm_range(col0, nblk, blk0, kind):
            b = 0
            while b < nblk:
                n = min(2, nblk - b)
                tl, bf = load(col0 + b * 128, n * 128)
                for i in range(n):
                    fm_block(tl, bf, i, 128, kind, self.pfm[blk0 + b + i])
                b += n

        fm_range(C_AQ, 12, 0, "q")
        fm_range(C_AK, 12, 12, "k")
        fm_range(C_BQ, 8, 24, "q")
        fm_range(C_BKV + 0, 2, 32, "k")
        fm_range(C_BKV + 256, 2, 34, "k")
        fm_range(C_BKV + 512, 2, 36, "k")
        fm_range(C_BKV + 1024, 2, 38, "k")
        fm_range(C_CQ, 8, 40, "q")
        fm_range(C_CK, 8, 48, "k")
        fm_range(C_DQ, 8, 56, "q")
        fm_range(C_DK, 1, 64, "k")
        fm_range(C_DIQ, 4, 65, "k")
        tl, bf = load(C_DIK, 64, dup=True)
        fm_block(tl, bf, 0, 128, "k", self.pfm[69])
        tl, bf = load(C_BG, 24)
        fm_block(tl, bf, 0, 24, "bg", None)
        tl, bf = load(C_CF, 8)
        fm_block(tl, bf, 0, 8, "cf", None)
        for gb in range(64):
            tl, bf = load(C_G + gb * 256, 256)
            for i in range(2):
                blk = gb * 2 + i
                fm_block(tl, bf, i, 128, "gate", self.gatesT[blk * 128:(blk + 1) * 128, :], arg=blk)
        self.barrier()
        S.sb_reset(mark)
        wtl = Rot(S, 2, [128, KC, 256], BF16, "wtm", sw=True)
        tst = Rot(S, 3, [128, 512], BF16, "tst", dma=True)
        ist = Rot(S, 2, [128, 8], F32, "ist", dma=True)
        tmjobs = [(C_AV + i * 256, 256, i * 256) for i in range(6)] + [(C_BKV + 6 * 128, 256, 1536), (C_BKV + 10 * 128, 256, 1792)] + \
                 [(C_CV + i * 256, 256, 2048 + i * 256) for i in range(4)] + [(C_DV, 128, 3072), (C_DIW, 8, -1)]
        for (col0, ncols, dcol) in tmjobs:
            tl, bf, ds = wtl.next()
            self.dma(tl[:, :, :ncols], w[:, col0:col0 + ncols].rearrange("(kc p) n -> p kc n", p=128), (), (bf,), ds, q="pool")
            for tt in range(NT):
                bk = self.bank()

                def mm(e, tl=tl, bk=bk, tt=tt, ncols=ncols):
                    for k in range(KC):
                        ins = e.matmul(ps[:, bk, :ncols], xb[:, k, tt * 128:(tt + 1) * 128], tl[:, k, :ncols], start=(k == 0), stop=(k == KC - 1))
                    return ins
                S.add("pe", mm, reads=[bf] + self.b_xb, writes=(self.psb[bk],))
                if dcol >= 0:
                    o, ob, ods = tst.next()
                    if tt % 2 == 0:
                        S.add("dve", lambda e, o=o, bk=bk, ncols=ncols: e.tensor_copy(out=o[:, :ncols], in_=ps[:, bk, :ncols]), reads=(self.psb[bk],), writes=(ob,))
                    else:
                        S.add("act", lambda e, o=o, bk=bk, ncols=ncols: e.activation(out=o[:, :ncols], in_=ps[:, bk, :ncols], func=AF.Copy), reads=(self.psb[bk],), writes=(ob,))
                    self.dma(self.ptm[tt * 128:(tt + 1) * 128, dcol:dcol + ncols], o[:, :ncols], (ob,), (), ods)
                else:
                    o, ob, ods = ist.next()
                    S.add("dve", lambda e, o=o, bk=bk: e.tensor_copy(out=o[:], in_=ps[:, bk, :8]), reads=(self.psb[bk],), writes=(ob,))
                    self.dma(self.iwtm[tt * 128:(tt + 1) * 128, :], o[:], (ob,), (), ods)
        self.barrier()
        S.sb_reset(mark)

    def attn_job(self, units, evac, ptile, tmpt):
        S = self.S
        ps = self.ps
        bo, bd = self.bank(), self.bank()
        ones = self.cstb[:, CO_ONE:CO_ONE + 128]
        first = True
        for u in units:
            nq, c0 = u["nq"], u["c0"]
            bs = self.bank()
            while bs in (bo, bd):
                bs = self.bank()
            rd = list(u["reads"])
            S.add("pe", lambda e, u=u, bs=bs, nq=nq: e.matmul(ps[:, bs, :nq], u["kT"], u["qT"], start=True, stop=True),
                  reads=rd, writes=(self.psb[bs],))
            p, pb, _ = ptile.next()
            bias = u.get("bias")
            cb = float(u.get("cb", 0.0))
            if bias is None:
                S.add("act", lambda e, p=p, bs=bs, nq=nq, cb=cb: e.activation(out=p[:, :nq], in_=ps[:, bs, :nq], func=AF.Exp, bias=cb),
                      reads=(self.psb[bs],), writes=(pb,))
            else:
                tm, tmb, _ = tmpt.next()
                if bias[0] == "alibi":
                    slope, d0 = bias[1], bias[2]
                    S.add("dve", lambda e, tm=tm, bs=bs, nq=nq, slope=slope, d0=d0: e.scalar_tensor_tensor(
                        out=tm[:, :nq], in0=d0, scalar=-float(slope), in1=ps[:, bs, :nq], op0=ALU.mult, op1=ALU.add),
                        reads=(self.psb[bs], self.b_cst), writes=(tmb,))
                else:
                    csT, csbc, brd = bias[1], bias[2], bias[3]
                    S.add("dve", lambda e, tm=tm, bs=bs, nq=nq, csT=csT, csbc=csbc: e.scalar_tensor_tensor(
                        out=tm[:, :nq], in0=ps[:, bs, :nq], scalar=csT, in1=csbc, op0=ALU.add, op1=ALU.subtract),
                        reads=[self.psb[bs]] + list(brd), writes=(tmb,))
                S.add("act", lambda e, p=p, tm=tm, nq=nq, cb=cb: e.activation(out=p[:, :nq], in_=tm[:, :nq], func=AF.Exp, bias=cb),
                      reads=(tmb,), writes=(pb,))
            for (mo, mw, map_) in u.get("masks", ()):
                S.add("pool", lambda e, p=p, mo=mo, mw=mw, map_=map_: e.tensor_tensor(out=p[:, mo:mo + mw], in0=p[:, mo:mo + mw], in1=map_, op=ALU.mult),
                      reads=(pb, self.b_cst), writes=(pb,))
            full = u.get("full")
            if full is not None:
                fap, fb = full
                S.add("pool", lambda e, p=p, nq=nq, fap=fap: e.tensor_tensor(out=p[:, :nq], in0=p[:, :nq], in1=fap, op=ALU.mult),
                      reads=(pb, fb), writes=(pb,))

            def pv(e, u=u, p=p, nq=nq, c0=c0, first=first):
                e.matmul(ps[:, bo, c0:c0 + nq], u["v"], p[:, :nq], start=first, stop=False, skip_group_check=True)
                return e.matmul(ps[:, bd, c0:c0 + nq], ones, p[:, :nq], start=first, stop=False, skip_group_check=True)
            S.add("pe", pv, reads=[pb, self.b_cst] + list(u["vreads"]), writes=(self.psb[bo], self.psb[bd]))
            first = False
        evac(bo, bd)

    def evac_norm(self, dst, dstb, c0=0, n=512):
        S = self.S
        ps = self.ps

        def f(bo, bd):
            r, rb, _ = self.rden.next()
            S.add("dve", lambda e, r=r, bd=bd: e.reciprocal(out=r[:, :n], in_=ps[:, bd, :n]), reads=(self.psb[bd],), writes=(rb,))
            S.add("dve", lambda e, r=r, bo=bo: e.tensor_tensor(out=dst[:, c0:c0 + n], in0=ps[:, bo, :n], in1=r[:, :n], op=ALU.mult),
                  reads=(self.psb[bo], rb), writes=(dstb,))
        return f

    def causal_units(self, qg, kT, qT, vt, bias_fn, reads, vreads, n_prev=None, full_fn=None):
        CM = self.cstb[:, CO_CM:CO_CM + 128]
        LT = self.cstb[:, CO_LT:CO_LT + 128]
        LE = self.cstb[:, CO_LE:CO_LE + 128]
        units = []
        k_lo = 0 if n_prev is None else max(0, 4 * qg - n_prev)
        for kt in range(k_lo, 4 * qg + 4):
            q_lo = max(kt, 4 * qg)
            q_hi = 4 * qg + 3 if n_prev is None else min(kt + n_prev, 4 * qg + 3)
            c0 = (q_lo - 4 * qg) * 128
            nq = (q_hi - q_lo + 1) * 128
            r0 = q_lo - kt
            masks = []
            if q_lo == kt:
                masks.append((0, 128, CM))
            if n_prev is not None and q_hi == kt + n_prev:
                masks.append((nq - 128, 128, LE if n_prev == 1 else LT))
            u = dict(kT=kT[:, kt * 128:(kt + 1) * 128], qT=qT[:, qg * 512 + c0: qg * 512 + c0 + nq], v=vt[:, kt, :],
                     c0=c0, nq=nq, masks=masks, reads=reads, vreads=vreads)
            bias_fn(u, kt, qg, c0, nq, r0)
            if full_fn is not None:
                u["full"] = full_fn(kt, c0, nq)
            units.append(u)
        return units

    def alibi_bias(self, slope):
        d0t = self.cst[:, CO_D0:CO_D0 + 512]
        d0p = self.cst[:, CO_D0P:CO_D0P + 512]

        def f(u, kt, qg, c0, nq, r0):
            u["bias"] = ("alibi", slope, (d0p if r0 == 0 else d0t)[:, :nq])
            u["cb"] = -slope * 128.0 * r0
        return f

    def load_qkv(self, qblk, kblk, vcol, rot):
        q, qb, qds = rot["q"].next()
        k, kb, kds = rot["k"].next()
        v, vb, vds = rot["v"].next()
        self.dma(q[:], self.pfm[qblk], (), (qb,), qds)
        self.dma(k[:], self.pfm[kblk], (), (kb,), kds)
        self.dma(v[:], self.ptm[:, vcol:vcol + 128].rearrange("(kt p) d -> p kt d", p=128), (), (vb,), vds)
        return (q, qb), (k, kb), (v, vb)

    def attn_common(self):
        S = self.S
        self.ptile = Rot(S, 3, [128, 512], BF16, "pt")
        self.tmpt = Rot(S, 3, [128, 512], F32, "tmp")
        self.rden = Rot(S, 2, [128, 512], F32, "rden")

    def mix_a(self):
        S = self.S
        ps = self.ps
        mark = S.sb_mark()
        self.attn_common()
        rot = {"q": Rot(S, 3, [128, L], BF16, "aq", dma=True), "k": Rot(S, 3, [128, L], BF16, "ak", dma=True),
               "v": Rot(S, 3, [128, NT, 128], BF16, "av", dma=True)}
        OA = Rot(S, 2, [128, L], F32, "OA")
        DA = Rot(S, 2, [128, L], F32, "DA")
        ob = Rot(S, 2, [128, L], BF16, "oab", dma=True)
        CMr = self.cstb[:, CO_CMR:CO_CMR + 512]
        D0r = self.cst[:, CO_D0R:CO_D0R + 512]
        ones = self.cstb[:, CO_ONE:CO_ONE + 128]
        for h in range(4):
            oa, oab, _ = OA.next()
            da, dab, _ = DA.next()
            slope = SLOPES[A_SLOPE_IDX[h]]
            (q, qb), (k, kb), (v, vb) = self.load_qkv(h, 12 + h, h * 128, rot)
            for qg in range(4):
                units = self.causal_units(qg, k, q, v, self.alibi_bias(slope), (qb, kb), (vb,), n_prev=1)

                def ev(bo, bd, qg=qg, oa=oa, da=da, oab=oab, dab=dab):
                    sl = slice(qg * 512, (qg + 1) * 512)
                    S.add("dve", lambda e: e.tensor_copy(out=oa[:, sl], in_=ps[:, bo, :]), reads=(self.psb[bo],), writes=(oab,))
                    S.add("act", lambda e: e.activation(out=da[:, sl], in_=ps[:, bd, :], func=AF.Copy), reads=(self.psb[bd],), writes=(dab,))
                self.attn_job(units, ev, self.ptile, self.tmpt)
            hd = 4 + h
            slope = SLOPES[A_SLOPE_IDX[hd]] * 4.0
            q, qb, qds = rot["q"].next()
            k, kb, kds = rot["k"].next()
            v, vb, vds = rot["v"].next()
            self.dma(q[:], self.pfm[hd], (), (qb,), qds)
            self.dma(k[:], self.pfm[12 + hd], (), (kb,), kds)
            for r in range(4):
                self.dma(v[:, r * 4:(r + 1) * 4, :],
                         self.ptm[:, hd * 128:(hd + 1) * 128].rearrange("(kt j r) d -> r j kt d", r=4, j=128)[r], (), (vb,), vds)
            for r in range(4):
                qs = q[:, r:L:4]
                ks = k[:, r:L:4]
                units = self.causal_units(0, ks, qs, v[:, r * 4:(r + 1) * 4, :], self.alibi_bias(slope), (qb, kb), (vb,), n_prev=1)

                def ev(bo, bd, r=r, oa=oa, da=da, oab=oab, dab=dab):
                    S.add("dve", lambda e: e.tensor_tensor(out=oa[:, r:L:4], in0=oa[:, r:L:4], in1=ps[:, bo, :], op=ALU.add),
                          reads=(self.psb[bo], oab), writes=(oab,))
                    S.add("dve", lambda e: e.tensor_tensor(out=da[:, r:L:4], in0=da[:, r:L:4], in1=ps[:, bd, :], op=ALU.add),
                          reads=(self.psb[bd], dab), writes=(dab,))
                self.attn_job(units, ev, self.ptile, self.tmpt)
            hd = 8 + h
            slope = SLOPES[A_SLOPE_IDX[hd]] * 16.0
            q, qb, qds = rot["q"].next()
            k, kb, kds = rot["k"].next()
            v, vb, vds = rot["v"].next()
            self.dma(q[:], self.pfm[hd], (), (qb,), qds)
            self.dma(k[:], self.pfm[12 + hd], (), (kb,), kds)
            for r4 in range(4):
                self.dma(v[:, r4 * 4:(r4 + 1) * 4, :],
                         self.ptm[:, hd * 128:(hd + 1) * 128].rearrange("(j r) d -> j r d", r=16)[:, r4 * 4:(r4 + 1) * 4, :], (), (vb,), vds)
            for r4 in range(4):
                bo, bd, bs = self.bank(), self.bank(), self.bank()

                def qk(e, r4=r4, bs=bs, q=q, k=k):
                    for i in range(4):
                        r = r4 * 4 + i
                        ins = e.matmul(ps[:, bs, i * 128:(i + 1) * 128], k[:, r:L:16], q[:, r:L:16], start=True, stop=True)
                    return ins
                S.add("pe", qk, reads=(qb, kb), writes=(self.psb[bs],))
                tm, tmb, _ = self.tmpt.next()
                p, pb, _ = self.ptile.next()
                S.add("dve", lambda e, tm=tm, bs=bs, slope=slope: e.scalar_tensor_tensor(out=tm[:], in0=D0r, scalar=-float(slope), in1=ps[:, bs, :],
                                                                                       op0=ALU.mult, op1=ALU.add),
                      reads=(self.psb[bs], self.b_cst), writes=(tmb,))
                S.add("act", lambda e, p=p, tm=tm: e.activation(out=p[:], in_=tm[:], func=AF.Exp), reads=(tmb,), writes=(pb,))
                S.add("pool", lambda e, p=p: e.tensor_tensor(out=p[:], in0=p[:], in1=CMr, op=ALU.mult), reads=(pb, self.b_cst), writes=(pb,))

                def pv(e, r4=r4, p=p, bo=bo, bd=bd, v=v):
                    for i in range(4):
                        e.matmul(ps[:, bo, i * 128:(i + 1) * 128], v[:, r4 * 4 + i, :], p[:, i * 128:(i + 1) * 128], start=(i == 0), stop=False, skip_group_check=True)
                    for i in range(4):
                        ins = e.matmul(ps[:, bd, i * 128:(i + 1) * 128], ones, p[:, i * 128:(i + 1) * 128], start=(i == 0), stop=False, skip_group_check=True)
                    return ins
                S.add("pe", pv, reads=(pb, vb, self.b_cst), writes=(self.psb[bo], self.psb[bd]))
                oav = oa[:].rearrange("p (j r) -> p r j", r=16)[:, r4 * 4:(r4 + 1) * 4, :]
                dav = da[:].rearrange("p (j r) -> p r j", r=16)[:, r4 * 4:(r4 + 1) * 4, :]
                S.add("dve", lambda e, oav=oav, bo=bo: e.tensor_tensor(out=oav, in0=oav, in1=ps[:, bo, :].rearrange("p (i j) -> p i j", i=4), op=ALU.add),
                      reads=(self.psb[bo], oab), writes=(oab,))
                S.add("dve", lambda e, dav=dav, bd=bd: e.tensor_tensor(out=dav, in0=dav, in1=ps[:, bd, :].rearrange("p (i j) -> p i j", i=4), op=ALU.add),
                      reads=(self.psb[bd], dab), writes=(dab,))
            o, obb, ods = ob.next()
            S.add("dve", lambda e, da=da: e.reciprocal(out=da[:], in_=da[:]), reads=(dab,), writes=(dab,))
            S.add("dve", lambda e, o=o, oa=oa, da=da: e.tensor_tensor(out=o[:], in0=oa[:], in1=da[:], op=ALU.mult), reads=(oab, dab), writes=(obb,))
            self.dma(self.obT[h], o[:], (obb,), (), ods)
        self.barrier()
        S.sb_reset(mark)

    def mix_c(self):
        S = self.S
        ps = self.ps
        mark = S.sb_mark()
        self.attn_common()
        rot = {"q": Rot(S, 2, [128, L], BF16, "cq", dma=True), "k": Rot(S, 2, [128, L], BF16, "ck", dma=True),
               "v": Rot(S, 2, [128, NT, 128], BF16, "cv", dma=True)}
        ob = Rot(S, 2, [128, L], BF16, "ocb", dma=True)
        cf = S.sb([8, L], F32, "cf")
        cs = S.sb([8, L], F32, "cs")
        onesr = S.sb([8, L], F32, "onesr")
        b_cf = S.buf("cf")
        ds = S.dsem()
        self.dma(cf[:], self.cfT, (), (b_cf,), ds)
        S.add("act", lambda e: e.activation(out=cf[:], in_=cf[:], func=AF.Exp, scale=-1.0), reads=(b_cf,), writes=(b_cf,))
        S.add("act", lambda e: e.activation(out=cf[:], in_=cf[:], func=AF.Ln, bias=1.0), reads=(b_cf,), writes=(b_cf,))
        S.add("dve", lambda e: e.memset(onesr[:], 1.0), reads=(), writes=(b_cf,))
        S.add("dve", lambda e: e.tensor_tensor_scan(out=cs[:], data0=onesr[:], data1=cf[:], initial=0.0, op0=ALU.mult, op1=ALU.add),
              reads=(b_cf,), writes=(b_cf,))
        b_csd = S.buf("csd")
        self.dma(self.csd, cs[:], (b_cf,), (b_csd,), ds)
        csT = S.sb([128, NT, 8], F32, "csT")
        b_csT = S.buf("csT")
        for tt in range(NT):
            self.tr_small(csT[:, tt, :], cs[:, tt * 128:(tt + 1) * 128], 8, (b_cf,), b_csT)
        cbc = Rot(S, 2, [128, L], F32, "cbc", dma=True)
        for h in range(8):
            (q, qb), (k, kb), (v, vb) = self.load_qkv(40 + h, 48 + h, 2048 + h * 128, rot)
            cb_t, cb_b, cb_ds = cbc.next()
            self.dma(cb_t[:], self.csd[h:h + 1, :].to_broadcast([128, L]), (b_csd,), (cb_b,), cb_ds)
            o, obb, ods = ob.next()

            def bias_fn(u, kt, qg, c0, nq, r0, h=h, cb_t=cb_t, cb_b=cb_b):
                u["bias"] = ("fox", csT[:, kt, h:h + 1], cb_t[:, qg * 512 + c0: qg * 512 + c0 + nq], (b_csT, cb_b))
            for qg in range(4):
                units = self.causal_units(qg, k, q, v, bias_fn, (qb, kb), (vb,))
                self.attn_job(units, self.evac_norm(o, obb, qg * 512), self.ptile, self.tmpt)
            self.dma(self.obT[12 + h], o[:], (obb,), (), ods)
        self.barrier()
        S.sb_reset(mark)

    def mix_d(self):
        S = self.S
        ps = self.ps
        mark = S.sb_mark()
        self.attn_common()
        ident = self.cst[:, CO_ID:CO_ID + 128]
        NLE = self.cst[:, CO_NLE:CO_NLE + 128]
        qs = S.sb([128, 8, L], BF16, "dq")
        kk = S.sb([128, L], BF16, "dk")
        vv = S.sb([128, NT, 128], BF16, "dv")
        iq = S.sb([128, 4, L], BF16, "diq")
        ik = S.sb([128, L], BF16, "dik")
        iw = S.sb([128, NT, 8], F32, "diw")
        b_in = S.buf("din")
        ds = S.dsem()
        for h in range(8):
            self.dma(qs[:, h, :], self.pfm[56 + h], (), (b_in,), ds)
        self.dma(kk[:], self.pfm[64], (), (b_in,), ds)
        self.dma(vv[:], self.ptm[:, 3072:3200].rearrange("(kt p) d -> p kt d", p=128), (), (b_in,), ds)
        for b in range(4):
            self.dma(iq[:, b, :], self.pfm[65 + b], (), (b_in,), ds)
        self.dma(ik[:], self.pfm[69], (), (b_in,), ds)
        self.dma(iw[:], self.iwtm.rearrange("(tt p) h -> p tt h", p=128), (), (b_in,), ds)
        score = Rot(S, 2, [128, L], F32, "score")
        work = S.sb([128, L], F32, "work")
        b_work = S.buf("work")
        m8 = S.sb([128, 8], F32, "m8")
        msk = Rot(S, 2, [128, L], F32, "msk")
        relu = Rot(S, 3, [128, 512], F32, "relu")
        maskT = Rot(S, 2, [128, NT, 512], BF16, "maskT")
        ob = Rot(S, 8, [128, 512], BF16, "odb", dma=True)
        for qg in range(4):
            mT, mTb, _ = maskT.next()
            for qi in range(4):
                qt = qg * 4 + qi
                nk = (qt + 1) * 128
                sc, scb, _ = score.next()
                for kg in range((nk + 511) // 512):
                    n = min(512, nk - kg * 512)
                    for h in range(8):
                        bk = self.bank()
                        pr = 64 * (h % 2)
                        S.add("pe", lambda e, bk=bk, h=h, pr=pr, qt=qt, kg=kg, n=n: e.matmul(
                            ps[:, bk, :n], iq[pr:pr + 64, h // 2, qt * 128:(qt + 1) * 128], ik[pr:pr + 64, kg * 512:kg * 512 + n], start=True, stop=True),
                            reads=(b_in,), writes=(self.psb[bk],))
                        r, rb, _ = relu.next()
                        S.add("act", lambda e, r=r, bk=bk, n=n: e.activation(out=r[:, :n], in_=ps[:, bk, :n], func=AF.Relu), reads=(self.psb[bk],), writes=(rb,))
                        sl = slice(kg * 512, kg * 512 + n)
                        if h == 0:
                            S.add("dve", lambda e, sc=sc, r=r, n=n, sl=sl, qt=qt: e.tensor_scalar(out=sc[:, sl], in0=r[:, :n], scalar1=iw[:, qt, 0:1], scalar2=None, op0=ALU.mult),
                                  reads=(rb, b_in), writes=(scb,))
                        else:
                            S.add("dve", lambda e, sc=sc, r=r, n=n, sl=sl, qt=qt, h=h: e.scalar_tensor_tensor(out=sc[:, sl], in0=r[:, :n], scalar=iw[:, qt, h:h + 1], in1=sc[:, sl],
                                                                                                        op0=ALU.mult, op1=ALU.add),
                                  reads=(rb, b_in, scb), writes=(scb,))
                dsl = slice(qt * 128, (qt + 1) * 128)
                S.add("dve", lambda e, sc=sc, dsl=dsl: e.tensor_tensor(out=sc[:, dsl], in0=sc[:, dsl], in1=NLE, op=ALU.add), reads=(scb, self.b_cst), writes=(scb,))
                mk, mkb, _ = msk.next()
                if qt < 2:
                    S.add("dve", lambda e, mk=mk, sc=sc, nk=nk: e.tensor_scalar(out=mk[:, :nk], in0=sc[:, :nk], scalar1=-1e29, scalar2=None, op0=ALU.is_ge),
                          reads=(scb,), writes=(mkb,))
                else:
                    S.add("dve", lambda e, sc=sc, nk=nk: e.tensor_copy(out=work[:, :nk], in_=sc[:, :nk]), reads=(scb,), writes=(b_work,))
                    for rnd in range(32):
                        S.add("dve", lambda e, nk=nk: e.max(out=m8[:], in_=work[:, :nk]), reads=(b_work,), writes=(b_work,))
                        if rnd < 31:
                            S.add("dve", lambda e, nk=nk: e.match_replace(out=work[:, :nk], in_to_replace=m8[:], in_values=work[:, :nk], imm_value=-3e38),
                                  reads=(b_work,), writes=(b_work,))
                    S.add("dve", lambda e, mk=mk, sc=sc, nk=nk: e.tensor_scalar(out=mk[:, :nk], in0=sc[:, :nk], scalar1=m8[:, 7:8], scalar2=None, op0=ALU.is_ge),
                          reads=(scb, b_work), writes=(mkb,))
                for k4 in range((qt + 4) // 4):
                    nkt = min(4, qt + 1 - k4 * 4)
                    bk = self.bank()

                    def tr(e, mk=mk, bk=bk, k4=k4, nkt=nkt):
                        for j in range(nkt):
                            kt = k4 * 4 + j
                            ins = e.transpose(ps[:, bk, j * 128:(j + 1) * 128], mk[:, kt * 128:(kt + 1) * 128], ident)
                        return ins
                    S.add("pe", tr, reads=(mkb, self.b_cst), writes=(self.psb[bk],))
                    S.add("act", lambda e, mT=mT, bk=bk, k4=k4, nkt=nkt, qi=qi: e.activation(
                        out=mT[:, k4 * 4:k4 * 4 + nkt, qi * 128:(qi + 1) * 128], in_=ps[:, bk, :nkt * 128].rearrange("p (j q) -> p j q", j=nkt), func=AF.Copy),
                        reads=(self.psb[bk],), writes=(mTb,))
            for h in range(8):
                slope = SLOPES[D_SLOPE_IDX[h]]
                o, obb, ods = ob.next()
                units = self.causal_units(qg, kk, qs[:, h, :], vv, self.alibi_bias(slope), (b_in,), (b_in,),
                                          full_fn=lambda kt, c0, nq, mT=mT, mTb=mTb: (mT[:, kt, c0:c0 + nq], mTb))
                self.attn_job(units, self.evac_norm(o, obb, 0), self.ptile, self.tmpt)
                self.dma(self.obT[20 + h, :, qg * 512:(qg + 1) * 512], o[:], (obb,), (), ods)
        self.barrier()
        S.sb_reset(mark)

    def mix_b(self, l):
        S = self.S
        ps = self.ps
        mark = S.sb_mark()
        self.attn_common()
        ident = self.cst[:, CO_ID:CO_ID + 128]
        kcT = S.sb([128, 2, 128], BF16, "kcT")
        vc = S.sb([128, 2, 128], BF16, "vc")
        b_kc, b_vc = S.buf("kc"), S.buf("vc")
        m2 = S.sb_mark()
        w1s = Rot(S, 2, [128, 32, 512], BF16, "w1s", sw=True)
        w2s = Rot(S, 2, [128, 4, 128], BF16, "w2s", sw=True)
        posT = Rot(S, 2, [128, 32], F32, "posT")
        posraw = Rot(S, 2, [32, 128], F32, "posraw", dma=True)
        src_t = Rot(S, 2, [128, L], BF16, "cmpsrc", dma=True)
        kvp = Rot(S, 2, [128, 32, 128], BF16, "kvp")
        hid = Rot(S, 2, [128, 4, 128], BF16, "hid")
        gt = Rot(S, 4, [128, 128], F32, "gelu")
        for j in range(2):
            w1, w1b, w1d = w1s.next()
            for q4 in range(4):
                self.dma(w1[:, q4 * 8:(q4 + 1) * 8, :], self.w["cmp_w1"][l * 2 + j, q4 * 1024:(q4 + 1) * 1024, :].rearrange("(p d) h -> d p h", d=128),
                         (), (w1b,), w1d, q="pool")
            w2, w2b, w2d = w2s.next()
            self.dma(w2[:], self.w["cmp_w2"][l * 2 + j].rearrange("(hc p) d -> p hc d", p=128), (), (w2b,), w2d, q="pool")
            pT, pTb, pTd = posT.next()
            pr_t, pr_b, pr_d = posraw.next()
            self.dma(pr_t[:], self.w["cmp_pos"][l * 2 + j], (), (pr_b,), pr_d)
            self.tr_small(pT[:], pr_t[:], 32, (pr_b,), pTb)
            for g in range(2):
                sr, srb, srd = src_t.next()
                self.dma(sr[:], self.pfm[32 + j * 2 + g], (), (srb,), srd)
                kv, kvb, _ = kvp.next()
                for p in range(32):
                    S.add("dve" if p % 2 == 0 else "pool",
                          lambda e, kv=kv, sr=sr, pT=pT, p=p: e.tensor_scalar(out=kv[:, p, 0:127], in0=sr[:, p:p + 16 * 126 + 1:16], scalar1=pT[:, p:p + 1], scalar2=None, op0=ALU.add),
                          reads=(srb, pTb), writes=(kvb,))
                hd, hdb, _ = hid.next()
                for hc in range(4):
                    bk = self.bank()

                    def mm(e, w1=w1, kv=kv, bk=bk, hc=hc):
                        for p in range(32):
                            ins = e.matmul(ps[:, bk, :127], w1[:, p, hc * 128:(hc + 1) * 128], kv[:, p, 0:127], start=(p == 0), stop=(p == 31))
                        return ins
                    S.add("pe", mm, reads=(w1b, kvb), writes=(self.psb[bk],))
                    u, ub, _ = gt.next()
                    S.add("act", lambda e, u=u, bk=bk: e.activation(out=u[:, :127], in_=ps[:, bk, :127], func=AF.Square), reads=(self.psb[bk],), writes=(ub,))
                    S.add("dve", lambda e, u=u: e.tensor_scalar(out=u[:, :127], in0=u[:, :127], scalar1=0.044715, scalar2=1.0, op0=ALU.mult, op1=ALU.add), reads=(ub,), writes=(ub,))
                    S.add("dve", lambda e, u=u, bk=bk: e.tensor_tensor(out=u[:, :127], in0=u[:, :127], in1=ps[:, bk, :127], op=ALU.mult), reads=(ub, self.psb[bk]), writes=(ub,))
                    S.add("act", lambda e, u=u: e.activation(out=u[:, :127], in_=u[:, :127], func=AF.Sigmoid, scale=1.5957691216057308), reads=(ub,), writes=(ub,))
                    S.add("dve", lambda e, u=u, hd=hd, hc=hc, bk=bk: e.tensor_tensor(out=hd[:, hc, :127], in0=u[:, :127], in1=ps[:, bk, :127], op=ALU.mult),
                          reads=(ub, self.psb[bk]), writes=(hdb,))
                bk = self.bank()
                if j == 0:
                    def mm2(e, w2=w2, hd=hd, bk=bk):
                        for hc in range(4):
                            ins = e.matmul(ps[:, bk, :127], w2[:, hc, :], hd[:, hc, :127], start=(hc == 0), stop=(hc == 3))
                        return ins
                    S.add("pe", mm2, reads=(w2b, hdb), writes=(self.psb[bk],))
                    S.add("dve", lambda e, g=g, bk=bk: e.tensor_copy(out=kcT[:, g, :127], in_=ps[:, bk, :127]), reads=(self.psb[bk],), writes=(b_kc,))
                else:
                    def mm2(e, w2=w2, hd=hd, bk=bk):
                        for hc in range(4):
                            ins = e.matmul(ps[:127, bk, :128], hd[:, hc, :127], w2[:, hc, :], start=(hc == 0), stop=(hc == 3))
                        return ins
                    S.add("pe", mm2, reads=(w2b, hdb), writes=(self.psb[bk],))
                    S.add("dve", lambda e, g=g, bk=bk: e.tensor_copy(out=vc[:127, g, :], in_=ps[:127, bk, :128]), reads=(self.psb[bk],), writes=(b_vc,))
        self.barrier()
        S.all_bufs.extend([b_kc, b_vc])
        S.sb_reset(m2)
        qall = S.sb([128, 8, L], BF16, "bq")
        b_q = S.buf("bq")
        dq = S.dsem()
        for hb in range(8):
            self.dma(qall[:, hb, :], self.pfm[24 + hb], (), (b_q,), dq)
        selc = S.sb([128, 16, 96], F32, "selc")
        b_selc = S.buf("selc")
        self.dma(selc[:], self.selc_d.rearrange("p (a b) -> p a b", b=96), (), (b_selc,), dq)
        selE = S.sb([32, L], BF16, "selE")
        b_selE = S.buf("selE")
        self.dma(selE[:], self.sel_d, (), (b_selE,), S.dsem(True), q="pool")
        ocmp = S.sb([128, 8, L], BF16, "ocmp")
        b_oc = [S.buf("oc") for _ in range(8)]
        bmT = S.sb([32, 2, L], BF16, "bmT")
        b_bm = [S.buf("bm0"), S.buf("bm1")]
        dcl = Rot(S, 2, [128, 128], F32, "dcl")
        ngm = Rot(S, 2, [128, 128], F32, "ngm")
        et = Rot(S, 3, [128, 128], F32, "et")
        t1 = Rot(S, 3, [128, 128], F32, "t1")
        den = Rot(S, 4, [128, 2], F32, "den")
        imp = Rot(S, 2, [128, 132], F32, "imp")
        psl = Rot(S, 2, [128, 64], F32, "psl")
        m8 = Rot(S, 2, [128, 8], F32, "m8b")
        pT = Rot(S, 3, [128, 128], BF16, "pT")
        DC = self.cst[:, CO_DC:CO_DC + 128]
        for qt in range(NT):
            dc, dcb, _ = dcl.next()
            ng, ngb, _ = ngm.next()
            S.add("dve", lambda e, dc=dc, qt=qt: e.tensor_scalar(out=dc[:], in0=DC, scalar1=float(128 * qt), scalar2=0.0, op0=ALU.add, op1=ALU.max),
                  reads=(self.b_cst,), writes=(dcb,))
            S.add("dve", lambda e, ng=ng, qt=qt: e.tensor_scalar(out=ng[:], in0=DC, scalar1=float(-128 * qt), scalar2=NEGBIG, op0=ALU.is_lt, op1=ALU.mult),
                  reads=(self.b_cst,), writes=(ngb,))
            qsl = slice(qt * 128, (qt + 1) * 128)
            for g in range(2):
                im, imb, _ = imp.next()
                S.add("pool", lambda e, im=im: e.memset(im[:], 0.0), reads=(), writes=(imb,))
                bo = self.bank()
                for r in range(4):
                    hb = g * 4 + r
                    slope = SLOPES[B_SLOPE_IDX[hb]]
                    bk = self.bank()
                    while bk == bo:
                        bk = self.bank()
                    S.add("pe", lambda e, bk=bk, hb=hb, g=g, qsl=qsl: e.matmul(ps[:, bk, :127], qall[:, hb, qsl], kcT[:, g, :127], start=True, stop=True),
                          reads=(b_q, b_kc), writes=(self.psb[bk],))
                    tt, ttb, _ = t1.next()
                    S.add("dve", lambda e, tt=tt, dc=dc, ng=ng, slope=slope: e.scalar_tensor_tensor(out=tt[:, :127], in0=dc[:, :127], scalar=-float(slope), in1=ng[:, :127],
                                                                                                 op0=ALU.mult, op1=ALU.add),
                          reads=(dcb, ngb), writes=(ttb,))
                    S.add("dve", lambda e, tt=tt, bk=bk: e.tensor_tensor(out=tt[:, :127], in0=tt[:, :127], in1=ps[:, bk, :127], op=ALU.add),
                          reads=(ttb, self.psb[bk]), writes=(ttb,))
                    e_t, eb, _ = et.next()
                    dn, dnb, _ = den.next()
                    S.add("pool", lambda e, e_t=e_t: e.memset(e_t[:, 127:128], 0.0), reads=(), writes=(eb,))
                    S.add("pool", lambda e, dn=dn: e.memset(dn[:], 0.0), reads=(), writes=(dnb,))
                    S.add("act", lambda e, e_t=e_t, tt=tt, dn=dn: e.activation(out=e_t[:, :127], in_=tt[:, :127], func=AF.Exp, accum_out=dn[:, 0:1]),
                          reads=(ttb,), writes=(eb, dnb))
                    S.add("dve", lambda e, dn=dn: e.tensor_scalar(out=dn[:, 1:2], in0=dn[:, 0:1], scalar1=1e-30, scalar2=None, op0=ALU.max), reads=(dnb,), writes=(dnb,))
                    S.add("dve", lambda e, dn=dn: e.reciprocal(out=dn[:, 1:2], in_=dn[:, 1:2]), reads=(dnb,), writes=(dnb,))
                    S.add("dve", lambda e, e_t=e_t, dn=dn: e.tensor_scalar(out=e_t[:], in0=e_t[:], scalar1=dn[:, 1:2], scalar2=None, op0=ALU.mult), reads=(eb, dnb), writes=(eb,))
                    S.add("pool", lambda e, im=im, e_t=e_t: e.tensor_tensor(out=im[:, 1:128], in0=im[:, 1:128], in1=e_t[:, 0:127], op=ALU.add), reads=(eb, imb), writes=(imb,))
                    bt = self.bank()
                    while bt == bo:
                        bt = self.bank()
                    S.add("pe", lambda e, bt=bt, e_t=e_t: e.transpose(ps[:, bt, :128], e_t[:], ident), reads=(eb, self.b_cst), writes=(self.psb[bt],))
                    pt_, ptb, _ = pT.next()
                    S.add("act", lambda e, pt_=pt_, bt=bt: e.activation(out=pt_[:], in_=ps[:, bt, :128], func=AF.Copy), reads=(self.psb[bt],), writes=(ptb,))
                    S.add("pe", lambda e, bo=bo, r=r, g=g, pt_=pt_: e.matmul(ps[:, bo, r * 128:(r + 1) * 128], vc[:127, g, :], pt_[:127, :], start=(r == 0), stop=False, skip_group_check=True),
                          reads=(ptb, b_vc), writes=(self.psb[bo],))
                for r in range(4):
                    hb = g * 4 + r
                    S.add("act", lambda e, hb=hb, r=r, bo=bo, qsl=qsl: e.activation(out=ocmp[:, hb, qsl], in_=ps[:, bo, r * 128:(r + 1) * 128], func=AF.Copy),
                          reads=(self.psb[bo],), writes=(b_oc[hb],))
                pl, plb, _ = psl.next()
                S.add("dve", lambda e, pl=pl, im=im: e.tensor_scalar(out=pl[:, 0:32], in0=im[:, 0:128:4], scalar1=1.0, scalar2=None, op0=ALU.mult), reads=(imb,), writes=(plb,))
                for o, wgt in ((1, 2.0), (2, 2.0), (3, 2.0), (4, 1.0)):
                    S.add("dve", lambda e, pl=pl, im=im, o=o, wgt=wgt: e.scalar_tensor_tensor(out=pl[:, 0:32], in0=im[:, o:o + 128:4], scalar=wgt, in1=pl[:, 0:32], op0=ALU.mult, op1=ALU.add),
                          reads=(imb, plb), writes=(plb,))
                S.add("dve", lambda e, pl=pl, qt=qt: e.tensor_tensor(out=pl[:, 0:32], in0=pl[:, 0:32], in1=selc[:, qt, 0:32], op=ALU.mult), reads=(plb, b_selc), writes=(plb,))
                S.add("dve", lambda e, pl=pl, qt=qt: e.tensor_tensor(out=pl[:, 0:32], in0=pl[:, 0:32], in1=selc[:, qt, 32:64], op=ALU.add), reads=(plb, b_selc), writes=(plb,))
                S.add("dve", lambda e, pl=pl, qt=qt: e.tensor_tensor(out=pl[:, 0:32], in0=pl[:, 0:32], in1=selc[:, qt, 64:96], op=ALU.max), reads=(plb, b_selc), writes=(plb,))
                m, mb, _ = m8.next()
                S.add("dve", lambda e, m=m, pl=pl: e.max(out=m[:], in_=pl[:, 0:32]), reads=(plb,), writes=(mb,))
                S.add("dve", lambda e, m=m, pl=pl: e.tensor_scalar(out=pl[:, 32:64], in0=pl[:, 0:32], scalar1=m[:, 7:8], scalar2=None, op0=ALU.is_ge), reads=(plb, mb), writes=(plb,))
                bt = self.bank()
                S.add("pe", lambda e, bt=bt, pl=pl: e.transpose(ps[:32, bt, :128], pl[:, 32:64], ident), reads=(plb, self.b_cst), writes=(self.psb[bt],))
                S.add("act", lambda e, bt=bt, g=g, qsl=qsl: e.activation(out=bmT[:, g, qsl], in_=ps[:32, bt, :128], func=AF.Copy), reads=(self.psb[bt],), writes=(b_bm[g],))
        rot = {"k": Rot(S, 2, [128, L], BF16, "bk", dma=True), "v": Rot(S, 2, [128, NT, 128], BF16, "bv", dma=True)}
        mTr = Rot(S, 2, [128, NT, 512], BF16, "bmaskT")
        osl = Rot(S, 2, [128, 512], F32, "osl")
        owi = Rot(S, 2, [128, 512], F32, "owi")
        gbc = Rot(S, 2, [128, 3, 512], F32, "gbc", dma=True)
        ob = Rot(S, 3, [128, 512], BF16, "obb", dma=True)
        for g in range(2):
            ks, ksb, ksd = rot["k"].next()
            vs, vsb, vsd = rot["v"].next()
            kw, kwb, kwd = rot["k"].next()
            vw, vwb, vwd = rot["v"].next()
            self.dma(ks[:], self.pfm[36 + g], (), (ksb,), ksd)
            self.dma(kw[:], self.pfm[38 + g], (), (kwb,), kwd)
            self.dma(vs[:], self.ptm[:, 1536 + g * 128:1536 + (g + 1) * 128].rearrange("(kt p) d -> p kt d", p=128), (), (vsb,), vsd)
            self.dma(vw[:], self.ptm[:, 1792 + g * 128:1792 + (g + 1) * 128].rearrange("(kt p) d -> p kt d", p=128), (), (vwb,), vwd)
            for qg in range(4):
                mT, mTb, _ = mTr.next()
                for kt in range(4 * qg + 4):
                    bk = self.bank()
                    S.add("pe", lambda e, bk=bk, kt=kt, g=g, qg=qg: e.matmul(ps[:, bk, :], selE[:, kt * 128:(kt + 1) * 128], bmT[:, g, qg * 512:(qg + 1) * 512], start=True, stop=True),
                          reads=(b_selE, b_bm[g]), writes=(self.psb[bk],))
                    if kt % 2 == 0:
                        S.add("act", lambda e, mT=mT, bk=bk, kt=kt: e.activation(out=mT[:, kt, :], in_=ps[:, bk, :], func=AF.Copy), reads=(self.psb[bk],), writes=(mTb,))
                    else:
                        S.add("dve", lambda e, mT=mT, bk=bk, kt=kt: e.tensor_copy(out=mT[:, kt, :], in_=ps[:, bk, :]), reads=(self.psb[bk],), writes=(mTb,))
                for r in range(4):
                    hb = g * 4 + r
                    slope = SLOPES[B_SLOPE_IDX[hb]]
                    q = qall[:, hb, :]
                    o_s, osb, _ = osl.next()
                    o_w, owb, _ = owi.next()
                    units = self.causal_units(qg, ks, q, vs, self.alibi_bias(slope), (b_q, ksb), (vsb,),
                                              full_fn=lambda kt, c0, nq, mT=mT, mTb=mTb: (mT[:, kt, c0:c0 + nq], mTb))
                    self.attn_job(units, self.evac_norm(o_s, osb, 0), self.ptile, self.tmpt)
                    units = self.causal_units(qg, kw, q, vw, self.alibi_bias(slope), (b_q, kwb), (vwb,), n_prev=4)
                    self.attn_job(units, self.evac_norm(o_w, owb, 0), self.ptile, self.tmpt)
                    gb_t, gbb, gbd = gbc.next()
                    for br in range(3):
                        self.dma(gb_t[:, br, :], self.bgT[br * 8 + hb:br * 8 + hb + 1, qg * 512:(qg + 1) * 512].to_broadcast([128, 512]), (), (gbb,), gbd)
                    o, obb_, ods = ob.next()
                    sl = slice(qg * 512, (qg + 1) * 512)
                    S.add("dve", lambda e, o_s=o_s, gb_t=gb_t: e.tensor_tensor(out=o_s[:], in0=o_s[:], in1=gb_t[:, 1, :], op=ALU.mult), reads=(osb, gbb), writes=(osb,))
                    S.add("pool", lambda e, o_w=o_w, gb_t=gb_t: e.tensor_tensor(out=o_w[:], in0=o_w[:], in1=gb_t[:, 2, :], op=ALU.mult), reads=(owb, gbb), writes=(owb,))
                    S.add("dve", lambda e, o_s=o_s, o_w=o_w: e.tensor_tensor(out=o_s[:], in0=o_s[:], in1=o_w[:], op=ALU.add), reads=(osb, owb), writes=(osb,))
                    S.add("pool", lambda e, o_w=o_w, gb_t=gb_t, hb=hb, sl=sl: e.tensor_tensor(out=o_w[:], in0=ocmp[:, hb, sl], in1=gb_t[:, 0, :], op=ALU.mult),
                          reads=(b_oc[hb], gbb, owb), writes=(owb,))
                    S.add("dve", lambda e, o=o, o_s=o_s, o_w=o_w: e.tensor_tensor(out=o[:], in0=o_s[:], in1=o_w[:], op=ALU.add), reads=(osb, owb), writes=(obb_,))
                    self.dma(self.obT[4 + hb, :, sl], o[:], (obb_,), (), ods)
        self.barrier()
        S.sb_reset(mark)

    def ph_merge(self, l):
        S = self.S
        ps = self.ps
        mark = S.sb_mark()
        wb = self.w["w_branch"][l]
        S.sb_reset(self.mark_noxb)
        osb_t = S.sb([128, 28, L], BF16, "obres")
        b_ob = [S.buf("ob") for _ in range(28)]
        dso = S.dsem()
        for b in range(28):
            self.dma(osb_t[:, b, :], self.obT[b], (), (b_ob[b],), dso)
        wsl = Rot(S, 2, [128, 28, 128], BF16, "wbr", sw=True)
        gt = Rot(S, 2, [128, 4, L], BF16, "gin", dma=True)
        acc = Rot(S, 2, [128, 512], F32, "macc")
        tmp = Rot(S, 2, [128, 512], F32, "mtmp")
        mo = Rot(S, 2, [128, L], BF16, "mout", dma=True)
        br_chunks = ((0, 4), (4, 8), (12, 8), (20, 8))
        for m in range(KC):
            tl, bf, ds = wsl.next()
            self.dma(tl[:], wb[:, m * 128:(m + 1) * 128].rearrange("(kc p) n -> p kc n", p=128), (), (bf,), ds, q="pool")
            g_t, gb, gds = gt.next()
            self.dma(g_t[:], self.gatesT.rearrange("(br f) t -> f br t", br=4)[m * 128:(m + 1) * 128], (), (gb,), gds)
            o, ob, ods = mo.next()
            for t in range(4):
                sl = slice(t * 512, (t + 1) * 512)
                a, ab, _ = acc.next()
                for bi, (c0, nc_) in enumerate(br_chunks):
                    bk = self.bank()

                    def mm(e, tl=tl, bk=bk, c0=c0, nc_=nc_, sl=sl):
                        for k in range(nc_):
                            ins = e.matmul(ps[:, bk, :], tl[:, c0 + k, :], osb_t[:, c0 + k, sl], start=(k == 0), stop=(k == nc_ - 1))
                        return ins
                    S.add("pe", mm, reads=[bf] + b_ob[c0:c0 + nc_], writes=(self.psb[bk],))
                    if bi == 0:
                        S.add("dve", lambda e, a=a, bk=bk, g_t=g_t, sl=sl: e.tensor_tensor(out=a[:], in0=ps[:, bk, :], in1=g_t[:, 0, sl], op=ALU.mult),
                              reads=(self.psb[bk], gb), writes=(ab,))
                    else:
                        tm, tmb, _ = tmp.next()
                        S.add("dve", lambda e, tm=tm, bk=bk, g_t=g_t, sl=sl, bi=bi: e.tensor_tensor(out=tm[:], in0=ps[:, bk, :], in1=g_t[:, bi, sl], op=ALU.mult),
                              reads=(self.psb[bk], gb), writes=(tmb,))
                        if bi < 3:
                            S.add("pool", lambda e, a=a, tm=tm: e.tensor_tensor(out=a[:], in0=a[:], in1=tm[:], op=ALU.add), reads=(ab, tmb), writes=(ab,))
                        else:
                            S.add("pool", lambda e, a=a, tm=tm, o=o, sl=sl: e.tensor_tensor(out=o[:, sl], in0=a[:], in1=tm[:], op=ALU.add), reads=(ab, tmb), writes=(ob,))
            self.dma(self.mergedT[m * 128:(m + 1) * 128, :], o[:], (ob,), (), ods)
        self.barrier()
        S.sb_reset(mark)


def host_selc():
    c = np.zeros((128, 16, 96), np.float32)
    p = np.arange(128)[:, None]
    j = np.arange(32)[None, :]
    for qt in range(16):
        cur = 2 * qt + (p >= 64)
        forced = (j == 0) | (j == cur) | (j == cur - 1)
        valid = (j <= cur)
        c[:, qt, 0:32] = valid
        c[:, qt, 32:64] = (valid - 1.0) * 1e9
        c[:, qt, 64:96] = np.where(forced, 1e9, -2e9)
    return c.reshape(128, 16 * 96)


N_CORES_USED = 4


def make_inputs(prog, inputs, xs):
    m = {}
    for nm in prog.declared:
        if nm == "x":
            m[nm] = np.ascontiguousarray(xs, dtype=np.float32)
        elif nm == "consts":
            m[nm] = host_consts()
        elif nm == "sel":
            m[nm] = host_sel()
        elif nm == "selc":
            m[nm] = host_selc()
        elif nm in ("ln_g", "ln_b"):
            m[nm] = np.ascontiguousarray(inputs[nm], dtype=np.float32).reshape(DEPTH * 3, D)
        elif nm in ("cmp_w1", "cmp_w2", "cmp_pos"):
            a = np.asarray(inputs[nm], dtype=np.float32)
            m[nm] = np.ascontiguousarray(a.reshape((DEPTH * 2,) + a.shape[2:]))
        else:
            m[nm] = np.ascontiguousarray(inputs[nm], dtype=np.float32)
    return m


def kernel(**inputs):
    R = N_CORES_USED
    spc = 8 // R
    prog = Prog(spc)
    x = np.asarray(inputs["x"], dtype=np.float32)
    in_maps = [make_inputs(prog, inputs, x[c * spc:(c + 1) * spc]) for c in range(R)]
    res = run_bass_kernel_spmd(prog.nc, in_maps, core_ids=list(range(R)))
    return np.concatenate([np.asarray(res.results[c]["out"]) for c in range(R)], axis=0).astype(np.float32)
```

```python
import math
from contextlib import ExitStack

import numpy as np
import concourse.bass as bass
import concourse.mybir as mybir
from concourse.bass_utils import run_bass_kernel_spmd

F32 = mybir.dt.float32
BF16 = mybir.dt.bfloat16
AF = mybir.ActivationFunctionType
ALU = mybir.AluOpType
AX = mybir.AxisListType

SB_BASE = 16640
SB_END = 229376

D = 4096
L = 2048
DFF = 8192
DEPTH = 2
KC = D // 128
NT = L // 128
ALPHA = (2 * DEPTH) ** 0.25
SCALE = 128 ** -0.5
EPS = 1e-5
D_IN = 28520
N_ALIBI = 28
SLOPES = [2.0 ** (-8.0 * i / N_ALIBI) for i in range(1, N_ALIBI + 1)]
A_SLOPE_IDX = (0, 1, 2, 3, 12, 13, 14, 15, 24, 25, 26, 27)
B_SLOPE_IDX = (4, 5, 6, 7, 8, 9, 10, 11)
D_SLOPE_IDX = (16, 17, 18, 19, 20, 21, 22, 23)
NEGBIG = -30000.0

C_AQ, C_AK, C_AV = 0, 1536, 3072
C_BQ, C_BKV, C_BG = 4608, 5632, 7168
C_CQ, C_CK, C_CV, C_CF = 7192, 8216, 9240, 10264
C_DQ, C_DK, C_DV, C_DIQ, C_DIK, C_DIW = 10272, 11296, 11424, 11552, 12064, 12128
C_G = 12136


class Buf:
    __slots__ = ("name", "w", "rd")

    def __init__(self, name=""):
        self.name = name
        self.w = None
        self.rd = []


class Op:
    __slots__ = ("eng", "fn", "deps", "dma", "dsem", "signal", "tok", "seq")
    _n = 0

    def __init__(self, eng, fn, dma, dsem):
        Op._n += 1
        self.seq = Op._n
        self.eng = eng
        self.fn = fn
        self.deps = []
        self.dma = dma
        self.dsem = dsem
        self.signal = dma
        self.tok = None


ENGS = ("pe", "act", "dve", "pool", "sp")


class Sched:
    def __init__(self, nc):
        self.nc = nc
        self.ops = {e: [] for e in ENGS}
        self.sb_off = SB_BASE
        self.n_dsem = 0
        self.cur_dsem = 0
        self.n_sw = 0
        self.cur_sw = 0
        self.all_bufs = []
        self.nalloc = 0

    def sb(self, shape, dtype, name=None):
        nbytes = 2 if dtype == BF16 else 4
        per_part = nbytes
        for s in shape[1:]:
            per_part *= s
        off = (self.sb_off + 63) // 64 * 64
        assert off + per_part <= SB_END, f"SBUF overflow {name} {off + per_part - SB_END}"
        self.sb_off = off + per_part
        self.nalloc += 1
        return self.nc.alloc_sbuf_tensor_at(f"{name or 't'}{self.nalloc}", list(shape), dtype, offset=off)

    def sb_mark(self):
        return (self.sb_off, self.cur_dsem, self.cur_sw)

    def sb_reset(self, mark):
        self.sb_off, self.cur_dsem, self.cur_sw = mark

    def buf(self, name=""):
        b = Buf(name)
        self.all_bufs.append(b)
        return b

    def dsem(self, sw=False):
        if sw:
            self.cur_sw += 1
            self.n_sw = max(self.n_sw, self.cur_sw)
            return ("s", self.cur_sw - 1)
        self.cur_dsem += 1
        self.n_dsem = max(self.n_dsem, self.cur_dsem)
        return ("h", self.cur_dsem - 1)

    def add(self, eng, fn, reads=(), writes=(), dma=False, dsem=None):
        if dma:
            assert (dsem[0] == "s") == (eng == "pool"), (eng, dsem)
        op = Op(eng, fn, dma, dsem)
        deps = op.deps
        for b in reads:
            w = b.w
            if w is not None:
                if dma or w.eng != eng or w.dma or eng != "pe":
                    deps.append(w)
            b.rd.append(op)
        for b in writes:
            w = b.w
            if w is not None and (dma or w.dma or w.eng != eng):
                deps.append(w)
            for r in b.rd:
                if r is not op and (dma or r.dma or r.eng != eng):
                    deps.append(r)
            b.w = op
            b.rd = []
        for d in deps:
            d.signal = True
        self.ops[eng].append(op)
        return op

    def barrier(self):
        pend = set()
        for b in self.all_bufs:
            if b.w is not None:
                pend.add(b.w)
            for r in b.rd:
                pend.add(r)
        pend = list(pend)
        for e in ENGS:
            op = Op(e, None, False, None)
            for d in pend:
                if d.eng != e or d.dma:
                    op.deps.append(d)
                    d.signal = True
            self.ops[e].append(op)
        for b in self.all_bufs:
            b.w = None
            b.rd = []
        if len(self.all_bufs) > 20000:
            self.all_bufs = self.all_bufs[-5000:]

    def emit(self):
        nc = self.nc
        with ExitStack() as ctx:
            esem = {e: ctx.enter_context(nc.semaphore(f"s_{e}")) for e in ENGS}
            dsems = {("h", i): ctx.enter_context(nc.semaphore(f"d_{i}")) for i in range(self.n_dsem)}
            dsems.update({("s", i): ctx.enter_context(nc.semaphore(f"w_{i}")) for i in range(self.n_sw)})
            for e in ENGS:
                cnt = 0
                for op in self.ops[e]:
                    if op.dma or not op.signal or op.fn is None:
                        continue
                    cnt += 1
                    op.tok = (esem[e], cnt)
            dcnt = {k: 0 for k in dsems}
            alld = [op for e in ENGS for op in self.ops[e] if op.dma]
            alld.sort(key=lambda o: o.seq)
            import bisect
            dhist = {k: ([], []) for k in dsems}
            for op in alld:
                dcnt[op.dsem] += 16
                op.tok = (dsems[op.dsem], dcnt[op.dsem])
                dhist[op.dsem][0].append(op.seq)
                dhist[op.dsem][1].append(dcnt[op.dsem])
            block = ctx.enter_context(nc.Block())

            def run(e, eng):
                known = {}
                for op in self.ops[e]:
                    need = {}
                    for d in op.deps:
                        if d.tok is None:
                            continue
                        s, v = d.tok
                        if d.dma:
                            seqs, vals = dhist[d.dsem]
                            v = vals[bisect.bisect_left(seqs, op.seq) - 1]
                        k = id(s)
                        if known.get(k, 0) < v and (k not in need or need[k][1] < v):
                            need[k] = (s, v)
                    for k, (s, v) in need.items():
                        eng.wait_ge(s, v)
                        known[k] = v
                    if op.fn is None:
                        continue
                    ins = op.fn(eng)
                    if op.dma:
                        ins.then_inc(op.tok[0], 16)
                    elif op.signal:
                        ins.then_inc(op.tok[0], 1)

            @block.tensor
            def _(eng):
                run("pe", eng)

            @block.scalar
            def _(eng):
                run("act", eng)

            @block.vector
            def _(eng):
                run("dve", eng)

            @block.gpsimd
            def _(eng):
                run("pool", eng)

            @block.sync
            def _(eng):
                run("sp", eng)


class Rot:
    def __init__(self, S, n, shape, dtype, name, dma=False, sw=False):
        self.slots = []
        for i in range(n):
            t = S.sb(shape, dtype, name)
            self.slots.append((t, S.buf(name), S.dsem(sw) if (dma or sw) else None))
        self.i = 0

    def next(self):
        s = self.slots[self.i % len(self.slots)]
        self.i += 1
        return s


CO_ID = 0
CO_D0 = 128
CO_CM = 640
CO_LE = 768
CO_LT = 896
CO_ONE = 1024
CO_DC = 1152
CO_D0R = 1280
CO_CMR = 1792
CO_NLE = 2304
CO_D0P = 2432
CO_N = 2944


def host_consts():
    c = np.zeros((128, CO_N), np.float32)
    s = np.arange(128)[:, None]
    c[:, CO_ID:CO_ID + 128] = np.eye(128)
    c[:, CO_D0:CO_D0 + 512] = np.arange(512)[None, :] - s
    t = np.arange(128)[None, :]
    c[:, CO_CM:CO_CM + 128] = (t >= s)
    c[:, CO_LE:CO_LE + 128] = (t <= s)
    c[:, CO_LT:CO_LT + 128] = (t < s)
    c[:, CO_ONE:CO_ONE + 128] = 1.0
    c[:, CO_DC:CO_DC + 128] = s - 16 * t - 31
    c[:, CO_D0P:CO_D0P + 512] = np.maximum(np.arange(512)[None, :] - s, 0)
    for i in range(4):
        c[:, CO_D0R + i * 128:CO_D0R + (i + 1) * 128] = np.maximum(t - s, 0)
        c[:, CO_CMR + i * 128:CO_CMR + (i + 1) * 128] = (t >= s)
    c[:, CO_NLE:CO_NLE + 128] = ((t <= s) - 1.0) * 1e30
    return c


def host_sel():
    e = np.zeros((32, L), np.float32)
    for j in range(32):
        e[j, j * 64:(j + 1) * 64] = 1.0
    return e


class Prog:
    def __init__(self, spc, dbg=None, stages=None):
        self.spc = spc
        self.dbg = dbg or ()
        self.stages = stages
        nc = bass.Bass("TRN2", target_bir_lowering=False)
        self.nc = nc
        S = Sched(nc)
        self.S = S

        self.declared = []

        def din(name, shape):
            self.declared.append(name)
            return nc.dram_tensor(name, list(shape), F32, kind="ExternalInput").ap()

        self.x = din("x", [spc, L, D])
        self.ln_g = din("ln_g", [DEPTH * 3, D])
        self.ln_b = din("ln_b", [DEPTH * 3, D])
        wshapes = dict((("ffn1_w_gate", [DEPTH, D, DFF]), ("ffn1_w_up", [DEPTH, D, DFF]), ("ffn1_w_down", [DEPTH, DFF, D]),
                        ("w_in", [DEPTH, D, D_IN]), ("b_forget", [DEPTH, 8]), ("b_gate", [DEPTH, 4 * D]),
                        ("cmp_w1", [DEPTH * 2, 4096, 512]), ("cmp_w2", [DEPTH * 2, 512, 128]), ("cmp_pos", [DEPTH * 2, 32, 128]),
                        ("w_branch", [DEPTH, 3584, D]), ("w_out", [DEPTH, D, D]),
                        ("ffn2_w_gate", [DEPTH, D, DFF]), ("ffn2_w_up", [DEPTH, D, DFF]), ("ffn2_w_down", [DEPTH, DFF, D])))

        class LazyW(dict):
            def __missing__(s2, nm):
                s2[nm] = din(nm, wshapes[nm])
                return s2[nm]
        self.w = LazyW()
        self.consts_d = din("consts", [128, CO_N])
        self.sel_d = din("sel", [32, L])
        self.selc_d = din("selc", [128, 16 * 96])
        self.out = nc.dram_tensor("out", [spc, L, D], F32, kind="ExternalOutput").ap()

        def scr(name, shape, dt):
            kind = "ExternalOutput" if name in self.dbg else "Internal"
            return nc.dram_tensor(name, list(shape), dt, kind=kind).ap()

        self.xaT = scr("xaT", [D, L], F32)
        self.zT = scr("zT", [D, L], F32)
        self.hT = scr("hT", [DFF, L], BF16)
        self.pfm = scr("pfm", [70, 128, L], BF16)
        self.bgT = scr("bgT", [24, L], F32)
        self.cfT = scr("cfT", [8, L], F32)
        self.iwtm = scr("iwtm", [L, 8], F32)
        self.ptm = scr("ptm", [L, 3200], BF16)
        self.gatesT = scr("gatesT", [4 * D, L], BF16)
        self.obT = scr("obT", [28, 128, L], BF16)
        self.mergedT = scr("mergedT", [D, L], BF16)
        self.xbT_d = nc.dram_tensor("xbT_d", [D, L], BF16, kind=("ExternalOutput" if "xb" in self.dbg else "Internal")).ap()
        self.csd = scr("csd", [8, L], F32)
        self.bmd = scr("bmd", [2, 32, L], BF16)

        self.ps = nc.alloc_psum_tensor("ps", [128, 8, 512], F32)
        self.psb = [S.buf(f"ps{i}") for i in range(8)]
        self.psi = 0
        self.build()
        S.barrier()
        S.emit()

    def bank(self):
        i = self.psi % 8
        self.psi += 1
        return i

    def dma(self, out, in_, reads, writes, dsem, q="sp"):
        self.S.add(q, lambda e: e.dma_start(out=out, in_=in_), reads=reads, writes=writes, dma=True, dsem=dsem)

    def tr_small(self, dst, src, n, reads, wbuf):
        S = self.S
        ps = self.ps
        bk = self.bank()
        ident = self.cst[:n, CO_ID:CO_ID + n]
        S.add("pe", lambda e: e.transpose(ps[:, bk, :n], src, ident), reads=list(reads) + [self.b_cst], writes=(self.psb[bk],))
        S.add("act", lambda e: e.activation(out=dst, in_=ps[:, bk, :n], func=AF.Copy), reads=(self.psb[bk],), writes=(wbuf,))

    def load_colvec(self, dst, src1d, n, wbuf, tmp, tmpb, ds):
        self.dma(tmp[:n, :], src1d.rearrange("(c p) -> c p", p=128), (), (tmpb,), ds)
        self.tr_small(dst, tmp[:n, :], n, (tmpb,), wbuf)

    def reset_bufs(self):
        S = self.S
        for b in self.persist_bufs + self.psb:
            if b not in S.all_bufs[:64]:
                S.all_bufs.insert(0, b)

    def barrier(self):
        self.S.barrier()
        self.reset_bufs()

    def build(self):
        S = self.S
        self.cst = S.sb([128, CO_N], F32, "cst")
        self.cstb = S.sb([128, CO_N], BF16, "cstb")
        self.mark_noxb = S.sb_mark()
        self.xb = S.sb([128, KC, L], BF16, "xb")
        self.b_cst = S.buf("cst")
        self.b_xb = [S.buf(f"xb{c}") for c in range(KC)]
        self.persist_bufs = [self.b_cst] + self.b_xb
        ds = S.dsem()
        self.dma(self.cst[:], self.consts_d, (), (self.b_cst,), ds)
        S.add("dve", lambda e: e.tensor_copy(out=self.cstb[:], in_=self.cst[:]), reads=(self.b_cst,), writes=(self.b_cst,))
        self.base_mark = S.sb_mark()
        st = self.stages or {}
        layers = st.get("layers", list(range(DEPTH)))
        parts = st.get("parts", ("ffn1", "mixer", "ffn2"))
        for s in range(self.spc):
            if not st.get("skip_input"):
                self.ph_input(s, st.get("in_scale", ALPHA))
            for l in layers:
                if "ffn1" in parts:
                    self.ffn(l, 1)
                if "mixer" in parts:
                    self.mixer(l)
                    if "stop" in st:
                        break
                if "ffn2" in parts:
                    self.ffn(l, 2, final=(l == DEPTH - 1))
            if "xb" in self.dbg:
                dsx = S.dsem()
                for c in range(KC):
                    self.dma(self.xbT_d[c * 128:(c + 1) * 128, :], self.xb[:, c, :], (self.b_xb[c],), (), dsx)
            if not st:
                self.ph_output(s)

    def ph_input(self, s, in_scale=ALPHA):
        S = self.S
        nc = self.nc
        mark = S.sb_mark()
        xt = Rot(S, 8, [128, 1024], F32, "xin", dma=True)
        st_a = Rot(S, 3, [128, 512], F32, "xast", dma=True)
        ident = self.cst[:, CO_ID:CO_ID + 128]
        ps = self.ps
        for tg in range(4):
            for fq in range(4):
                tiles = []
                for j in range(4):
                    tl, bf, ds = xt.next()
                    tt = tg * 4 + j
                    self.dma(tl[:], self.x[s, tt * 128:(tt + 1) * 128, fq * 1024:(fq + 1) * 1024], (), (bf,), ds)
                    tiles.append((tl, bf))
                for ci in range(8):
                    c = fq * 8 + ci
                    bk = self.bank()

                    def tr(e, tiles=tiles, ci=ci, bk=bk):
                        for j, (tl, bf) in enumerate(tiles):
                            ins = e.transpose(ps[:, bk, j * 128:(j + 1) * 128], tl[:, ci * 128:(ci + 1) * 128], ident)
                        return ins
                    S.add("pe", tr, reads=[b for _, b in tiles] + [self.b_cst], writes=(self.psb[bk],))
                    sa, sab, sads = st_a.next()
                    S.add("act", lambda e, sa=sa, bk=bk: e.activation(out=sa[:], in_=ps[:, bk, :], func=AF.Copy, scale=float(in_scale)),
                          reads=(self.psb[bk],), writes=(sab,))
                    self.dma(self.xaT[c * 128:(c + 1) * 128, tg * 512:(tg + 1) * 512], sa[:], (sab,), (), sads)
                    S.add("dve", lambda e, c=c, tg=tg, sa=sa: e.tensor_scalar(out=self.xb[:, c, tg * 512:(tg + 1) * 512], in0=sa[:], scalar1=1.0 / float(in_scale),
                                                                         scalar2=None, op0=ALU.mult),
                          reads=(sab,), writes=(self.b_xb[c],))
        self.barrier()
        S.sb_reset(mark)

    def ph_output(self, s):
        S = self.S
        mark = S.sb_mark()
        zi = Rot(S, 6, [128, L], F32, "oin", dma=True)
        so = Rot(S, 3, [128, 512], F32, "oout", dma=True)
        ident = self.cst[:, CO_ID:CO_ID + 128]
        ps = self.ps
        for cg in range(8):
            tiles = []
            for j in range(4):
                c = cg * 4 + j
                tl, bf, ds = zi.next()
                self.dma(tl[:], self.zT[c * 128:(c + 1) * 128, :], (), (bf,), ds)
                tiles.append((tl, bf))
            for tt in range(NT):
                bk = self.bank()

                def tr(e, tiles=tiles, tt=tt, bk=bk):
                    for j, (tl, bf) in enumerate(tiles):
                        ins = e.transpose(ps[:, bk, j * 128:(j + 1) * 128], tl[:, tt * 128:(tt + 1) * 128], ident)
                    return ins
                S.add("pe", tr, reads=[b for _, b in tiles] + [self.b_cst], writes=(self.psb[bk],))
                so_t, sob, sods = so.next()
                if tt % 2 == 0:
                    S.add("dve", lambda e, so_t=so_t, bk=bk: e.tensor_copy(out=so_t[:], in_=ps[:, bk, :]), reads=(self.psb[bk],), writes=(sob,))
                else:
                    S.add("act", lambda e, so_t=so_t, bk=bk: e.activation(out=so_t[:], in_=ps[:, bk, :], func=AF.Copy), reads=(self.psb[bk],), writes=(sob,))
                self.dma(self.out[s, tt * 128:(tt + 1) * 128, cg * 512:(cg + 1) * 512], so_t[:], (sob,), (), sods)
        self.barrier()
        S.sb_reset(mark)

    def ffn(self, l, which, final=False, seq=0):
        wg = self.w[f"ffn{which}_w_gate"][l]
        wu = self.w[f"ffn{which}_w_up"][l]
        wd = self.w[f"ffn{which}_w_down"][l]
        fp = (self.stages or {}).get("ffn_parts", ("up", "down", "ln"))
        if "up" in fp:
            self.ph_up(wg, wu)
        if "down" in fp:
            self.ph_down(self.hT, wd, DFF // 128, 0.5)
        if "ln" in fp:
            self.ph_ln(l * 3 + (0 if which == 1 else 2), final)

    def ph_up(self, wg, wu):
        S = self.S
        ps = self.ps
        mark = S.sb_mark()
        CG = 128
        wsl = {"g": Rot(S, 3, [128, KC, CG], BF16, "wg", sw=True), "u": Rot(S, 3, [128, KC, CG], BF16, "wu", sw=True)}
        sg = Rot(S, 2, [128, 512], F32, "sg")
        hb = Rot(S, 3, [128, 512], BF16, "hb", dma=True)
        xb = self.xb
        import os
        for cg in range(int(os.environ.get('UPN', DFF // CG))):
            cur = {}
            for nm, w in (("g", wg), ("u", wu)):
                tl, bf, ds = wsl[nm].next()
                self.dma(tl[:], w[:, cg * CG:(cg + 1) * CG].rearrange("(kc p) n -> p kc n", p=128), (), (bf,), ds, q="pool")
                cur[nm] = (tl, bf)
            for mi in range(CG // 128):
                m = cg * (CG // 128) + mi
                for th in range(2):
                    banks = [self.bank() for _ in range(4)]
                    for j, nm in enumerate(("g", "u")):
                        tl, bf = cur[nm]
                        for tt in range(2):
                            bk = banks[j * 2 + tt]
                            t = th * 2 + tt

                            def mm(e, tl=tl, bk=bk, t=t, mi=mi):
                                for k in range(KC):
                                    ins = e.matmul(ps[:, bk, :], tl[:, k, mi * 128:(mi + 1) * 128], xb[:, k, t * 512:(t + 1) * 512],
                                                   start=(k == 0), stop=(k == KC - 1))
                                return ins
                            S.add("pe", mm, reads=[bf] + self.b_xb, writes=(self.psb[bk],))
                    for tt in range(2):
                        t = th * 2 + tt
                        sgt, sgb, _ = sg.next()
                        hbt, hbb, hds = hb.next()
                        bg, bu = banks[tt], banks[2 + tt]
                        S.add("act", lambda e, sgt=sgt, bg=bg: e.activation(out=sgt[:], in_=ps[:, bg, :], func=AF.Silu),
                              reads=(self.psb[bg],), writes=(sgb,))
                        S.add("dve", lambda e, hbt=hbt, sgt=sgt, bu=bu: e.tensor_tensor(out=hbt[:], in0=sgt[:], in1=ps[:, bu, :], op=ALU.mult),
                              reads=(sgb, self.psb[bu]), writes=(hbb,))
                        self.dma(self.hT[m * 128:(m + 1) * 128, t * 512:(t + 1) * 512], hbt[:], (hbb,), (), hds)
        self.barrier()
        S.sb_reset(mark)

    def ph_down(self, srcT, wd, kc, mul):
        S = self.S
        ps = self.ps
        mark = S.sb_mark()
        S.sb_reset(self.mark_noxb)
        TH = 1024 if kc > 32 else 2048
        npass = L // TH
        ntt = TH // 512
        hs = S.sb([128, kc, TH], BF16, "hs")
        hsb = [S.buf("hs") for _ in range(kc)]
        hds = S.dsem()
        wsl = Rot(S, 2, [128, kc, 128], BF16, "wd", sw=True)
        xat = Rot(S, 3, [128, 512], F32, "xat", dma=True)
        zst = Rot(S, 3, [128, 512], F32, "zst", dma=True)
        for p in range(npass):
            for k in range(kc):
                self.dma(hs[:, k, :], srcT[k * 128:(k + 1) * 128, p * TH:(p + 1) * TH], (), (hsb[k],), hds)
            for m in range(D // 128):
                tl, bf, ds = wsl.next()
                nsplit = 4
                ks = kc // nsplit
                for q in range(nsplit):
                    self.dma(tl[:, q * ks:(q + 1) * ks, :],
                             wd[q * ks * 128:(q + 1) * ks * 128, m * 128:(m + 1) * 128].rearrange("(kc p) n -> p kc n", p=128),
                             (), (bf,), ds, q="pool")
                for tt in range(ntt):
                    t0 = p * TH + tt * 512
                    bk = self.bank()

                    def mm(e, tl=tl, bk=bk, tt=tt):
                        for k in range(kc):
                            ins = e.matmul(ps[:, bk, :], tl[:, k, :], hs[:, k, tt * 512:(tt + 1) * 512], start=(k == 0), stop=(k == kc - 1))
                        return ins
                    S.add("pe", mm, reads=[bf] + hsb, writes=(self.psb[bk],))
                    xa, xab, xads = xat.next()
                    self.dma(xa[:], self.xaT[m * 128:(m + 1) * 128, t0:t0 + 512], (), (xab,), xads)
                    z, zb, zds = zst.next()
                    S.add("dve", lambda e, z=z, xa=xa, bk=bk: e.scalar_tensor_tensor(out=z[:], in0=ps[:, bk, :], scalar=float(mul), in1=xa[:],
                                                                                  op0=ALU.mult, op1=ALU.add),
                          reads=(self.psb[bk], xab), writes=(zb,))
                    self.dma(self.zT[m * 128:(m + 1) * 128, t0:t0 + 512], z[:], (zb,), (), zds)
        self.barrier()
        S.sb_reset(mark)

    def ph_ln(self, idx, final):
        S = self.S
        ps = self.ps
        mark = S.sb_mark()
        H = L // 2
        zin = Rot(S, 2, [128, H], F32, "zin", dma=True)
        acc1 = S.sb([128, H], F32, "acc1")
        acc2 = S.sb([128, H], F32, "acc2")
        b_a1, b_a2 = S.buf("a1"), S.buf("a2")
        sq = Rot(S, 2, [128, H], F32, "sq")
        gb = S.sb([128, 4, KC], F32, "gb")
        b_gb = S.buf("gb")
        gds = S.dsem()
        cvt = S.sb([32, 2, 128], F32, "cvtmp")
        b_cvt = S.buf("cvt")
        self.load_colvec(gb[:, 0, :], self.ln_g[idx], KC, b_gb, cvt[:, 0, :], b_cvt, gds)
        self.load_colvec(gb[:, 1, :], self.ln_b[idx], KC, b_gb, cvt[:, 1, :], b_cvt, gds)
        S.add("dve", lambda e: e.tensor_scalar(out=gb[:, 2:4, :], in0=gb[:, 0:2, :], scalar1=float(ALPHA), scalar2=None, op0=ALU.mult),
              reads=(b_gb,), writes=(b_gb,))
        ones = self.cst[:, CO_ONE:CO_ONE + 128]
        mu = S.sb([128, H], F32, "mu")
        rs = S.sb([128, H], F32, "rs")
        b_mu, b_rs = S.buf("mu"), S.buf("rs")
        yt = Rot(S, 2, [128, H], F32, "yt")
        xo = Rot(S, 2, [128, H], F32, "xo", dma=True)
        for hf in range(2):
            hs_ = slice(hf * H, (hf + 1) * H)
            for c in range(KC):
                z, zb, zds = zin.next()
                self.dma(z[:], self.zT[c * 128:(c + 1) * 128, hs_], (), (zb,), zds)
                s_t, s_b, _ = sq.next()
                S.add("act", lambda e, s_t=s_t, z=z: e.activation(out=s_t[:], in_=z[:], func=AF.Square), reads=(zb,), writes=(s_b,))
                if c == 0:
                    S.add("pool", lambda e, z=z: e.tensor_copy(out=acc1[:], in_=z[:]), reads=(zb,), writes=(b_a1,))
                    S.add("dve", lambda e, s_t=s_t: e.tensor_copy(out=acc2[:], in_=s_t[:]), reads=(s_b,), writes=(b_a2,))
                else:
                    S.add("pool", lambda e, z=z: e.tensor_tensor(out=acc1[:], in0=acc1[:], in1=z[:], op=ALU.add), reads=(zb, b_a1), writes=(b_a1,))
                    S.add("dve", lambda e, s_t=s_t: e.tensor_tensor(out=acc2[:], in0=acc2[:], in1=s_t[:], op=ALU.add), reads=(s_b, b_a2), writes=(b_a2,))
            for tt in range(H // 512):
                b1, b2 = self.bank(), self.bank()
                sl = slice(tt * 512, (tt + 1) * 512)
                S.add("pe", lambda e, b1=b1, sl=sl: e.matmul(ps[:, b1, :], ones, acc1[:, sl], start=True, stop=True),
                      reads=(b_a1, self.b_cst), writes=(self.psb[b1],))
                S.add("pe", lambda e, b2=b2, sl=sl: e.matmul(ps[:, b2, :], ones, acc2[:, sl], start=True, stop=True),
                      reads=(b_a2, self.b_cst), writes=(self.psb[b2],))
                S.add("act", lambda e, b1=b1, sl=sl: e.activation(out=mu[:, sl], in_=ps[:, b1, :], func=AF.Copy, scale=1.0 / D),
                      reads=(self.psb[b1],), writes=(b_mu,))
                S.add("dve", lambda e, sl=sl: e.tensor_tensor(out=rs[:, sl], in0=mu[:, sl], in1=mu[:, sl], op=ALU.mult),
                      reads=(b_mu,), writes=(b_rs,))
                S.add("dve", lambda e, b2=b2, sl=sl: e.scalar_tensor_tensor(out=rs[:, sl], in0=ps[:, b2, :], scalar=1.0 / D, in1=rs[:, sl],
                                                                         op0=ALU.mult, op1=ALU.subtract),
                      reads=(self.psb[b2], b_rs), writes=(b_rs,))
                S.add("dve", lambda e, sl=sl: e.tensor_scalar(out=rs[:, sl], in0=rs[:, sl], scalar1=float(EPS), scalar2=None, op0=ALU.add),
                      reads=(b_rs,), writes=(b_rs,))
                S.add("act", lambda e, sl=sl: e.activation(out=rs[:, sl], in_=rs[:, sl], func=AF.Sqrt), reads=(b_rs,), writes=(b_rs,))
                S.add("dve", lambda e, sl=sl: e.reciprocal(out=rs[:, sl], in_=rs[:, sl]), reads=(b_rs,), writes=(b_rs,))
            for c in range(KC):
                z, zb, zds = zin.next()
                self.dma(z[:], self.zT[c * 128:(c + 1) * 128, hs_], (), (zb,), zds)
                y, yb, _ = yt.next()
                S.add("pool", lambda e, y=y, z=z: e.tensor_tensor(out=y[:], in0=z[:], in1=mu[:], op=ALU.subtract), reads=(zb, b_mu), writes=(yb,))
                S.add("dve", lambda e, y=y: e.tensor_tensor(out=y[:], in0=y[:], in1=rs[:], op=ALU.mult), reads=(yb, b_rs), writes=(yb,))
                S.add("act", lambda e, y=y, c=c, hs_=hs_: e.activation(out=self.xb[:, c, hs_], in_=y[:], func=AF.Identity, bias=gb[:, 1, c:c + 1], scale=gb[:, 0, c:c + 1]),
                      reads=(yb, b_gb), writes=(self.b_xb[c],))
                o, ob, ods = xo.next()
                if final:
                    S.add("act", lambda e, y=y, o=o, c=c: e.activation(out=o[:], in_=y[:], func=AF.Identity, bias=gb[:, 1, c:c + 1], scale=gb[:, 0, c:c + 1]),
                          reads=(yb, b_gb), writes=(ob,))
                    self.dma(self.zT[c * 128:(c + 1) * 128, hs_], o[:], (ob, zb), (), ods)
                else:
                    S.add("act", lambda e, y=y, o=o, c=c: e.activation(out=o[:], in_=y[:], func=AF.Identity, bias=gb[:, 3, c:c + 1], scale=gb[:, 2, c:c + 1]),
                          reads=(yb, b_gb), writes=(ob,))
                    self.dma(self.xaT[c * 128:(c + 1) * 128, hs_], o[:], (ob,), (), ods)
        self.barrier()
        S.sb_reset(mark)

    def mixer(self, l):
        self.ph_proj(l)
        st = self.stages or {}
        if st.get("stop") == ("proj", l):
            return
        self.S.sb_reset(self.mark_noxb)
        import os
        mx = os.environ.get("MIX", "acdb")
        if "a" in mx:
            self.mix_a()
        if "c" in mx:
            self.mix_c()
        if "d" in mx:
            self.mix_d()
        if "b" in mx:
            self.mix_b(l)
        self.S.sb_reset(self.base_mark)
        if st.get("stop") == ("attn", l):
            return
        self.ph_merge(l)
        self.ph_down(self.mergedT, self.w["w_out"][l], KC, 1.0)
        self.ph_ln(l * 3 + 1, False)

    def ph_proj(self, l):
        S = self.S
        ps = self.ps
        xb = self.xb
        w = self.w["w_in"][l]
        mark = S.sb_mark()
        wsl = Rot(S, 2, [128, KC, 256], BF16, "wfm", sw=True)
        stg = Rot(S, 3, [128, L], BF16, "pst", dma=True)
        bgate = S.sb([128, 128], F32, "bgate")
        b_bg = S.buf("bgate")
        ds0 = S.dsem()
        cvt = S.sb([128, 128], F32, "cvtmp")
        b_cvt = S.buf("cvt")
        self.load_colvec(bgate[:], self.w["b_gate"][l], 128, b_bg, cvt, b_cvt, ds0)
        bfor = S.sb([8, 1], F32, "bfor")
        self.dma(bfor[:], self.w["b_forget"][l].rearrange("(p o) -> p o", o=1), (b_bg,), (b_bg,), ds0)
        sm = S.sb([128, L], F32, "smallst")
        b_sm = S.buf("sm")
        dsm = S.dsem()

        def fm_block(tl, bf, bi, ncol, kind, dst, arg=None):
            st_t, st_b, st_ds = (None, None, None)
            if kind in ("q", "k", "gate"):
                st_t, st_b, st_ds = stg.next()
            for t in range(4):
                bk = self.bank()

                def mm(e, tl=tl, bk=bk, t=t, bi=bi, ncol=ncol):
                    for k in range(KC):
                        ins = e.matmul(ps[:ncol, bk, :], tl[:, k, bi * 128:bi * 128 + ncol], xb[:, k, t * 512:(t + 1) * 512],
                                       start=(k == 0), stop=(k == KC - 1))
                    return ins
                S.add("pe", mm, reads=[bf] + self.b_xb, writes=(self.psb[bk],))
                sl = slice(t * 512, (t + 1) * 512)
                if kind == "q":
                    S.add("act", lambda e, st_t=st_t, bk=bk, sl=sl: e.activation(out=st_t[:, sl], in_=ps[:, bk, :], func=AF.Copy, scale=SCALE),
                          reads=(self.psb[bk],), writes=(st_b,))
                elif kind == "k":
                    S.add("dve", lambda e, st_t=st_t, bk=bk, sl=sl: e.tensor_copy(out=st_t[:, sl], in_=ps[:, bk, :]),
                          reads=(self.psb[bk],), writes=(st_b,))
                elif kind == "gate":
                    S.add("act", lambda e, st_t=st_t, bk=bk, sl=sl, arg=arg: e.activation(out=st_t[:, sl], in_=ps[:, bk, :], func=AF.Sigmoid,
                                                                                       bias=bgate[:, arg:arg + 1]),
                          reads=(self.psb[bk], b_bg), writes=(st_b,))
                elif kind == "bg":
                    S.add("act", lambda e, bk=bk, sl=sl: e.activation(out=sm[:24, sl], in_=ps[:24, bk, :], func=AF.Sigmoid),
                          reads=(self.psb[bk],), writes=(b_sm,))
                elif kind == "cf":
                    S.add("act", lambda e, bk=bk, sl=sl: e.activation(out=sm[:8, sl], in_=ps[:8, bk, :], func=AF.Identity, bias=bfor[:, 0:1]),
                          reads=(self.psb[bk], b_bg), writes=(b_sm,))
            if kind in ("q", "k", "gate"):
                self.dma(dst, st_t[:], (st_b,), (), st_ds)
            elif kind == "bg":
                self.dma(self.bgT, sm[:24, :], (b_sm,), (), dsm)
            elif kind == "cf":
                self.dma(self.cfT, sm[:8, :], (b_sm,), (), dsm)

        def load(col0, ncols, dup=False):
            tl, bf, ds = wsl.next()
            if dup:
                for h in range(2):
                    self.dma(tl[:, :, h * 64:(h + 1) * 64], w[:, col0:col0 + 64].rearrange("(kc p) n -> p kc n", p=128), (), (bf,), ds, q="pool")
            else:
                self.dma(tl[:, :, :ncols], w[:, col0:col0 + ncols].rearrange("(kc p) n -> p kc n", p=128), (), (bf,), ds, q="pool")
            return tl, bf

        def fm_range(col0, nblk, blk0, kind):
            b = 0
            while b < nblk:
                n = min(2, nblk - b)
                tl, bf = load(col0 + b * 128, n * 128)
                for i in range(n):
                    fm_block(tl, bf, i, 128, kind, self.pfm[blk0 + b + i])
                b += n

        fm_range(C_AQ, 12, 0, "q")
        fm_range(C_AK, 12, 12, "k")
        fm_range(C_BQ, 8, 24, "q")
        fm_range(C_BKV + 0, 2, 32, "k")
        fm_range(C_BKV + 256, 2, 34, "k")
        fm_range(C_BKV + 512, 2, 36, "k")
        fm_range(C_BKV + 1024, 2, 38, "k")
        fm_range(C_CQ, 8, 40, "q")
        fm_range(C_CK, 8, 48, "k")
        fm_range(C_DQ, 8, 56, "q")
        fm_range(C_DK, 1, 64, "k")
        fm_range(C_DIQ, 4, 65, "k")
        tl, bf = load(C_DIK, 64, dup=True)
        fm_block(tl, bf, 0, 128, "k", self.pfm[69])
        tl, bf = load(C_BG, 24)
        fm_block(tl, bf, 0, 24, "bg", None)
        tl, bf = load(C_CF, 8)
        fm_block(tl, bf, 0, 8, "cf", None)
        for gb in range(64):
            tl, bf = load(C_G + gb * 256, 256)
            for i in range(2):
                blk = gb * 2 + i
                fm_block(tl, bf, i, 128, "gate", self.gatesT[blk * 128:(blk + 1) * 128, :], arg=blk)
        self.barrier()
        S.sb_reset(mark)
        wtl = Rot(S, 2, [128, KC, 256], BF16, "wtm", sw=True)
        tst = Rot(S, 3, [128, 512], BF16, "tst", dma=True)
        ist = Rot(S, 2, [128, 8], F32, "ist", dma=True)
        tmjobs = [(C_AV + i * 256, 256, i * 256) for i in range(6)] + [(C_BKV + 6 * 128, 256, 1536), (C_BKV + 10 * 128, 256, 1792)] + \
                 [(C_CV + i * 256, 256, 2048 + i * 256) for i in range(4)] + [(C_DV, 128, 3072), (C_DIW, 8, -1)]
        for (col0, ncols, dcol) in tmjobs:
            tl, bf, ds = wtl.next()
            self.dma(tl[:, :, :ncols], w[:, col0:col0 + ncols].rearrange("(kc p) n -> p kc n", p=128), (), (bf,), ds, q="pool")
            for tt in range(NT):
                bk = self.bank()

                def mm(e, tl=tl, bk=bk, tt=tt, ncols=ncols):
                    for k in range(KC):
                        ins = e.matmul(ps[:, bk, :ncols], xb[:, k, tt * 128:(tt + 1) * 128], tl[:, k, :ncols], start=(k == 0), stop=(k == KC - 1))
                    return ins
                S.add("pe", mm, reads=[bf] + self.b_xb, writes=(self.psb[bk],))
                if dcol >= 0:
                    o, ob, ods = tst.next()
                    if tt % 2 == 0:
                        S.add("dve", lambda e, o=o, bk=bk, ncols=ncols: e.tensor_copy(out=o[:, :ncols], in_=ps[:, bk, :ncols]), reads=(self.psb[bk],), writes=(ob,))
                    else:
                        S.add("act", lambda e, o=o, bk=bk, ncols=ncols: e.activation(out=o[:, :ncols], in_=ps[:, bk, :ncols], func=AF.Copy), reads=(self.psb[bk],), writes=(ob,))
                    self.dma(self.ptm[tt * 128:(tt + 1) * 128, dcol:dcol + ncols], o[:, :ncols], (ob,), (), ods)
                else:
                    o, ob, ods = ist.next()
                    S.add("dve", lambda e, o=o, bk=bk: e.tensor_copy(out=o[:], in_=ps[:, bk, :8]), reads=(self.psb[bk],), writes=(ob,))
                    self.dma(self.iwtm[tt * 128:(tt + 1) * 128, :], o[:], (ob,), (), ods)
        self.barrier()
        S.sb_reset(mark)

    def attn_job(self, units, evac, ptile, tmpt):
        S = self.S
        ps = self.ps
        bo, bd = self.bank(), self.bank()
        ones = self.cstb[:, CO_ONE:CO_ONE + 128]
        first = True
        for u in units:
            nq, c0 = u["nq"], u["c0"]
            bs = self.bank()
            while bs in (bo, bd):
                bs = self.bank()
            rd = list(u["reads"])
            S.add("pe", lambda e, u=u, bs=bs, nq=nq: e.matmul(ps[:, bs, :nq], u["kT"], u["qT"], start=True, stop=True),
                  reads=rd, writes=(self.psb[bs],))
            p, pb, _ = ptile.next()
            bias = u.get("bias")
            cb = float(u.get("cb", 0.0))
            if bias is None:
                S.add("act", lambda e, p=p, bs=bs, nq=nq, cb=cb: e.activation(out=p[:, :nq], in_=ps[:, bs, :nq], func=AF.Exp, bias=cb),
                      reads=(self.psb[bs],), writes=(pb,))
            else:
                tm, tmb, _ = tmpt.next()
                if bias[0] == "alibi":
                    slope, d0 = bias[1], bias[2]
                    S.add("dve", lambda e, tm=tm, bs=bs, nq=nq, slope=slope, d0=d0: e.scalar_tensor_tensor(
                        out=tm[:, :nq], in0=d0, scalar=-float(slope), in1=ps[:, bs, :nq], op0=ALU.mult, op1=ALU.add),
                        reads=(self.psb[bs], self.b_cst), writes=(tmb,))
                else:
                    csT, csbc, brd = bias[1], bias[2], bias[3]
                    S.add("dve", lambda e, tm=tm, bs=bs, nq=nq, csT=csT, csbc=csbc: e.scalar_tensor_tensor(
                        out=tm[:, :nq], in0=ps[:, bs, :nq], scalar=csT, in1=csbc, op0=ALU.add, op1=ALU.subtract),
                        reads=[self.psb[bs]] + list(brd), writes=(tmb,))
                S.add("act", lambda e, p=p, tm=tm, nq=nq, cb=cb: e.activation(out=p[:, :nq], in_=tm[:, :nq], func=AF.Exp, bias=cb),
                      reads=(tmb,), writes=(pb,))
            for (mo, mw, map_) in u.get("masks", ()):
                S.add("pool", lambda e, p=p, mo=mo, mw=mw, map_=map_: e.tensor_tensor(out=p[:, mo:mo + mw], in0=p[:, mo:mo + mw], in1=map_, op=ALU.mult),
                      reads=(pb, self.b_cst), writes=(pb,))
            full = u.get("full")
            if full is not None:
                fap, fb = full
                S.add("pool", lambda e, p=p, nq=nq, fap=fap: e.tensor_tensor(out=p[:, :nq], in0=p[:, :nq], in1=fap, op=ALU.mult),
                      reads=(pb, fb), writes=(pb,))

            def pv(e, u=u, p=p, nq=nq, c0=c0, first=first):
                e.matmul(ps[:, bo, c0:c0 + nq], u["v"], p[:, :nq], start=first, stop=False, skip_group_check=True)
                return e.matmul(ps[:, bd, c0:c0 + nq], ones, p[:, :nq], start=first, stop=False, skip_group_check=True)
            S.add("pe", pv, reads=[pb, self.b_cst] + list(u["vreads"]), writes=(self.psb[bo], self.psb[bd]))
            first = False
        evac(bo, bd)

    def evac_norm(self, dst, dstb, c0=0, n=512):
        S = self.S
        ps = self.ps

        def f(bo, bd):
            r, rb, _ = self.rden.next()
            S.add("dve", lambda e, r=r, bd=bd: e.reciprocal(out=r[:, :n], in_=ps[:, bd, :n]), reads=(self.psb[bd],), writes=(rb,))
            S.add("dve", lambda e, r=r, bo=bo: e.tensor_tensor(out=dst[:, c0:c0 + n], in0=ps[:, bo, :n], in1=r[:, :n], op=ALU.mult),
                  reads=(self.psb[bo], rb), writes=(dstb,))
        return f

    def causal_units(self, qg, kT, qT, vt, bias_fn, reads, vreads, n_prev=None, full_fn=None):
        CM = self.cstb[:, CO_CM:CO_CM + 128]
        LT = self.cstb[:, CO_LT:CO_LT + 128]
        LE = self.cstb[:, CO_LE:CO_LE + 128]
        units = []
        k_lo = 0 if n_prev is None else max(0, 4 * qg - n_prev)
        for kt in range(k_lo, 4 * qg + 4):
            q_lo = max(kt, 4 * qg)
            q_hi = 4 * qg + 3 if n_prev is None else min(kt + n_prev, 4 * qg + 3)
            c0 = (q_lo - 4 * qg) * 128
            nq = (q_hi - q_lo + 1) * 128
            r0 = q_lo - kt
            masks = []
            if q_lo == kt:
                masks.append((0, 128, CM))
            if n_prev is not None and q_hi == kt + n_prev:
                masks.append((nq - 128, 128, LE if n_prev == 1 else LT))
            u = dict(kT=kT[:, kt * 128:(kt + 1) * 128], qT=qT[:, qg * 512 + c0: qg * 512 + c0 + nq], v=vt[:, kt, :],
                     c0=c0, nq=nq, masks=masks, reads=reads, vreads=vreads)
            bias_fn(u, kt, qg, c0, nq, r0)
            if full_fn is not None:
                u["full"] = full_fn(kt, c0, nq)
            units.append(u)
        return units

    def alibi_bias(self, slope):
        d0t = self.cst[:, CO_D0:CO_D0 + 512]
        d0p = self.cst[:, CO_D0P:CO_D0P + 512]

        def f(u, kt, qg, c0, nq, r0):
            u["bias"] = ("alibi", slope, (d0p if r0 == 0 else d0t)[:, :nq])
            u["cb"] = -slope * 128.0 * r0
        return f

    def load_qkv(self, qblk, kblk, vcol, rot):
        q, qb, qds = rot["q"].next()
        k, kb, kds = rot["k"].next()
        v, vb, vds = rot["v"].next()
        self.dma(q[:], self.pfm[qblk], (), (qb,), qds)
        self.dma(k[:], self.pfm[kblk], (), (kb,), kds)
        self.dma(v[:], self.ptm[:, vcol:vcol + 128].rearrange("(kt p) d -> p kt d", p=128), (), (vb,), vds)
        return (q, qb), (k, kb), (v, vb)

    def attn_common(self):
        S = self.S
        self.ptile = Rot(S, 3, [128, 512], BF16, "pt")
        self.tmpt = Rot(S, 3, [128, 512], F32, "tmp")
        self.rden = Rot(S, 2, [128, 512], F32, "rden")

    def mix_a(self):
        S = self.S
        ps = self.ps
        mark = S.sb_mark()
        self.attn_common()
        rot = {"q": Rot(S, 3, [128, L], BF16, "aq", dma=True), "k": Rot(S, 3, [128, L], BF16, "ak", dma=True),
               "v": Rot(S, 3, [128, NT, 128], BF16, "av", dma=True)}
        OA = Rot(S, 2, [128, L], F32, "OA")
        DA = Rot(S, 2, [128, L], F32, "DA")
        ob = Rot(S, 2, [128, L], BF16, "oab", dma=True)
        CMr = self.cstb[:, CO_CMR:CO_CMR + 512]
        D0r = self.cst[:, CO_D0R:CO_D0R + 512]
        ones = self.cstb[:, CO_ONE:CO_ONE + 128]
        for h in range(4):
            oa, oab, _ = OA.next()
            da, dab, _ = DA.next()
            slope = SLOPES[A_SLOPE_IDX[h]]
            (q, qb), (k, kb), (v, vb) = self.load_qkv(h, 12 + h, h * 128, rot)
            for qg in range(4):
                units = self.causal_units(qg, k, q, v, self.alibi_bias(slope), (qb, kb), (vb,), n_prev=1)

                def ev(bo, bd, qg=qg, oa=oa, da=da, oab=oab, dab=dab):
                    sl = slice(qg * 512, (qg + 1) * 512)
                    S.add("dve", lambda e: e.tensor_copy(out=oa[:, sl], in_=ps[:, bo, :]), reads=(self.psb[bo],), writes=(oab,))
                    S.add("act", lambda e: e.activation(out=da[:, sl], in_=ps[:, bd, :], func=AF.Copy), reads=(self.psb[bd],), writes=(dab,))
                self.attn_job(units, ev, self.ptile, self.tmpt)
            hd = 4 + h
            slope = SLOPES[A_SLOPE_IDX[hd]] * 4.0
            q, qb, qds = rot["q"].next()
            k, kb, kds = rot["k"].next()
            v, vb, vds = rot["v"].next()
            self.dma(q[:], self.pfm[hd], (), (qb,), qds)
            self.dma(k[:], self.pfm[12 + hd], (), (kb,), kds)
            for r in range(4):
                self.dma(v[:, r * 4:(r + 1) * 4, :],
                         self.ptm[:, hd * 128:(hd + 1) * 128].rearrange("(kt j r) d -> r j kt d", r=4, j=128)[r], (), (vb,), vds)
            for r in range(4):
                qs = q[:, r:L:4]
                ks = k[:, r:L:4]
                units = self.causal_units(0, ks, qs, v[:, r * 4:(r + 1) * 4, :], self.alibi_bias(slope), (qb, kb), (vb,), n_prev=1)

                def ev(bo, bd, r=r, oa=oa, da=da, oab=oab, dab=dab):
                    S.add("dve", lambda e: e.tensor_tensor(out=oa[:, r:L:4], in0=oa[:, r:L:4], in1=ps[:, bo, :], op=ALU.add),
                          reads=(self.psb[bo], oab), writes=(oab,))
                    S.add("dve", lambda e: e.tensor_tensor(out=da[:, r:L:4], in0=da[:, r:L:4], in1=ps[:, bd, :], op=ALU.add),
                          reads=(self.psb[bd], dab), writes=(dab,))
                self.attn_job(units, ev, self.ptile, self.tmpt)
            hd = 8 + h
            slope = SLOPES[A_SLOPE_IDX[hd]] * 16.0
            q, qb, qds = rot["q"].next()
            k, kb, kds = rot["k"].next()
            v, vb, vds = rot["v"].next()
            self.dma(q[:], self.pfm[hd], (), (qb,), qds)
            self.dma(k[:], self.pfm[12 + hd], (), (kb,), kds)
            for r4 in range(4):
                self.dma(v[:, r4 * 4:(r4 + 1) * 4, :],
                         self.ptm[:, hd * 128:(hd + 1) * 128].rearrange("(j r) d -> j r d", r=16)[:, r4 * 4:(r4 + 1) * 4, :], (), (vb,), vds)
            for r4 in range(4):
                bo, bd, bs = self.bank(), self.bank(), self.bank()

                def qk(e, r4=r4, bs=bs, q=q, k=k):
                    for i in range(4):
                        r = r4 * 4 + i
                        ins = e.matmul(ps[:, bs, i * 128:(i + 1) * 128], k[:, r:L:16], q[:, r:L:16], start=True, stop=True)
                    return ins
                S.add("pe", qk, reads=(qb, kb), writes=(self.psb[bs],))
                tm, tmb, _ = self.tmpt.next()
                p, pb, _ = self.ptile.next()
                S.add("dve", lambda e, tm=tm, bs=bs, slope=slope: e.scalar_tensor_tensor(out=tm[:], in0=D0r, scalar=-float(slope), in1=ps[:, bs, :],
                                                                                       op0=ALU.mult, op1=ALU.add),
                      reads=(self.psb[bs], self.b_cst), writes=(tmb,))
                S.add("act", lambda e, p=p, tm=tm: e.activation(out=p[:], in_=tm[:], func=AF.Exp), reads=(tmb,), writes=(pb,))
                S.add("pool", lambda e, p=p: e.tensor_tensor(out=p[:], in0=p[:], in1=CMr, op=ALU.mult), reads=(pb, self.b_cst), writes=(pb,))

                def pv(e, r4=r4, p=p, bo=bo, bd=bd, v=v):
                    for i in range(4):
                        e.matmul(ps[:, bo, i * 128:(i + 1) * 128], v[:, r4 * 4 + i, :], p[:, i * 128:(i + 1) * 128], start=(i == 0), stop=False, skip_group_check=True)
                    for i in range(4):
                        ins = e.matmul(ps[:, bd, i * 128:(i + 1) * 128], ones, p[:, i * 128:(i + 1) * 128], start=(i == 0), stop=False, skip_group_check=True)
                    return ins
                S.add("pe", pv, reads=(pb, vb, self.b_cst), writes=(self.psb[bo], self.psb[bd]))
                oav = oa[:].rearrange("p (j r) -> p r j", r=16)[:, r4 * 4:(r4 + 1) * 4, :]
                dav = da[:].rearrange("p (j r) -> p r j", r=16)[:, r4 * 4:(r4 + 1) * 4, :]
                S.add("dve", lambda e, oav=oav, bo=bo: e.tensor_tensor(out=oav, in0=oav, in1=ps[:, bo, :].rearrange("p (i j) -> p i j", i=4), op=ALU.add),
                      reads=(self.psb[bo], oab), writes=(oab,))
                S.add("dve", lambda e, dav=dav, bd=bd: e.tensor_tensor(out=dav, in0=dav, in1=ps[:, bd, :].rearrange("p (i j) -> p i j", i=4), op=ALU.add),
                      reads=(self.psb[bd], dab), writes=(dab,))
            o, obb, ods = ob.next()
            S.add("dve", lambda e, da=da: e.reciprocal(out=da[:], in_=da[:]), reads=(dab,), writes=(dab,))
            S.add("dve", lambda e, o=o, oa=oa, da=da: e.tensor_tensor(out=o[:], in0=oa[:], in1=da[:], op=ALU.mult), reads=(oab, dab), writes=(obb,))
            self.dma(self.obT[h], o[:], (obb,), (), ods)
        self.barrier()
        S.sb_reset(mark)

    def mix_c(self):
        S = self.S
        ps = self.ps
        mark = S.sb_mark()
        self.attn_common()
        rot = {"q": Rot(S, 2, [128, L], BF16, "cq", dma=True), "k": Rot(S, 2, [128, L], BF16, "ck", dma=True),
               "v": Rot(S, 2, [128, NT, 128], BF16, "cv", dma=True)}
        ob = Rot(S, 2, [128, L], BF16, "ocb", dma=True)
        cf = S.sb([8, L], F32, "cf")
        cs = S.sb([8, L], F32, "cs")
        onesr = S.sb([8, L], F32, "onesr")
        b_cf = S.buf("cf")
        ds = S.dsem()
        self.dma(cf[:], self.cfT, (), (b_cf,), ds)
        S.add("act", lambda e: e.activation(out=cf[:], in_=cf[:], func=AF.Exp, scale=-1.0), reads=(b_cf,), writes=(b_cf,))
        S.add("act", lambda e: e.activation(out=cf[:], in_=cf[:], func=AF.Ln, bias=1.0), reads=(b_cf,), writes=(b_cf,))
        S.add("dve", lambda e: e.memset(onesr[:], 1.0), reads=(), writes=(b_cf,))
        S.add("dve", lambda e: e.tensor_tensor_scan(out=cs[:], data0=onesr[:], data1=cf[:], initial=0.0, op0=ALU.mult, op1=ALU.add),
              reads=(b_cf,), writes=(b_cf,))
        b_csd = S.buf("csd")
        self.dma(self.csd, cs[:], (b_cf,), (b_csd,), ds)
        csT = S.sb([128, NT, 8], F32, "csT")
        b_csT = S.buf("csT")
        for tt in range(NT):
            self.tr_small(csT[:, tt, :], cs[:, tt * 128:(tt + 1) * 128], 8, (b_cf,), b_csT)
        cbc = Rot(S, 2, [128, L], F32, "cbc", dma=True)
        for h in range(8):
            (q, qb), (k, kb), (v, vb) = self.load_qkv(40 + h, 48 + h, 2048 + h * 128, rot)
            cb_t, cb_b, cb_ds = cbc.next()
            self.dma(cb_t[:], self.csd[h:h + 1, :].to_broadcast([128, L]), (b_csd,), (cb_b,), cb_ds)
            o, obb, ods = ob.next()

            def bias_fn(u, kt, qg, c0, nq, r0, h=h, cb_t=cb_t, cb_b=cb_b):
                u["bias"] = ("fox", csT[:, kt, h:h + 1], cb_t[:, qg * 512 + c0: qg * 512 + c0 + nq], (b_csT, cb_b))
            for qg in range(4):
                units = self.causal_units(qg, k, q, v, bias_fn, (qb, kb), (vb,))
                self.attn_job(units, self.evac_norm(o, obb, qg * 512), self.ptile, self.tmpt)
            self.dma(self.obT[12 + h], o[:], (obb,), (), ods)
        self.barrier()
        S.sb_reset(mark)

    def mix_d(self):
        S = self.S
        ps = self.ps
        mark = S.sb_mark()
        self.attn_common()
        ident = self.cst[:, CO_ID:CO_ID + 128]
        NLE = self.cst[:, CO_NLE:CO_NLE + 128]
        qs = S.sb([128, 8, L], BF16, "dq")
        kk = S.sb([128, L], BF16, "dk")
        vv = S.sb([128, NT, 128], BF16, "dv")
        iq = S.sb([128, 4, L], BF16, "diq")
        ik = S.sb([128, L], BF16, "dik")
        iw = S.sb([128, NT, 8], F32, "diw")
        b_in = S.buf("din")
        ds = S.dsem()
        for h in range(8):
            self.dma(qs[:, h, :], self.pfm[56 + h], (), (b_in,), ds)
        self.dma(kk[:], self.pfm[64], (), (b_in,), ds)
        self.dma(vv[:], self.ptm[:, 3072:3200].rearrange("(kt p) d -> p kt d", p=128), (), (b_in,), ds)
        for b in range(4):
            self.dma(iq[:, b, :], self.pfm[65 + b], (), (b_in,), ds)
        self.dma(ik[:], self.pfm[69], (), (b_in,), ds)
        self.dma(iw[:], self.iwtm.rearrange("(tt p) h -> p tt h", p=128), (), (b_in,), ds)
        score = Rot(S, 2, [128, L], F32, "score")
        work = S.sb([128, L], F32, "work")
        b_work = S.buf("work")
        m8 = S.sb([128, 8], F32, "m8")
        msk = Rot(S, 2, [128, L], F32, "msk")
        relu = Rot(S, 3, [128, 512], F32, "relu")
        maskT = Rot(S, 2, [128, NT, 512], BF16, "maskT")
        ob = Rot(S, 8, [128, 512], BF16, "odb", dma=True)
        for qg in range(4):
            mT, mTb, _ = maskT.next()
            for qi in range(4):
                qt = qg * 4 + qi
                nk = (qt + 1) * 128
                sc, scb, _ = score.next()
                for kg in range((nk + 511) // 512):
                    n = min(512, nk - kg * 512)
                    for h in range(8):
                        bk = self.bank()
                        pr = 64 * (h % 2)
                        S.add("pe", lambda e, bk=bk, h=h, pr=pr, qt=qt, kg=kg, n=n: e.matmul(
                            ps[:, bk, :n], iq[pr:pr + 64, h // 2, qt * 128:(qt + 1) * 128], ik[pr:pr + 64, kg * 512:kg * 512 + n], start=True, stop=True),
                            reads=(b_in,), writes=(self.psb[bk],))
                        r, rb, _ = relu.next()
                        S.add("act", lambda e, r=r, bk=bk, n=n: e.activation(out=r[:, :n], in_=ps[:, bk, :n], func=AF.Relu), reads=(self.psb[bk],), writes=(rb,))
                        sl = slice(kg * 512, kg * 512 + n)
                        if h == 0:
                            S.add("dve", lambda e, sc=sc, r=r, n=n, sl=sl, qt=qt: e.tensor_scalar(out=sc[:, sl], in0=r[:, :n], scalar1=iw[:, qt, 0:1], scalar2=None, op0=ALU.mult),
                                  reads=(rb, b_in), writes=(scb,))
                        else:
                            S.add("dve", lambda e, sc=sc, r=r, n=n, sl=sl, qt=qt, h=h: e.scalar_tensor_tensor(out=sc[:, sl], in0=r[:, :n], scalar=iw[:, qt, h:h + 1], in1=sc[:, sl],
                                                                                                        op0=ALU.mult, op1=ALU.add),
                                  reads=(rb, b_in, scb), writes=(scb,))
                dsl = slice(qt * 128, (qt + 1) * 128)
                S.add("dve", lambda e, sc=sc, dsl=dsl: e.tensor_tensor(out=sc[:, dsl], in0=sc[:, dsl], in1=NLE, op=ALU.add), reads=(scb, self.b_cst), writes=(scb,))
                mk, mkb, _ = msk.next()
                if qt < 2:
                    S.add("dve", lambda e, mk=mk, sc=sc, nk=nk: e.tensor_scalar(out=mk[:, :nk], in0=sc[:, :nk], scalar1=-1e29, scalar2=None, op0=ALU.is_ge),
                          reads=(scb,), writes=(mkb,))
                else:
                    S.add("dve", lambda e, sc=sc, nk=nk: e.tensor_copy(out=work[:, :nk], in_=sc[:, :nk]), reads=(scb,), writes=(b_work,))
                    for rnd in range(32):
                        S.add("dve", lambda e, nk=nk: e.max(out=m8[:], in_=work[:, :nk]), reads=(b_work,), writes=(b_work,))
                        if rnd < 31:
                            S.add("dve", lambda e, nk=nk: e.match_replace(out=work[:, :nk], in_to_replace=m8[:], in_values=work[:, :nk], imm_value=-3e38),
                                  reads=(b_work,), writes=(b_work,))
                    S.add("dve", lambda e, mk=mk, sc=sc, nk=nk: e.tensor_scalar(out=mk[:, :nk], in0=sc[:, :nk], scalar1=m8[:, 7:8], scalar2=None, op0=ALU.is_ge),
                          reads=(scb, b_work), writes=(mkb,))
                for k4 in range((qt + 4) // 4):
                    nkt = min(4, qt + 1 - k4 * 4)
                    bk = self.bank()

                    def tr(e, mk=mk, bk=bk, k4=k4, nkt=nkt):
                        for j in range(nkt):
                            kt = k4 * 4 + j
                            ins = e.transpose(ps[:, bk, j * 128:(j + 1) * 128], mk[:, kt * 128:(kt + 1) * 128], ident)
                        return ins
                    S.add("pe", tr, reads=(mkb, self.b_cst), writes=(self.psb[bk],))
                    S.add("act", lambda e, mT=mT, bk=bk, k4=k4, nkt=nkt, qi=qi: e.activation(
                        out=mT[:, k4 * 4:k4 * 4 + nkt, qi * 128:(qi + 1) * 128], in_=ps[:, bk, :nkt * 128].rearrange("p (j q) -> p j q", j=nkt), func=AF.Copy),
                        reads=(self.psb[bk],), writes=(mTb,))
            for h in range(8):
                slope = SLOPES[D_SLOPE_IDX[h]]
                o, obb, ods = ob.next()
                units = self.causal_units(qg, kk, qs[:, h, :], vv, self.alibi_bias(slope), (b_in,), (b_in,),
                                          full_fn=lambda kt, c0, nq, mT=mT, mTb=mTb: (mT[:, kt, c0:c0 + nq], mTb))
                self.attn_job(units, self.evac_norm(o, obb, 0), self.ptile, self.tmpt)
                self.dma(self.obT[20 + h, :, qg * 512:(qg + 1) * 512], o[:], (obb,), (), ods)
        self.barrier()
        S.sb_reset(mark)

    def mix_b(self, l):
        S = self.S
        ps = self.ps
        mark = S.sb_mark()
        self.attn_common()
        ident = self.cst[:, CO_ID:CO_ID + 128]
        kcT = S.sb([128, 2, 128], BF16, "kcT")
        vc = S.sb([128, 2, 128], BF16, "vc")
        b_kc, b_vc = S.buf("kc"), S.buf("vc")
        m2 = S.sb_mark()
        w1s = Rot(S, 2, [128, 32, 512], BF16, "w1s", sw=True)
        w2s = Rot(S, 2, [128, 4, 128], BF16, "w2s", sw=True)
        posT = Rot(S, 2, [128, 32], F32, "posT")
        posraw = Rot(S, 2, [32, 128], F32, "posraw", dma=True)
        src_t = Rot(S, 2, [128, L], BF16, "cmpsrc", dma=True)
        kvp = Rot(S, 2, [128, 32, 128], BF16, "kvp")
        hid = Rot(S, 2, [128, 4, 128], BF16, "hid")
        gt = Rot(S, 4, [128, 128], F32, "gelu")
        for j in range(2):
            w1, w1b, w1d = w1s.next()
            for q4 in range(4):
                self.dma(w1[:, q4 * 8:(q4 + 1) * 8, :], self.w["cmp_w1"][l * 2 + j, q4 * 1024:(q4 + 1) * 1024, :].rearrange("(p d) h -> d p h", d=128),
                         (), (w1b,), w1d, q="pool")
            w2, w2b, w2d = w2s.next()
            self.dma(w2[:], self.w["cmp_w2"][l * 2 + j].rearrange("(hc p) d -> p hc d", p=128), (), (w2b,), w2d, q="pool")
            pT, pTb, pTd = posT.next()
            pr_t, pr_b, pr_d = posraw.next()
            self.dma(pr_t[:], self.w["cmp_pos"][l * 2 + j], (), (pr_b,), pr_d)
            self.tr_small(pT[:], pr_t[:], 32, (pr_b,), pTb)
            for g in range(2):
                sr, srb, srd = src_t.next()
                self.dma(sr[:], self.pfm[32 + j * 2 + g], (), (srb,), srd)
                kv, kvb, _ = kvp.next()
                for p in range(32):
                    S.add("dve" if p % 2 == 0 else "pool",
                          lambda e, kv=kv, sr=sr, pT=pT, p=p: e.tensor_scalar(out=kv[:, p, 0:127], in0=sr[:, p:p + 16 * 126 + 1:16], scalar1=pT[:, p:p + 1], scalar2=None, op0=ALU.add),
                          reads=(srb, pTb), writes=(kvb,))
                hd, hdb, _ = hid.next()
                for hc in range(4):
                    bk = self.bank()

                    def mm(e, w1=w1, kv=kv, bk=bk, hc=hc):
                        for p in range(32):
                            ins = e.matmul(ps[:, bk, :127], w1[:, p, hc * 128:(hc + 1) * 128], kv[:, p, 0:127], start=(p == 0), stop=(p == 31))
                        return ins
                    S.add("pe", mm, reads=(w1b, kvb), writes=(self.psb[bk],))
                    u, ub, _ = gt.next()
                    S.add("act", lambda e, u=u, bk=bk: e.activation(out=u[:, :127], in_=ps[:, bk, :127], func=AF.Square), reads=(self.psb[bk],), writes=(ub,))
                    S.add("dve", lambda e, u=u: e.tensor_scalar(out=u[:, :127], in0=u[:, :127], scalar1=0.044715, scalar2=1.0, op0=ALU.mult, op1=ALU.add), reads=(ub,), writes=(ub,))
                    S.add("dve", lambda e, u=u, bk=bk: e.tensor_tensor(out=u[:, :127], in0=u[:, :127], in1=ps[:, bk, :127], op=ALU.mult), reads=(ub, self.psb[bk]), writes=(ub,))
                    S.add("act", lambda e, u=u: e.activation(out=u[:, :127], in_=u[:, :127], func=AF.Sigmoid, scale=1.5957691216057308), reads=(ub,), writes=(ub,))
                    S.add("dve", lambda e, u=u, hd=hd, hc=hc, bk=bk: e.tensor_tensor(out=hd[:, hc, :127], in0=u[:, :127], in1=ps[:, bk, :127], op=ALU.mult),
                          reads=(ub, self.psb[bk]), writes=(hdb,))
                bk = self.bank()
                if j == 0:
                    def mm2(e, w2=w2, hd=hd, bk=bk):
                        for hc in range(4):
                            ins = e.matmul(ps[:, bk, :127], w2[:, hc, :], hd[:, hc, :127], start=(hc == 0), stop=(hc == 3))
                        return ins
                    S.add("pe", mm2, reads=(w2b, hdb), writes=(self.psb[bk],))
                    S.add("dve", lambda e, g=g, bk=bk: e.tensor_copy(out=kcT[:, g, :127], in_=ps[:, bk, :127]), reads=(self.psb[bk],), writes=(b_kc,))
                else:
                    def mm2(e, w2=w2, hd=hd, bk=bk):
                        for hc in range(4):
                            ins = e.matmul(ps[:127, bk, :128], hd[:, hc, :127], w2[:, hc, :], start=(hc == 0), stop=(hc == 3))
                        return ins
                    S.add("pe", mm2, reads=(w2b, hdb), writes=(self.psb[bk],))
                    S.add("dve", lambda e, g=g, bk=bk: e.tensor_copy(out=vc[:127, g, :], in_=ps[:127, bk, :128]), reads=(self.psb[bk],), writes=(b_vc,))
        self.barrier()
        S.all_bufs.extend([b_kc, b_vc])
        S.sb_reset(m2)
        qall = S.sb([128, 8, L], BF16, "bq")
        b_q = S.buf("bq")
        dq = S.dsem()
        for hb in range(8):
            self.dma(qall[:, hb, :], self.pfm[24 + hb], (), (b_q,), dq)
        selc = S.sb([128, 16, 96], F32, "selc")
        b_selc = S.buf("selc")
        self.dma(selc[:], self.selc_d.rearrange("p (a b) -> p a b", b=96), (), (b_selc,), dq)
        selE = S.sb([32, L], BF16, "selE")
        b_selE = S.buf("selE")
        self.dma(selE[:], self.sel_d, (), (b_selE,), S.dsem(True), q="pool")
        ocmp = S.sb([128, 8, L], BF16, "ocmp")
        b_oc = [S.buf("oc") for _ in range(8)]
        bmT = S.sb([32, 2, L], BF16, "bmT")
        b_bm = [S.buf("bm0"), S.buf("bm1")]
        dcl = Rot(S, 2, [128, 128], F32, "dcl")
        ngm = Rot(S, 2, [128, 128], F32, "ngm")
        et = Rot(S, 3, [128, 128], F32, "et")
        t1 = Rot(S, 3, [128, 128], F32, "t1")
        den = Rot(S, 4, [128, 2], F32, "den")
        imp = Rot(S, 2, [128, 132], F32, "imp")
        psl = Rot(S, 2, [128, 64], F32, "psl")
        m8 = Rot(S, 2, [128, 8], F32, "m8b")
        pT = Rot(S, 3, [128, 128], BF16, "pT")
        DC = self.cst[:, CO_DC:CO_DC + 128]
        for qt in range(NT):
            dc, dcb, _ = dcl.next()
            ng, ngb, _ = ngm.next()
            S.add("dve", lambda e, dc=dc, qt=qt: e.tensor_scalar(out=dc[:], in0=DC, scalar1=float(128 * qt), scalar2=0.0, op0=ALU.add, op1=ALU.max),
                  reads=(self.b_cst,), writes=(dcb,))
            S.add("dve", lambda e, ng=ng, qt=qt: e.tensor_scalar(out=ng[:], in0=DC, scalar1=float(-128 * qt), scalar2=NEGBIG, op0=ALU.is_lt, op1=ALU.mult),
                  reads=(self.b_cst,), writes=(ngb,))
            qsl = slice(qt * 128, (qt + 1) * 128)
            for g in range(2):
                im, imb, _ = imp.next()
                S.add("pool", lambda e, im=im: e.memset(im[:], 0.0), reads=(), writes=(imb,))
                bo = self.bank()
                for r in range(4):
                    hb = g * 4 + r
                    slope = SLOPES[B_SLOPE_IDX[hb]]
                    bk = self.bank()
                    while bk == bo:
                        bk = self.bank()
                    S.add("pe", lambda e, bk=bk, hb=hb, g=g, qsl=qsl: e.matmul(ps[:, bk, :127], qall[:, hb, qsl], kcT[:, g, :127], start=True, stop=True),
                          reads=(b_q, b_kc), writes=(self.psb[bk],))
                    tt, ttb, _ = t1.next()
                    S.add("dve", lambda e, tt=tt, dc=dc, ng=ng, slope=slope: e.scalar_tensor_tensor(out=tt[:, :127], in0=dc[:, :127], scalar=-float(slope), in1=ng[:, :127],
                                                                                                 op0=ALU.mult, op1=ALU.add),
                          reads=(dcb, ngb), writes=(ttb,))
                    S.add("dve", lambda e, tt=tt, bk=bk: e.tensor_tensor(out=tt[:, :127], in0=tt[:, :127], in1=ps[:, bk, :127], op=ALU.add),
                          reads=(ttb, self.psb[bk]), writes=(ttb,))
                    e_t, eb, _ = et.next()
                    dn, dnb, _ = den.next()
                    S.add("pool", lambda e, e_t=e_t: e.memset(e_t[:, 127:128], 0.0), reads=(), writes=(eb,))
                    S.add("pool", lambda e, dn=dn: e.memset(dn[:], 0.0), reads=(), writes=(dnb,))
                    S.add("act", lambda e, e_t=e_t, tt=tt, dn=dn: e.activation(out=e_t[:, :127], in_=tt[:, :127], func=AF.Exp, accum_out=dn[:, 0:1]),
                          reads=(ttb,), writes=(eb, dnb))
                    S.add("dve", lambda e, dn=dn: e.tensor_scalar(out=dn[:, 1:2], in0=dn[:, 0:1], scalar1=1e-30, scalar2=None, op0=ALU.max), reads=(dnb,), writes=(dnb,))
                    S.add("dve", lambda e, dn=dn: e.reciprocal(out=dn[:, 1:2], in_=dn[:, 1:2]), reads=(dnb,), writes=(dnb,))
                    S.add("dve", lambda e, e_t=e_t, dn=dn: e.tensor_scalar(out=e_t[:], in0=e_t[:], scalar1=dn[:, 1:2], scalar2=None, op0=ALU.mult), reads=(eb, dnb), writes=(eb,))
                    S.add("pool", lambda e, im=im, e_t=e_t: e.tensor_tensor(out=im[:, 1:128], in0=im[:, 1:128], in1=e_t[:, 0:127], op=ALU.add), reads=(eb, imb), writes=(imb,))
                    bt = self.bank()
                    while bt == bo:
                        bt = self.bank()
                    S.add("pe", lambda e, bt=bt, e_t=e_t: e.transpose(ps[:, bt, :128], e_t[:], ident), reads=(eb, self.b_cst), writes=(self.psb[bt],))
                    pt_, ptb, _ = pT.next()
                    S.add("act", lambda e, pt_=pt_, bt=bt: e.activation(out=pt_[:], in_=ps[:, bt, :128], func=AF.Copy), reads=(self.psb[bt],), writes=(ptb,))
                    S.add("pe", lambda e, bo=bo, r=r, g=g, pt_=pt_: e.matmul(ps[:, bo, r * 128:(r + 1) * 128], vc[:127, g, :], pt_[:127, :], start=(r == 0), stop=False, skip_group_check=True),
                          reads=(ptb, b_vc), writes=(self.psb[bo],))
                for r in range(4):
                    hb = g * 4 + r
                    S.add("act", lambda e, hb=hb, r=r, bo=bo, qsl=qsl: e.activation(out=ocmp[:, hb, qsl], in_=ps[:, bo, r * 128:(r + 1) * 128], func=AF.Copy),
                          reads=(self.psb[bo],), writes=(b_oc[hb],))
                pl, plb, _ = psl.next()
                S.add("dve", lambda e, pl=pl, im=im: e.tensor_scalar(out=pl[:, 0:32], in0=im[:, 0:128:4], scalar1=1.0, scalar2=None, op0=ALU.mult), reads=(imb,), writes=(plb,))
                for o, wgt in ((1, 2.0), (2, 2.0), (3, 2.0), (4, 1.0)):
                    S.add("dve", lambda e, pl=pl, im=im, o=o, wgt=wgt: e.scalar_tensor_tensor(out=pl[:, 0:32], in0=im[:, o:o + 128:4], scalar=wgt, in1=pl[:, 0:32], op0=ALU.mult, op1=ALU.add),
                          reads=(imb, plb), writes=(plb,))
                S.add("dve", lambda e, pl=pl, qt=qt: e.tensor_tensor(out=pl[:, 0:32], in0=pl[:, 0:32], in1=selc[:, qt, 0:32], op=ALU.mult), reads=(plb, b_selc), writes=(plb,))
                S.add("dve", lambda e, pl=pl, qt=qt: e.tensor_tensor(out=pl[:, 0:32], in0=pl[:, 0:32], in1=selc[:, qt, 32:64], op=ALU.add), reads=(plb, b_selc), writes=(plb,))
                S.add("dve", lambda e, pl=pl, qt=qt: e.tensor_tensor(out=pl[:, 0:32], in0=pl[:, 0:32], in1=selc[:, qt, 64:96], op=ALU.max), reads=(plb, b_selc), writes=(plb,))
                m, mb, _ = m8.next()
                S.add("dve", lambda e, m=m, pl=pl: e.max(out=m[:], in_=pl[:, 0:32]), reads=(plb,), writes=(mb,))
                S.add("dve", lambda e, m=m, pl=pl: e.tensor_scalar(out=pl[:, 32:64], in0=pl[:, 0:32], scalar1=m[:, 7:8], scalar2=None, op0=ALU.is_ge), reads=(plb, mb), writes=(plb,))
                bt = self.bank()
                S.add("pe", lambda e, bt=bt, pl=pl: e.transpose(ps[:32, bt, :128], pl[:, 32:64], ident), reads=(plb, self.b_cst), writes=(self.psb[bt],))
                S.add("act", lambda e, bt=bt, g=g, qsl=qsl: e.activation(out=bmT[:, g, qsl], in_=ps[:32, bt, :128], func=AF.Copy), reads=(self.psb[bt],), writes=(b_bm[g],))
        rot = {"k": Rot(S, 2, [128, L], BF16, "bk", dma=True), "v": Rot(S, 2, [128, NT, 128], BF16, "bv", dma=True)}
        mTr = Rot(S, 2, [128, NT, 512], BF16, "bmaskT")
        osl = Rot(S, 2, [128, 512], F32, "osl")
        owi = Rot(S, 2, [128, 512], F32, "owi")
        gbc = Rot(S, 2, [128, 3, 512], F32, "gbc", dma=True)
        ob = Rot(S, 3, [128, 512], BF16, "obb", dma=True)
        for g in range(2):
            ks, ksb, ksd = rot["k"].next()
            vs, vsb, vsd = rot["v"].next()
            kw, kwb, kwd = rot["k"].next()
            vw, vwb, vwd = rot["v"].next()
            self.dma(ks[:], self.pfm[36 + g], (), (ksb,), ksd)
            self.dma(kw[:], self.pfm[38 + g], (), (kwb,), kwd)
            self.dma(vs[:], self.ptm[:, 1536 + g * 128:1536 + (g + 1) * 128].rearrange("(kt p) d -> p kt d", p=128), (), (vsb,), vsd)
            self.dma(vw[:], self.ptm[:, 1792 + g * 128:1792 + (g + 1) * 128].rearrange("(kt p) d -> p kt d", p=128), (), (vwb,), vwd)
            for qg in range(4):
                mT, mTb, _ = mTr.next()
                for kt in range(4 * qg + 4):
                    bk = self.bank()
                    S.add("pe", lambda e, bk=bk, kt=kt, g=g, qg=qg: e.matmul(ps[:, bk, :], selE[:, kt * 128:(kt + 1) * 128], bmT[:, g, qg * 512:(qg + 1) * 512], start=True, stop=True),
                          reads=(b_selE, b_bm[g]), writes=(self.psb[bk],))
                    if kt % 2 == 0:
                        S.add("act", lambda e, mT=mT, bk=bk, kt=kt: e.activation(out=mT[:, kt, :], in_=ps[:, bk, :], func=AF.Copy), reads=(self.psb[bk],), writes=(mTb,))
                    else:
                        S.add("dve", lambda e, mT=mT, bk=bk, kt=kt: e.tensor_copy(out=mT[:, kt, :], in_=ps[:, bk, :]), reads=(self.psb[bk],), writes=(mTb,))
                for r in range(4):
                    hb = g * 4 + r
                    slope = SLOPES[B_SLOPE_IDX[hb]]
                    q = qall[:, hb, :]
                    o_s, osb, _ = osl.next()
                    o_w, owb, _ = owi.next()
                    units = self.causal_units(qg, ks, q, vs, self.alibi_bias(slope), (b_q, ksb), (vsb,),
                                              full_fn=lambda kt, c0, nq, mT=mT, mTb=mTb: (mT[:, kt, c0:c0 + nq], mTb))
                    self.attn_job(units, self.evac_norm(o_s, osb, 0), self.ptile, self.tmpt)
                    units = self.causal_units(qg, kw, q, vw, self.alibi_bias(slope), (b_q, kwb), (vwb,), n_prev=4)
                    self.attn_job(units, self.evac_norm(o_w, owb, 0), self.ptile, self.tmpt)
                    gb_t, gbb, gbd = gbc.next()
                    for br in range(3):
                        self.dma(gb_t[:, br, :], self.bgT[br * 8 + hb:br * 8 + hb + 1, qg * 512:(qg + 1) * 512].to_broadcast([128, 512]), (), (gbb,), gbd)
                    o, obb_, ods = ob.next()
                    sl = slice(qg * 512, (qg + 1) * 512)
                    S.add("dve", lambda e, o_s=o_s, gb_t=gb_t: e.tensor_tensor(out=o_s[:], in0=o_s[:], in1=gb_t[:, 1, :], op=ALU.mult), reads=(osb, gbb), writes=(osb,))
                    S.add("pool", lambda e, o_w=o_w, gb_t=gb_t: e.tensor_tensor(out=o_w[:], in0=o_w[:], in1=gb_t[:, 2, :], op=ALU.mult), reads=(owb, gbb), writes=(owb,))
                    S.add("dve", lambda e, o_s=o_s, o_w=o_w: e.tensor_tensor(out=o_s[:], in0=o_s[:], in1=o_w[:], op=ALU.add), reads=(osb, owb), writes=(osb,))
                    S.add("pool", lambda e, o_w=o_w, gb_t=gb_t, hb=hb, sl=sl: e.tensor_tensor(out=o_w[:], in0=ocmp[:, hb, sl], in1=gb_t[:, 0, :], op=ALU.mult),
                          reads=(b_oc[hb], gbb, owb), writes=(owb,))
                    S.add("dve", lambda e, o=o, o_s=o_s, o_w=o_w: e.tensor_tensor(out=o[:], in0=o_s[:], in1=o_w[:], op=ALU.add), reads=(osb, owb), writes=(obb_,))
                    self.dma(self.obT[4 + hb, :, sl], o[:], (obb_,), (), ods)
        self.barrier()
        S.sb_reset(mark)

    def ph_merge(self, l):
        S = self.S
        ps = self.ps
        mark = S.sb_mark()
        wb = self.w["w_branch"][l]
        S.sb_reset(self.mark_noxb)
        osb_t = S.sb([128, 28, L], BF16, "obres")
        b_ob = [S.buf("ob") for _ in range(28)]
        dso = S.dsem()
        for b in range(28):
            self.dma(osb_t[:, b, :], self.obT[b], (), (b_ob[b],), dso)
        wsl = Rot(S, 2, [128, 28, 128], BF16, "wbr", sw=True)
        gt = Rot(S, 2, [128, 4, L], BF16, "gin", dma=True)
        acc = Rot(S, 2, [128, 512], F32, "macc")
        tmp = Rot(S, 2, [128, 512], F32, "mtmp")
        mo = Rot(S, 2, [128, L], BF16, "mout", dma=True)
        br_chunks = ((0, 4), (4, 8), (12, 8), (20, 8))
        for m in range(KC):
            tl, bf, ds = wsl.next()
            self.dma(tl[:], wb[:, m * 128:(m + 1) * 128].rearrange("(kc p) n -> p kc n", p=128), (), (bf,), ds, q="pool")
            g_t, gb, gds = gt.next()
            self.dma(g_t[:], self.gatesT.rearrange("(br f) t -> f br t", br=4)[m * 128:(m + 1) * 128], (), (gb,), gds)
            o, ob, ods = mo.next()
            for t in range(4):
                sl = slice(t * 512, (t + 1) * 512)
                a, ab, _ = acc.next()
                for bi, (c0, nc_) in enumerate(br_chunks):
                    bk = self.bank()

                    def mm(e, tl=tl, bk=bk, c0=c0, nc_=nc_, sl=sl):
                        for k in range(nc_):
                            ins = e.matmul(ps[:, bk, :], tl[:, c0 + k, :], osb_t[:, c0 + k, sl], start=(k == 0), stop=(k == nc_ - 1))
                        return ins
                    S.add("pe", mm, reads=[bf] + b_ob[c0:c0 + nc_], writes=(self.psb[bk],))
                    if bi == 0:
                        S.add("dve", lambda e, a=a, bk=bk, g_t=g_t, sl=sl: e.tensor_tensor(out=a[:], in0=ps[:, bk, :], in1=g_t[:, 0, sl], op=ALU.mult),
                              reads=(self.psb[bk], gb), writes=(ab,))
                    else:
                        tm, tmb, _ = tmp.next()
                        S.add("dve", lambda e, tm=tm, bk=bk, g_t=g_t, sl=sl, bi=bi: e.tensor_tensor(out=tm[:], in0=ps[:, bk, :], in1=g_t[:, bi, sl], op=ALU.mult),
                              reads=(self.psb[bk], gb), writes=(tmb,))
                        if bi < 3:
                            S.add("pool", lambda e, a=a, tm=tm: e.tensor_tensor(out=a[:], in0=a[:], in1=tm[:], op=ALU.add), reads=(ab, tmb), writes=(ab,))
                        else:
                            S.add("pool", lambda e, a=a, tm=tm, o=o, sl=sl: e.tensor_tensor(out=o[:, sl], in0=a[:], in1=tm[:], op=ALU.add), reads=(ab, tmb), writes=(ob,))
            self.dma(self.mergedT[m * 128:(m + 1) * 128, :], o[:], (ob,), (), ods)
        self.barrier()
        S.sb_reset(mark)


def host_selc():
    c = np.zeros((128, 16, 96), np.float32)
    p = np.arange(128)[:, None]
    j = np.arange(32)[None, :]
    for qt in range(16):
        cur = 2 * qt + (p >= 64)
        forced = (j == 0) | (j == cur) | (j == cur - 1)
        valid = (j <= cur)
        c[:, qt, 0:32] = valid
        c[:, qt, 32:64] = (valid - 1.0) * 1e9
        c[:, qt, 64:96] = np.where(forced, 1e9, -2e9)
    return c.reshape(128, 16 * 96)


N_CORES_USED = 8


def make_inputs(prog, inputs, xs):
    m = {}
    for nm in prog.declared:
        if nm == "x":
            m[nm] = np.ascontiguousarray(xs, dtype=np.float32)
        elif nm == "consts":
            m[nm] = host_consts()
        elif nm == "sel":
            m[nm] = host_sel()
        elif nm == "selc":
            m[nm] = host_selc()
        elif nm in ("ln_g", "ln_b"):
            m[nm] = np.ascontiguousarray(inputs[nm], dtype=np.float32).reshape(DEPTH * 3, D)
        elif nm in ("cmp_w1", "cmp_w2", "cmp_pos"):
            a = np.asarray(inputs[nm], dtype=np.float32)
            m[nm] = np.ascontiguousarray(a.reshape((DEPTH * 2,) + a.shape[2:]))
        else:
            m[nm] = np.ascontiguousarray(inputs[nm], dtype=np.float32)
    return m


def kernel(**inputs):
    R = N_CORES_USED
    spc = 8 // R
    prog = Prog(spc)
    x = np.asarray(inputs["x"], dtype=np.float32)
    in_maps = [make_inputs(prog, inputs, x[c * spc:(c + 1) * spc]) for c in range(R)]
    res = run_bass_kernel_spmd(prog.nc, in_maps, core_ids=list(range(R)))
    return np.concatenate([np.asarray(res.results[c]["out"]) for c in range(R)], axis=0).astype(np.float32)
```

```python
import math
from contextlib import ExitStack

import numpy as np
import concourse.bass as bass
import concourse.mybir as mybir
from concourse.bass_utils import run_bass_kernel_spmd

F32 = mybir.dt.float32
BF16 = mybir.dt.bfloat16
AF = mybir.ActivationFunctionType
ALU = mybir.AluOpType
AX = mybir.AxisListType

SB_BASE = 16640
SB_END = 229376

D = 4096
L = 2048
DFF = 8192
DEPTH = 2
KC = D // 128
NT = L // 128
ALPHA = (2 * DEPTH) ** 0.25
SCALE = 128 ** -0.5
EPS = 1e-5
D_IN = 28520
N_ALIBI = 28
SLOPES = [2.0 ** (-8.0 * i / N_ALIBI) for i in range(1, N_ALIBI + 1)]
A_SLOPE_IDX = (0, 1, 2, 3, 12, 13, 14, 15, 24, 25, 26, 27)
B_SLOPE_IDX = (4, 5, 6, 7, 8, 9, 10, 11)
D_SLOPE_IDX = (16, 17, 18, 19, 20, 21, 22, 23)
NEGBIG = -30000.0

C_AQ, C_AK, C_AV = 0, 1536, 3072
C_BQ, C_BKV, C_BG = 4608, 5632, 7168
C_CQ, C_CK, C_CV, C_CF = 7192, 8216, 9240, 10264
C_DQ, C_DK, C_DV, C_DIQ, C_DIK, C_DIW = 10272, 11296, 11424, 11552, 12064, 12128
C_G = 12136


class Buf:
    __slots__ = ("name", "w", "rd")

    def __init__(self, name=""):
        self.name = name
        self.w = None
        self.rd = []


class Op:
    __slots__ = ("eng", "fn", "deps", "dma", "dsem", "signal", "tok", "seq", "phase")
    _n = 0

    def __init__(self, eng, fn, dma, dsem):
        Op._n += 1
        self.seq = Op._n
        self.eng = eng
        self.fn = fn
        self.deps = []
        self.dma = dma
        self.dsem = dsem
        self.signal = dma
        self.tok = None
        self.phase = None


ENGS = ("pe", "act", "dve", "pool", "sp")


class Sched:
    def __init__(self, nc):
        self.nc = nc
        self.ops = {e: [] for e in ENGS}
        self.sb_off = SB_BASE
        self.n_dsem = 0
        self.cur_dsem = 0
        self.n_sw = 0
        self.cur_sw = 0
        self.phase = None
        self.profile_scopes = False
        self.all_bufs = []
        self.nalloc = 0

    def sb(self, shape, dtype, name=None):
        nbytes = 2 if dtype == BF16 else 4
        per_part = nbytes
        for s in shape[1:]:
            per_part *= s
        off = (self.sb_off + 63) // 64 * 64
        assert off + per_part <= SB_END, f"SBUF overflow {name} {off + per_part - SB_END}"
        self.sb_off = off + per_part
        self.nalloc += 1
        return self.nc.alloc_sbuf_tensor_at(f"{name or 't'}{self.nalloc}", list(shape), dtype, offset=off)

    def sb_mark(self):
        return (self.sb_off, self.cur_dsem, self.cur_sw)

    def sb_reset(self, mark):
        self.sb_off, self.cur_dsem, self.cur_sw = mark

    def buf(self, name=""):
        b = Buf(name)
        self.all_bufs.append(b)
        return b

    def dsem(self, sw=False):
        if sw:
            self.cur_sw += 1
            self.n_sw = max(self.n_sw, self.cur_sw)
            return ("s", self.cur_sw - 1)
        self.cur_dsem += 1
        self.n_dsem = max(self.n_dsem, self.cur_dsem)
        return ("h", self.cur_dsem - 1)

    def add(self, eng, fn, reads=(), writes=(), dma=False, dsem=None):
        if dma:
            assert (dsem[0] == "s") == (eng == "pool"), (eng, dsem)
        op = Op(eng, fn, dma, dsem)
        op.phase = self.phase
        deps = op.deps
        for b in reads:
            w = b.w
            if w is not None:
                if dma or w.eng != eng or w.dma or eng != "pe":
                    deps.append(w)
            b.rd.append(op)
        for b in writes:
            w = b.w
            if w is not None and (dma or w.dma or w.eng != eng):
                deps.append(w)
            for r in b.rd:
                if r is not op and (dma or r.dma or r.eng != eng):
                    deps.append(r)
            b.w = op
            b.rd = []
        for d in deps:
            d.signal = True
        self.ops[eng].append(op)
        return op

    def barrier(self):
        pend = set()
        for b in self.all_bufs:
            if b.w is not None:
                pend.add(b.w)
            for r in b.rd:
                pend.add(r)
        pend = list(pend)
        for e in ENGS:
            op = Op(e, None, False, None)
            for d in pend:
                if d.eng != e or d.dma:
                    op.deps.append(d)
                    d.signal = True
            self.ops[e].append(op)
        for b in self.all_bufs:
            b.w = None
            b.rd = []
        if len(self.all_bufs) > 20000:
            self.all_bufs = self.all_bufs[-5000:]

    def emit(self):
        nc = self.nc
        with ExitStack() as ctx:
            esem = {e: ctx.enter_context(nc.semaphore(f"s_{e}")) for e in ENGS}
            dsems = {("h", i): ctx.enter_context(nc.semaphore(f"d_{i}")) for i in range(self.n_dsem)}
            dsems.update({("s", i): ctx.enter_context(nc.semaphore(f"w_{i}")) for i in range(self.n_sw)})
            for e in ENGS:
                cnt = 0
                for op in self.ops[e]:
                    if op.dma or not op.signal or op.fn is None:
                        continue
                    cnt += 1
                    op.tok = (esem[e], cnt)
            dcnt = {k: 0 for k in dsems}
            alld = [op for e in ENGS for op in self.ops[e] if op.dma]
            alld.sort(key=lambda o: o.seq)
            import bisect
            dhist = {k: ([], []) for k in dsems}
            for op in alld:
                dcnt[op.dsem] += 16
                op.tok = (dsems[op.dsem], dcnt[op.dsem])
                dhist[op.dsem][0].append(op.seq)
                dhist[op.dsem][1].append(dcnt[op.dsem])
            block = ctx.enter_context(nc.Block())

            def run(e, eng):
                known = {}
                cur_ph, cur_id = None, None
                for op in self.ops[e]:
                    if self.profile_scopes and e == "pe" and op.fn is not None and getattr(op, "phase", None) != cur_ph:
                        if cur_ph is not None:
                            nc.leave_named_scope(cur_ph, cur_id, False)
                        cur_ph = op.phase
                        cur_id = nc.enter_named_scope(cur_ph, False)[0] if cur_ph is not None else None
                    need = {}
                    for d in op.deps:
                        if d.tok is None:
                            continue
                        s, v = d.tok
                        if d.dma:
                            seqs, vals = dhist[d.dsem]
                            v = vals[bisect.bisect_left(seqs, op.seq) - 1]
                        k = id(s)
                        if known.get(k, 0) < v and (k not in need or need[k][1] < v):
                            need[k] = (s, v)
                    for k, (s, v) in need.items():
                        eng.wait_ge(s, v)
                        known[k] = v
                    if op.fn is None:
                        continue
                    ins = op.fn(eng)
                    if op.dma:
                        ins.then_inc(op.tok[0], 16)
                    elif op.signal:
                        ins.then_inc(op.tok[0], 1)
                if cur_ph is not None:
                    nc.leave_named_scope(cur_ph, cur_id, False)

            @block.tensor
            def _(eng):
                run("pe", eng)

            @block.scalar
            def _(eng):
                run("act", eng)

            @block.vector
            def _(eng):
                run("dve", eng)

            @block.gpsimd
            def _(eng):
                run("pool", eng)

            @block.sync
            def _(eng):
                run("sp", eng)


class Rot:
    def __init__(self, S, n, shape, dtype, name, dma=False, sw=False):
        self.slots = []
        for i in range(n):
            t = S.sb(shape, dtype, name)
            self.slots.append((t, S.buf(name), S.dsem(sw) if (dma or sw) else None))
        self.i = 0

    def next(self):
        s = self.slots[self.i % len(self.slots)]
        self.i += 1
        return s


CO_ID = 0
CO_D0 = 128
CO_CM = 640
CO_LE = 768
CO_LT = 896
CO_ONE = 1024
CO_DC = 1152
CO_D0R = 1280
CO_CMR = 1792
CO_NLE = 2304
CO_D0P = 2432
CO_N = 2944


def host_consts():
    c = np.zeros((128, CO_N), np.float32)
    s = np.arange(128)[:, None]
    c[:, CO_ID:CO_ID + 128] = np.eye(128)
    c[:, CO_D0:CO_D0 + 512] = np.arange(512)[None, :] - s
    t = np.arange(128)[None, :]
    c[:, CO_CM:CO_CM + 128] = (t >= s)
    c[:, CO_LE:CO_LE + 128] = (t <= s)
    c[:, CO_LT:CO_LT + 128] = (t < s)
    c[:, CO_ONE:CO_ONE + 128] = 1.0
    c[:, CO_DC:CO_DC + 128] = s - 16 * t - 31
    c[:, CO_D0P:CO_D0P + 512] = np.maximum(np.arange(512)[None, :] - s, 0)
    for i in range(4):
        c[:, CO_D0R + i * 128:CO_D0R + (i + 1) * 128] = np.maximum(t - s, 0)
        c[:, CO_CMR + i * 128:CO_CMR + (i + 1) * 128] = (t >= s)
    c[:, CO_NLE:CO_NLE + 128] = ((t <= s) - 1.0) * 1e30
    return c


def host_sel():
    e = np.zeros((32, L), np.float32)
    for j in range(32):
        e[j, j * 64:(j + 1) * 64] = 1.0
    return e


class Prog:
    def __init__(self, spc, dbg=None, stages=None):
        self.spc = spc
        self.dbg = dbg or ()
        self.stages = stages
        nc = bass.Bass("TRN2", target_bir_lowering=False)
        self.nc = nc
        S = Sched(nc)
        self.S = S

        self.declared = []

        def din(name, shape):
            self.declared.append(name)
            return nc.dram_tensor(name, list(shape), F32, kind="ExternalInput").ap()

        self.x = din("x", [spc, L, D])
        self.ln_g = din("ln_g", [DEPTH * 3, D])
        self.ln_b = din("ln_b", [DEPTH * 3, D])
        wshapes = dict((("ffn1_w_gate", [DEPTH, D, DFF]), ("ffn1_w_up", [DEPTH, D, DFF]), ("ffn1_w_down", [DEPTH, DFF, D]),
                        ("w_in", [DEPTH, D, D_IN]), ("b_forget", [DEPTH, 8]), ("b_gate", [DEPTH, 4 * D]),
                        ("cmp_w1", [DEPTH * 2, 4096, 512]), ("cmp_w2", [DEPTH * 2, 512, 128]), ("cmp_pos", [DEPTH * 2, 32, 128]),
                        ("w_branch", [DEPTH, 3584, D]), ("w_out", [DEPTH, D, D]),
                        ("ffn2_w_gate", [DEPTH, D, DFF]), ("ffn2_w_up", [DEPTH, D, DFF]), ("ffn2_w_down", [DEPTH, DFF, D])))

        class LazyW(dict):
            def __missing__(s2, nm):
                s2[nm] = din(nm, wshapes[nm])
                return s2[nm]
        self.w = LazyW()
        self.consts_d = din("consts", [128, CO_N])
        self.sel_d = din("sel", [32, L])
        self.selc_d = din("selc", [128, 16 * 96])
        self.out = nc.dram_tensor("out", [spc, L, D], F32, kind="ExternalOutput").ap()

        def scr(name, shape, dt):
            kind = "ExternalOutput" if name in self.dbg else "Internal"
            return nc.dram_tensor(name, list(shape), dt, kind=kind).ap()

        self.xaT = scr("xaT", [D, L], F32)
        self.zT = scr("zT", [D, L], F32)
        self.hT = scr("hT", [DFF, L], BF16)
        self.pfm = scr("pfm", [70, 128, L], BF16)
        self.bgT = scr("bgT", [24, L], F32)
        self.cfT = scr("cfT", [8, L], F32)
        self.iwtm = scr("iwtm", [L, 8], F32)
        self.ptm = scr("ptm", [L, 3200], BF16)
        self.gatesT = scr("gatesT", [4 * D, L], BF16)
        self.obT = scr("obT", [28, 128, L], BF16)
        self.mergedT = scr("mergedT", [D, L], BF16)
        self.xbT_d = nc.dram_tensor("xbT_d", [D, L], BF16, kind=("ExternalOutput" if "xb" in self.dbg else "Internal")).ap()
        self.csd = scr("csd", [8, L], F32)
        self.bmd = scr("bmd", [2, 32, L], BF16)

        self.ps = nc.alloc_psum_tensor("ps", [128, 8, 512], F32)
        self.psb = [S.buf(f"ps{i}") for i in range(8)]
        self.psi = 0
        import os
        S.profile_scopes = bool(os.environ.get("SCOPES"))
        self.build()
        S.barrier()
        S.emit()

    def bank(self):
        i = self.psi % 8
        self.psi += 1
        return i

    def dma(self, out, in_, reads, writes, dsem, q="sp"):
        self.S.add(q, lambda e: e.dma_start(out=out, in_=in_), reads=reads, writes=writes, dma=True, dsem=dsem)

    def tr_small(self, dst, src, n, reads, wbuf):
        S = self.S
        ps = self.ps
        bk = self.bank()
        ident = self.cst[:n, CO_ID:CO_ID + n]
        S.add("pe", lambda e: e.transpose(ps[:, bk, :n], src, ident), reads=list(reads) + [self.b_cst], writes=(self.psb[bk],))
        S.add("act", lambda e: e.activation(out=dst, in_=ps[:, bk, :n], func=AF.Copy), reads=(self.psb[bk],), writes=(wbuf,))

    def load_colvec(self, dst, src1d, n, wbuf, tmp, tmpb, ds):
        self.dma(tmp[:n, :], src1d.rearrange("(c p) -> c p", p=128), (), (tmpb,), ds)
        self.tr_small(dst, tmp[:n, :], n, (tmpb,), wbuf)

    def reset_bufs(self):
        S = self.S
        for b in self.persist_bufs + self.psb:
            if b not in S.all_bufs[:64]:
                S.all_bufs.insert(0, b)

    def barrier(self):
        self.S.barrier()
        self.reset_bufs()

    def build(self):
        S = self.S
        self.cst = S.sb([128, CO_N], F32, "cst")
        self.cstb = S.sb([128, CO_N], BF16, "cstb")
        self.mark_noxb = S.sb_mark()
        self.xb = S.sb([128, KC, L], BF16, "xb")
        self.b_cst = S.buf("cst")
        self.b_xb = [S.buf(f"xb{c}") for c in range(KC)]
        self.persist_bufs = [self.b_cst] + self.b_xb
        ds = S.dsem()
        self.dma(self.cst[:], self.consts_d, (), (self.b_cst,), ds)
        S.add("dve", lambda e: e.tensor_copy(out=self.cstb[:], in_=self.cst[:]), reads=(self.b_cst,), writes=(self.b_cst,))
        self.base_mark = S.sb_mark()
        st = self.stages or {}
        layers = st.get("layers", list(range(DEPTH)))
        parts = st.get("parts", ("ffn1", "mixer", "ffn2"))
        for s in range(self.spc):
            if not st.get("skip_input"):
                self.ph_input(s, st.get("in_scale", ALPHA))
            for l in layers:
                if "ffn1" in parts:
                    self.ffn(l, 1)
                if "mixer" in parts:
                    self.mixer(l)
                    if "stop" in st:
                        break
                if "ffn2" in parts:
                    self.ffn(l, 2, final=(l == DEPTH - 1))
            if "xb" in self.dbg:
                dsx = S.dsem()
                for c in range(KC):
                    self.dma(self.xbT_d[c * 128:(c + 1) * 128, :], self.xb[:, c, :], (self.b_xb[c],), (), dsx)
            if not st:
                self.ph_output(s)

    def ph_input(self, s, in_scale=ALPHA):
        self.S.phase = "input"
        S = self.S
        nc = self.nc
        mark = S.sb_mark()
        xt = Rot(S, 8, [128, 1024], F32, "xin", dma=True)
        st_a = Rot(S, 3, [128, 512], F32, "xast", dma=True)
        ident = self.cst[:, CO_ID:CO_ID + 128]
        ps = self.ps
        for tg in range(4):
            for fq in range(4):
                tiles = []
                for j in range(4):
                    tl, bf, ds = xt.next()
                    tt = tg * 4 + j
                    self.dma(tl[:], self.x[s, tt * 128:(tt + 1) * 128, fq * 1024:(fq + 1) * 1024], (), (bf,), ds)
                    tiles.append((tl, bf))
                for ci in range(8):
                    c = fq * 8 + ci
                    bk = self.bank()

                    def tr(e, tiles=tiles, ci=ci, bk=bk):
                        for j, (tl, bf) in enumerate(tiles):
                            ins = e.transpose(ps[:, bk, j * 128:(j + 1) * 128], tl[:, ci * 128:(ci + 1) * 128], ident)
                        return ins
                    S.add("pe", tr, reads=[b for _, b in tiles] + [self.b_cst], writes=(self.psb[bk],))
                    sa, sab, sads = st_a.next()
                    S.add("act", lambda e, sa=sa, bk=bk: e.activation(out=sa[:], in_=ps[:, bk, :], func=AF.Copy, scale=float(in_scale)),
                          reads=(self.psb[bk],), writes=(sab,))
                    self.dma(self.xaT[c * 128:(c + 1) * 128, tg * 512:(tg + 1) * 512], sa[:], (sab,), (), sads)
                    S.add("dve", lambda e, c=c, tg=tg, sa=sa: e.tensor_scalar(out=self.xb[:, c, tg * 512:(tg + 1) * 512], in0=sa[:], scalar1=1.0 / float(in_scale),
                                                                         scalar2=None, op0=ALU.mult),
                          reads=(sab,), writes=(self.b_xb[c],))
        self.barrier()
        S.sb_reset(mark)

    def ph_output(self, s):
        self.S.phase = "output"
        S = self.S
        mark = S.sb_mark()
        zi = Rot(S, 6, [128, L], F32, "oin", dma=True)
        so = Rot(S, 3, [128, 512], F32, "oout", dma=True)
        ident = self.cst[:, CO_ID:CO_ID + 128]
        ps = self.ps
        for cg in range(8):
            tiles = []
            for j in range(4):
                c = cg * 4 + j
                tl, bf, ds = zi.next()
                self.dma(tl[:], self.zT[c * 128:(c + 1) * 128, :], (), (bf,), ds)
                tiles.append((tl, bf))
            for tt in range(NT):
                bk = self.bank()

                def tr(e, tiles=tiles, tt=tt, bk=bk):
                    for j, (tl, bf) in enumerate(tiles):
                        ins = e.transpose(ps[:, bk, j * 128:(j + 1) * 128], tl[:, tt * 128:(tt + 1) * 128], ident)
                    return ins
                S.add("pe", tr, reads=[b for _, b in tiles] + [self.b_cst], writes=(self.psb[bk],))
                so_t, sob, sods = so.next()
                if tt % 2 == 0:
                    S.add("dve", lambda e, so_t=so_t, bk=bk: e.tensor_copy(out=so_t[:], in_=ps[:, bk, :]), reads=(self.psb[bk],), writes=(sob,))
                else:
                    S.add("act", lambda e, so_t=so_t, bk=bk: e.activation(out=so_t[:], in_=ps[:, bk, :], func=AF.Copy), reads=(self.psb[bk],), writes=(sob,))
                self.dma(self.out[s, tt * 128:(tt + 1) * 128, cg * 512:(cg + 1) * 512], so_t[:], (sob,), (), sods)
        self.barrier()
        S.sb_reset(mark)

    def ffn(self, l, which, final=False, seq=0):
        wg = self.w[f"ffn{which}_w_gate"][l]
        wu = self.w[f"ffn{which}_w_up"][l]
        wd = self.w[f"ffn{which}_w_down"][l]
        fp = (self.stages or {}).get("ffn_parts", ("up", "down", "ln"))
        if "up" in fp:
            self.ph_up(wg, wu)
        if "down" in fp:
            self.ph_down(self.hT, wd, DFF // 128, 0.5)
        if "ln" in fp:
            self.ph_ln(l * 3 + (0 if which == 1 else 2), final)

    def ph_up(self, wg, wu):
        self.S.phase = "up"
        S = self.S
        ps = self.ps
        mark = S.sb_mark()
        CG = 128
        wsl = {"g": Rot(S, 3, [128, KC, CG], BF16, "wg", sw=True), "u": Rot(S, 3, [128, KC, CG], BF16, "wu", sw=True)}
        sg = Rot(S, 2, [128, 512], F32, "sg")
        hb = Rot(S, 3, [128, 512], BF16, "hb", dma=True)
        xb = self.xb
        import os
        for cg in range(int(os.environ.get('UPN', DFF // CG))):
            cur = {}
            for nm, w in (("g", wg), ("u", wu)):
                tl, bf, ds = wsl[nm].next()
                self.dma(tl[:], w[:, cg * CG:(cg + 1) * CG].rearrange("(kc p) n -> p kc n", p=128), (), (bf,), ds, q="pool")
                cur[nm] = (tl, bf)
            for mi in range(CG // 128):
                m = cg * (CG // 128) + mi
                for th in range(2):
                    banks = [self.bank() for _ in range(4)]
                    for j, nm in enumerate(("g", "u")):
                        tl, bf = cur[nm]
                        for tt in range(2):
                            bk = banks[j * 2 + tt]
                            t = th * 2 + tt

                            def mm(e, tl=tl, bk=bk, t=t, mi=mi):
                                for k in range(KC):
                                    ins = e.matmul(ps[:, bk, :], tl[:, k, mi * 128:(mi + 1) * 128], xb[:, k, t * 512:(t + 1) * 512],
                                                   start=(k == 0), stop=(k == KC - 1))
                                return ins
                            S.add("pe", mm, reads=[bf] + self.b_xb, writes=(self.psb[bk],))
                    for tt in range(2):
                        t = th * 2 + tt
                        sgt, sgb, _ = sg.next()
                        hbt, hbb, hds = hb.next()
                        bg, bu = banks[tt], banks[2 + tt]
                        S.add("act", lambda e, sgt=sgt, bg=bg: e.activation(out=sgt[:], in_=ps[:, bg, :], func=AF.Silu),
                              reads=(self.psb[bg],), writes=(sgb,))
                        S.add("dve", lambda e, hbt=hbt, sgt=sgt, bu=bu: e.tensor_tensor(out=hbt[:], in0=sgt[:], in1=ps[:, bu, :], op=ALU.mult),
                              reads=(sgb, self.psb[bu]), writes=(hbb,))
                        self.dma(self.hT[m * 128:(m + 1) * 128, t * 512:(t + 1) * 512], hbt[:], (hbb,), (), hds)
        self.barrier()
        S.sb_reset(mark)

    def ph_down(self, srcT, wd, kc, mul):
        self.S.phase = "down"
        S = self.S
        ps = self.ps
        mark = S.sb_mark()
        S.sb_reset(self.mark_noxb)
        TH = 1024 if kc > 32 else 2048
        npass = L // TH
        ntt = TH // 512
        hs = S.sb([128, kc, TH], BF16, "hs")
        hsb = [S.buf("hs") for _ in range(kc)]
        hds = S.dsem()
        wsl = Rot(S, 2, [128, kc, 128], BF16, "wd", sw=True)
        xat = Rot(S, 3, [128, 512], F32, "xat", dma=True)
        zst = Rot(S, 3, [128, 512], F32, "zst", dma=True)
        for p in range(npass):
            for k in range(kc):
                self.dma(hs[:, k, :], srcT[k * 128:(k + 1) * 128, p * TH:(p + 1) * TH], (), (hsb[k],), hds)
            for m in range(D // 128):
                tl, bf, ds = wsl.next()
                nsplit = 4
                ks = kc // nsplit
                for q in range(nsplit):
                    self.dma(tl[:, q * ks:(q + 1) * ks, :],
                             wd[q * ks * 128:(q + 1) * ks * 128, m * 128:(m + 1) * 128].rearrange("(kc p) n -> p kc n", p=128),
                             (), (bf,), ds, q="pool")
                for tt in range(ntt):
                    t0 = p * TH + tt * 512
                    bk = self.bank()

                    def mm(e, tl=tl, bk=bk, tt=tt):
                        for k in range(kc):
                            ins = e.matmul(ps[:, bk, :], tl[:, k, :], hs[:, k, tt * 512:(tt + 1) * 512], start=(k == 0), stop=(k == kc - 1))
                        return ins
                    S.add("pe", mm, reads=[bf] + hsb, writes=(self.psb[bk],))
                    xa, xab, xads = xat.next()
                    self.dma(xa[:], self.xaT[m * 128:(m + 1) * 128, t0:t0 + 512], (), (xab,), xads)
                    z, zb, zds = zst.next()
                    S.add("dve", lambda e, z=z, xa=xa, bk=bk: e.scalar_tensor_tensor(out=z[:], in0=ps[:, bk, :], scalar=float(mul), in1=xa[:],
                                                                                  op0=ALU.mult, op1=ALU.add),
                          reads=(self.psb[bk], xab), writes=(zb,))
                    self.dma(self.zT[m * 128:(m + 1) * 128, t0:t0 + 512], z[:], (zb,), (), zds)
        self.barrier()
        S.sb_reset(mark)

    def ph_ln(self, idx, final):
        self.S.phase = "ln"
        S = self.S
        ps = self.ps
        mark = S.sb_mark()
        H = L // 2
        zin = Rot(S, 2, [128, H], F32, "zin", dma=True)
        acc1 = S.sb([128, H], F32, "acc1")
        acc2 = S.sb([128, H], F32, "acc2")
        b_a1, b_a2 = S.buf("a1"), S.buf("a2")
        sq = Rot(S, 2, [128, H], F32, "sq")
        gb = S.sb([128, 4, KC], F32, "gb")
        b_gb = S.buf("gb")
        gds = S.dsem()
        cvt = S.sb([32, 2, 128], F32, "cvtmp")
        b_cvt = S.buf("cvt")
        self.load_colvec(gb[:, 0, :], self.ln_g[idx], KC, b_gb, cvt[:, 0, :], b_cvt, gds)
        self.load_colvec(gb[:, 1, :], self.ln_b[idx], KC, b_gb, cvt[:, 1, :], b_cvt, gds)
        S.add("dve", lambda e: e.tensor_scalar(out=gb[:, 2:4, :], in0=gb[:, 0:2, :], scalar1=float(ALPHA), scalar2=None, op0=ALU.mult),
              reads=(b_gb,), writes=(b_gb,))
        ones = self.cst[:, CO_ONE:CO_ONE + 128]
        mu = S.sb([128, H], F32, "mu")
        rs = S.sb([128, H], F32, "rs")
        b_mu, b_rs = S.buf("mu"), S.buf("rs")
        yt = Rot(S, 2, [128, H], F32, "yt")
        xo = Rot(S, 2, [128, H], F32, "xo", dma=True)
        for hf in range(2):
            hs_ = slice(hf * H, (hf + 1) * H)
            for c in range(KC):
                z, zb, zds = zin.next()
                self.dma(z[:], self.zT[c * 128:(c + 1) * 128, hs_], (), (zb,), zds)
                s_t, s_b, _ = sq.next()
                S.add("act", lambda e, s_t=s_t, z=z: e.activation(out=s_t[:], in_=z[:], func=AF.Square), reads=(zb,), writes=(s_b,))
                if c == 0:
                    S.add("pool", lambda e, z=z: e.tensor_copy(out=acc1[:], in_=z[:]), reads=(zb,), writes=(b_a1,))
                    S.add("dve", lambda e, s_t=s_t: e.tensor_copy(out=acc2[:], in_=s_t[:]), reads=(s_b,), writes=(b_a2,))
                else:
                    S.add("pool", lambda e, z=z: e.tensor_tensor(out=acc1[:], in0=acc1[:], in1=z[:], op=ALU.add), reads=(zb, b_a1), writes=(b_a1,))
                    S.add("dve", lambda e, s_t=s_t: e.tensor_tensor(out=acc2[:], in0=acc2[:], in1=s_t[:], op=ALU.add), reads=(s_b, b_a2), writes=(b_a2,))
            for tt in range(H // 512):
                b1, b2 = self.bank(), self.bank()
                sl = slice(tt * 512, (tt + 1) * 512)
                S.add("pe", lambda e, b1=b1, sl=sl: e.matmul(ps[:, b1, :], ones, acc1[:, sl], start=True, stop=True),
                      reads=(b_a1, self.b_cst), writes=(self.psb[b1],))
                S.add("pe", lambda e, b2=b2, sl=sl: e.matmul(ps[:, b2, :], ones, acc2[:, sl], start=True, stop=True),
                      reads=(b_a2, self.b_cst), writes=(self.psb[b2],))
                S.add("act", lambda e, b1=b1, sl=sl: e.activation(out=mu[:, sl], in_=ps[:, b1, :], func=AF.Copy, scale=1.0 / D),
                      reads=(self.psb[b1],), writes=(b_mu,))
                S.add("dve", lambda e, sl=sl: e.tensor_tensor(out=rs[:, sl], in0=mu[:, sl], in1=mu[:, sl], op=ALU.mult),
                      reads=(b_mu,), writes=(b_rs,))
                S.add("dve", lambda e, b2=b2, sl=sl: e.scalar_tensor_tensor(out=rs[:, sl], in0=ps[:, b2, :], scalar=1.0 / D, in1=rs[:, sl],
                                                                         op0=ALU.mult, op1=ALU.subtract),
                      reads=(self.psb[b2], b_rs), writes=(b_rs,))
                S.add("dve", lambda e, sl=sl: e.tensor_scalar(out=rs[:, sl], in0=rs[:, sl], scalar1=float(EPS), scalar2=None, op0=ALU.add),
                      reads=(b_rs,), writes=(b_rs,))
                S.add("act", lambda e, sl=sl: e.activation(out=rs[:, sl], in_=rs[:, sl], func=AF.Sqrt), reads=(b_rs,), writes=(b_rs,))
                S.add("dve", lambda e, sl=sl: e.reciprocal(out=rs[:, sl], in_=rs[:, sl]), reads=(b_rs,), writes=(b_rs,))
            for c in range(KC):
                z, zb, zds = zin.next()
                self.dma(z[:], self.zT[c * 128:(c + 1) * 128, hs_], (), (zb,), zds)
                y, yb, _ = yt.next()
                S.add("pool", lambda e, y=y, z=z: e.tensor_tensor(out=y[:], in0=z[:], in1=mu[:], op=ALU.subtract), reads=(zb, b_mu), writes=(yb,))
                S.add("dve", lambda e, y=y: e.tensor_tensor(out=y[:], in0=y[:], in1=rs[:], op=ALU.mult), reads=(yb, b_rs), writes=(yb,))
                S.add("act", lambda e, y=y, c=c, hs_=hs_: e.activation(out=self.xb[:, c, hs_], in_=y[:], func=AF.Identity, bias=gb[:, 1, c:c + 1], scale=gb[:, 0, c:c + 1]),
                      reads=(yb, b_gb), writes=(self.b_xb[c],))
                o, ob, ods = xo.next()
                if final:
                    S.add("act", lambda e, y=y, o=o, c=c: e.activation(out=o[:], in_=y[:], func=AF.Identity, bias=gb[:, 1, c:c + 1], scale=gb[:, 0, c:c + 1]),
                          reads=(yb, b_gb), writes=(ob,))
                    self.dma(self.zT[c * 128:(c + 1) * 128, hs_], o[:], (ob, zb), (), ods)
                else:
                    S.add("act", lambda e, y=y, o=o, c=c: e.activation(out=o[:], in_=y[:], func=AF.Identity, bias=gb[:, 3, c:c + 1], scale=gb[:, 2, c:c + 1]),
                          reads=(yb, b_gb), writes=(ob,))
                    self.dma(self.xaT[c * 128:(c + 1) * 128, hs_], o[:], (ob,), (), ods)
        self.barrier()
        S.sb_reset(mark)

    def mixer(self, l):
        self.ph_proj(l)
        st = self.stages or {}
        if st.get("stop") == ("proj", l):
            return
        self.S.sb_reset(self.mark_noxb)
        import os
        mx = os.environ.get("MIX", "acdb")
        if "a" in mx:
            self.mix_a()
        if "c" in mx:
            self.mix_c()
        if "d" in mx:
            self.mix_d()
        if "b" in mx:
            self.mix_b(l)
        self.S.sb_reset(self.base_mark)
        if st.get("stop") == ("attn", l):
            return
        self.ph_merge(l)
        self.ph_down(self.mergedT, self.w["w_out"][l], KC, 1.0)
        self.ph_ln(l * 3 + 1, False)

    def ph_proj(self, l):
        self.S.phase = "proj"
        S = self.S
        ps = self.ps
        xb = self.xb
        w = self.w["w_in"][l]
        mark = S.sb_mark()
        wsl = Rot(S, 2, [128, KC, 256], BF16, "wfm", sw=True)
        stg = Rot(S, 3, [128, L], BF16, "pst", dma=True)
        bgate = S.sb([128, 128], F32, "bgate")
        b_bg = S.buf("bgate")
        ds0 = S.dsem()
        cvt = S.sb([128, 128], F32, "cvtmp")
        b_cvt = S.buf("cvt")
        self.load_colvec(bgate[:], self.w["b_gate"][l], 128, b_bg, cvt, b_cvt, ds0)
        bfor = S.sb([8, 1], F32, "bfor")
        self.dma(bfor[:], self.w["b_forget"][l].rearrange("(p o) -> p o", o=1), (b_bg,), (b_bg,), ds0)
        sm = S.sb([128, L], F32, "smallst")
        b_sm = S.buf("sm")
        dsm = S.dsem()

        def fm_block(tl, bf, bi, ncol, kind, dst, arg=None):
            st_t, st_b, st_ds = (None, None, None)
            if kind in ("q", "k", "gate"):
                st_t, st_b, st_ds = stg.next()
            for t in range(4):
                bk = self.bank()

                def mm(e, tl=tl, bk=bk, t=t, bi=bi, ncol=ncol):
                    for k in range(KC):
                        ins = e.matmul(ps[:ncol, bk, :], tl[:, k, bi * 128:bi * 128 + ncol], xb[:, k, t * 512:(t + 1) * 512],
                                       start=(k == 0), stop=(k == KC - 1))
                    return ins
                S.add("pe", mm, reads=[bf] + self.b_xb, writes=(self.psb[bk],))
                sl = slice(t * 512, (t + 1) * 512)
                if kind == "q":
                    S.add("act", lambda e, st_t=st_t, bk=bk, sl=sl: e.activation(out=st_t[:, sl], in_=ps[:, bk, :], func=AF.Copy, scale=SCALE),
                          reads=(self.psb[bk],), writes=(st_b,))
                elif kind == "k":
                    S.add("dve", lambda e, st_t=st_t, bk=bk, sl=sl: e.tensor_copy(out=st_t[:, sl], in_=ps[:, bk, :]),
                          reads=(self.psb[bk],), writes=(st_b,))
                elif kind == "gate":
                    S.add("act", lambda e, st_t=st_t, bk=bk, sl=sl, arg=arg: e.activation(out=st_t[:, sl], in_=ps[:, bk, :], func=AF.Sigmoid,
                                                                                       bias=bgate[:, arg:arg + 1]),
                          reads=(self.psb[bk], b_bg), writes=(st_b,))
                elif kind == "bg":
                    S.add("act", lambda e, bk=bk, sl=sl: e.activation(out=sm[:24, sl], in_=ps[:24, bk, :], func=AF.Sigmoid),
                          reads=(self.psb[bk],), writes=(b_sm,))
                elif kind == "cf":
                    S.add("act", lambda e, bk=bk, sl=sl: e.activation(out=sm[:8, sl], in_=ps[:8, bk, :], func=AF.Identity, bias=bfor[:, 0:1]),
                          reads=(self.psb[bk], b_bg), writes=(b_sm,))
            if kind in ("q", "k", "gate"):
                self.dma(dst, st_t[:], (st_b,), (), st_ds)
            elif kind == "bg":
                self.dma(self.bgT, sm[:24, :], (b_sm,), (), dsm)
            elif kind == "cf":
                self.dma(self.cfT, sm[:8, :], (b_sm,), (), dsm)

        def load(col0, ncols, dup=False):
            tl, bf, ds = wsl.next()
            if dup:
                for h in range(2):
                    self.dma(tl[:, :, h * 64:(h + 1) * 64], w[:, col0:col0 + 64].rearrange("(kc p) n -> p kc n", p=128), (), (bf,), ds, q="pool")
            else:
                self.dma(tl[:, :, :ncols], w[:, col0:col0 + ncols].rearrange("(kc p) n -> p kc n", p=128), (), (bf,), ds, q="pool")
            return tl, bf

        def fm_range(col0, nblk, blk0, kind):
            b = 0
            while b < nblk:
                n = min(2, nblk - b)
                tl, bf = load(col0 + b * 128, n * 128)
                for i in range(n):
                    fm_block(tl, bf, i, 128, kind, self.pfm[blk0 + b + i])
                b += n

        fm_range(C_AQ, 12, 0, "q")
        fm_range(C_AK, 12, 12, "k")
        fm_range(C_BQ, 8, 24, "q")
        fm_range(C_BKV + 0, 2, 32, "k")
        fm_range(C_BKV + 256, 2, 34, "k")
        fm_range(C_BKV + 512, 2, 36, "k")
        fm_range(C_BKV + 1024, 2, 38, "k")
        fm_range(C_CQ, 8, 40, "q")
        fm_range(C_CK, 8, 48, "k")
        fm_range(C_DQ, 8, 56, "q")
        fm_range(C_DK, 1, 64, "k")
        fm_range(C_DIQ, 4, 65, "k")
        tl, bf = load(C_DIK, 64, dup=True)
        fm_block(tl, bf, 0, 128, "k", self.pfm[69])
        tl, bf = load(C_BG, 24)
        fm_block(tl, bf, 0, 24, "bg", None)
        tl, bf = load(C_CF, 8)
        fm_block(tl, bf, 0, 8, "cf", None)
        for gb in range(64):
            tl, bf = load(C_G + gb * 256, 256)
            for i in range(2):
                blk = gb * 2 + i
                fm_block(tl, bf, i, 128, "gate", self.gatesT[blk * 128:(blk + 1) * 128, :], arg=blk)
        self.barrier()
        S.sb_reset(mark)
        wtl = Rot(S, 2, [128, KC, 256], BF16, "wtm", sw=True)
        tst = Rot(S, 3, [128, 512], BF16, "tst", dma=True)
        ist = Rot(S, 2, [128, 8], F32, "ist", dma=True)
        tmjobs = [(C_AV + i * 256, 256, i * 256) for i in range(6)] + [(C_BKV + 6 * 128, 256, 1536), (C_BKV + 10 * 128, 256, 1792)] + \
                 [(C_CV + i * 256, 256, 2048 + i * 256) for i in range(4)] + [(C_DV, 128, 3072), (C_DIW, 8, -1)]
        for (col0, ncols, dcol) in tmjobs:
            tl, bf, ds = wtl.next()
            self.dma(tl[:, :, :ncols], w[:, col0:col0 + ncols].rearrange("(kc p) n -> p kc n", p=128), (), (bf,), ds, q="pool")
            for tt in range(NT):
                bk = self.bank()

                def mm(e, tl=tl, bk=bk, tt=tt, ncols=ncols):
                    for k in range(KC):
                        ins = e.matmul(ps[:, bk, :ncols], xb[:, k, tt * 128:(tt + 1) * 128], tl[:, k, :ncols], start=(k == 0), stop=(k == KC - 1))
                    return ins
                S.add("pe", mm, reads=[bf] + self.b_xb, writes=(self.psb[bk],))
                if dcol >= 0:
                    o, ob, ods = tst.next()
                    if tt % 2 == 0:
                        S.add("dve", lambda e, o=o, bk=bk, ncols=ncols: e.tensor_copy(out=o[:, :ncols], in_=ps[:, bk, :ncols]), reads=(self.psb[bk],), writes=(ob,))
                    else:
                        S.add("act", lambda e, o=o, bk=bk, ncols=ncols: e.activation(out=o[:, :ncols], in_=ps[:, bk, :ncols], func=AF.Copy), reads=(self.psb[bk],), writes=(ob,))
                    self.dma(self.ptm[tt * 128:(tt + 1) * 128, dcol:dcol + ncols], o[:, :ncols], (ob,), (), ods)
                else:
                    o, ob, ods = ist.next()
                    S.add("dve", lambda e, o=o, bk=bk: e.tensor_copy(out=o[:], in_=ps[:, bk, :8]), reads=(self.psb[bk],), writes=(ob,))
                    self.dma(self.iwtm[tt * 128:(tt + 1) * 128, :], o[:], (ob,), (), ods)
        self.barrier()
        S.sb_reset(mark)

    def attn_job(self, units, evac, ptile, tmpt):
        S = self.S
        ps = self.ps
        bo, bd = self.bank(), self.bank()
        ones = self.cstb[:, CO_ONE:CO_ONE + 128]
        LA = 3
        pend = []

        def emit_pv(item, first):
            u, p, pb, nq, c0 = item

            def pv(e, u=u, p=p, nq=nq, c0=c0, first=first):
                e.matmul(ps[:, bo, c0:c0 + nq], u["v"], p[:, :nq], start=first, stop=False, skip_group_check=True)
                return e.matmul(ps[:, bd, c0:c0 + nq], ones, p[:, :nq], start=first, stop=False, skip_group_check=True)
            S.add("pe", pv, reads=[pb, self.b_cst] + list(u["vreads"]), writes=(self.psb[bo], self.psb[bd]))

        npv = 0
        for u in units:
            nq, c0 = u["nq"], u["c0"]
            bs = self.bank()
            while bs in (bo, bd):
                bs = self.bank()
            rd = list(u["reads"])
            S.add("pe", lambda e, u=u, bs=bs, nq=nq: e.matmul(ps[:, bs, :nq], u["kT"], u["qT"], start=True, stop=True),
                  reads=rd, writes=(self.psb[bs],))
            p, pb, _ = ptile.next()
            bias = u.get("bias")
            cb = float(u.get("cb", 0.0))
            if bias is None:
                S.add("act", lambda e, p=p, bs=bs, nq=nq, cb=cb: e.activation(out=p[:, :nq], in_=ps[:, bs, :nq], func=AF.Exp, bias=cb),
                      reads=(self.psb[bs],), writes=(pb,))
            else:
                tm, tmb, _ = tmpt.next()
                if bias[0] == "alibi":
                    slope, d0 = bias[1], bias[2]
                    S.add("dve", lambda e, tm=tm, bs=bs, nq=nq, slope=slope, d0=d0: e.scalar_tensor_tensor(
                        out=tm[:, :nq], in0=d0, scalar=-float(slope), in1=ps[:, bs, :nq], op0=ALU.mult, op1=ALU.add),
                        reads=(self.psb[bs], self.b_cst), writes=(tmb,))
                else:
                    csT, csbc, brd = bias[1], bias[2], bias[3]
                    S.add("dve", lambda e, tm=tm, bs=bs, nq=nq, csT=csT, csbc=csbc: e.scalar_tensor_tensor(
                        out=tm[:, :nq], in0=ps[:, bs, :nq], scalar=csT, in1=csbc, op0=ALU.add, op1=ALU.subtract),
                        reads=[self.psb[bs]] + list(brd), writes=(tmb,))
                S.add("act", lambda e, p=p, tm=tm, nq=nq, cb=cb: e.activation(out=p[:, :nq], in_=tm[:, :nq], func=AF.Exp, bias=cb),
                      reads=(tmb,), writes=(pb,))
            for (mo, mw, map_) in u.get("masks", ()):
                S.add("pool", lambda e, p=p, mo=mo, mw=mw, map_=map_: e.tensor_tensor(out=p[:, mo:mo + mw], in0=p[:, mo:mo + mw], in1=map_, op=ALU.mult),
                      reads=(pb, self.b_cst), writes=(pb,))
            full = u.get("full")
            if full is not None:
                fap, fb = full
                S.add("pool", lambda e, p=p, nq=nq, fap=fap: e.tensor_tensor(out=p[:, :nq], in0=p[:, :nq], in1=fap, op=ALU.mult),
                      reads=(pb, fb), writes=(pb,))
            pend.append((u, p, pb, nq, c0))
            if len(pend) > LA:
                emit_pv(pend.pop(0), npv == 0)
                npv += 1
        while pend:
            emit_pv(pend.pop(0), npv == 0)
            npv += 1
        evac(bo, bd)

    def evac_norm(self, dst, dstb, c0=0, n=512):
        S = self.S
        ps = self.ps

        def f(bo, bd):
            r, rb, _ = self.rden.next()
            S.add("dve", lambda e, r=r, bd=bd: e.reciprocal(out=r[:, :n], in_=ps[:, bd, :n]), reads=(self.psb[bd],), writes=(rb,))
            S.add("dve", lambda e, r=r, bo=bo: e.tensor_tensor(out=dst[:, c0:c0 + n], in0=ps[:, bo, :n], in1=r[:, :n], op=ALU.mult),
                  reads=(self.psb[bo], rb), writes=(dstb,))
        return f

    def causal_units(self, qg, kT, qT, vt, bias_fn, reads, vreads, n_prev=None, full_fn=None):
        CM = self.cstb[:, CO_CM:CO_CM + 128]
        LT = self.cstb[:, CO_LT:CO_LT + 128]
        LE = self.cstb[:, CO_LE:CO_LE + 128]
        units = []
        k_lo = 0 if n_prev is None else max(0, 4 * qg - n_prev)
        for kt in range(k_lo, 4 * qg + 4):
            q_lo = max(kt, 4 * qg)
            q_hi = 4 * qg + 3 if n_prev is None else min(kt + n_prev, 4 * qg + 3)
            c0 = (q_lo - 4 * qg) * 128
            nq = (q_hi - q_lo + 1) * 128
            r0 = q_lo - kt
            masks = []
            if q_lo == kt:
                masks.append((0, 128, CM))
            if n_prev is not None and q_hi == kt + n_prev:
                masks.append((nq - 128, 128, LE if n_prev == 1 else LT))
            u = dict(kT=kT[:, kt * 128:(kt + 1) * 128], qT=qT[:, qg * 512 + c0: qg * 512 + c0 + nq], v=vt[:, kt, :],
                     c0=c0, nq=nq, masks=masks, reads=reads, vreads=vreads)
            bias_fn(u, kt, qg, c0, nq, r0)
            if full_fn is not None:
                u["full"] = full_fn(kt, c0, nq)
            units.append(u)
        return units

    def alibi_bias(self, slope):
        d0t = self.cst[:, CO_D0:CO_D0 + 512]
        d0p = self.cst[:, CO_D0P:CO_D0P + 512]

        def f(u, kt, qg, c0, nq, r0):
            u["bias"] = ("alibi", slope, (d0p if r0 == 0 else d0t)[:, :nq])
            u["cb"] = -slope * 128.0 * r0
        return f

    def load_qkv(self, qblk, kblk, vcol, rot):
        q, qb, qds = rot["q"].next()
        k, kb, kds = rot["k"].next()
        v, vb, vds = rot["v"].next()
        self.dma(q[:], self.pfm[qblk], (), (qb,), qds)
        self.dma(k[:], self.pfm[kblk], (), (kb,), kds)
        self.dma(v[:], self.ptm[:, vcol:vcol + 128].rearrange("(kt p) d -> p kt d", p=128), (), (vb,), vds)
        return (q, qb), (k, kb), (v, vb)

    def attn_common(self):
        S = self.S
        self.ptile = Rot(S, 6, [128, 512], BF16, "pt")
        self.tmpt = Rot(S, 4, [128, 512], F32, "tmp")
        self.rden = Rot(S, 2, [128, 512], F32, "rden")

    def mix_a(self):
        self.S.phase = "mixa"
        S = self.S
        ps = self.ps
        mark = S.sb_mark()
        self.attn_common()
        rot = {"q": Rot(S, 3, [128, L], BF16, "aq", dma=True), "k": Rot(S, 3, [128, L], BF16, "ak", dma=True),
               "v": Rot(S, 3, [128, NT, 128], BF16, "av", dma=True)}
        OA = Rot(S, 2, [128, L], F32, "OA")
        DA = Rot(S, 2, [128, L], F32, "DA")
        ob = Rot(S, 2, [128, L], BF16, "oab", dma=True)
        CMr = self.cstb[:, CO_CMR:CO_CMR + 512]
        D0r = self.cst[:, CO_D0R:CO_D0R + 512]
        ones = self.cstb[:, CO_ONE:CO_ONE + 128]
        for h in range(4):
            oa, oab, _ = OA.next()
            da, dab, _ = DA.next()
            slope = SLOPES[A_SLOPE_IDX[h]]
            (q, qb), (k, kb), (v, vb) = self.load_qkv(h, 12 + h, h * 128, rot)
            for qg in range(4):
                units = self.causal_units(qg, k, q, v, self.alibi_bias(slope), (qb, kb), (vb,), n_prev=1)

                def ev(bo, bd, qg=qg, oa=oa, da=da, oab=oab, dab=dab):
                    sl = slice(qg * 512, (qg + 1) * 512)
                    S.add("dve", lambda e: e.tensor_copy(out=oa[:, sl], in_=ps[:, bo, :]), reads=(self.psb[bo],), writes=(oab,))
                    S.add("act", lambda e: e.activation(out=da[:, sl], in_=ps[:, bd, :], func=AF.Copy), reads=(self.psb[bd],), writes=(dab,))
                self.attn_job(units, ev, self.ptile, self.tmpt)
            hd = 4 + h
            slope = SLOPES[A_SLOPE_IDX[hd]] * 4.0
            q, qb, qds = rot["q"].next()
            k, kb, kds = rot["k"].next()
            v, vb, vds = rot["v"].next()
            self.dma(q[:], self.pfm[hd], (), (qb,), qds)
            self.dma(k[:], self.pfm[12 + hd], (), (kb,), kds)
            for r in range(4):
                self.dma(v[:, r * 4:(r + 1) * 4, :],
                         self.ptm[:, hd * 128:(hd + 1) * 128].rearrange("(kt j r) d -> r j kt d", r=4, j=128)[r], (), (vb,), vds)
            for r in range(4):
                qs = q[:, r:L:4]
                ks = k[:, r:L:4]
                units = self.causal_units(0, ks, qs, v[:, r * 4:(r + 1) * 4, :], self.alibi_bias(slope), (qb, kb), (vb,), n_prev=1)

                def ev(bo, bd, r=r, oa=oa, da=da, oab=oab, dab=dab):
                    S.add("dve", lambda e: e.tensor_tensor(out=oa[:, r:L:4], in0=oa[:, r:L:4], in1=ps[:, bo, :], op=ALU.add),
                          reads=(self.psb[bo], oab), writes=(oab,))
                    S.add("dve", lambda e: e.tensor_tensor(out=da[:, r:L:4], in0=da[:, r:L:4], in1=ps[:, bd, :], op=ALU.add),
                          reads=(self.psb[bd], dab), writes=(dab,))
                self.attn_job(units, ev, self.ptile, self.tmpt)
            hd = 8 + h
            slope = SLOPES[A_SLOPE_IDX[hd]] * 16.0
            q, qb, qds = rot["q"].next()
            k, kb, kds = rot["k"].next()
            v, vb, vds = rot["v"].next()
            self.dma(q[:], self.pfm[hd], (), (qb,), qds)
            self.dma(k[:], self.pfm[12 + hd], (), (kb,), kds)
            for r4 in range(4):
                self.dma(v[:, r4 * 4:(r4 + 1) * 4, :],
                         self.ptm[:, hd * 128:(hd + 1) * 128].rearrange("(j r) d -> j r d", r=16)[:, r4 * 4:(r4 + 1) * 4, :], (), (vb,), vds)
            for r4 in range(4):
                bo, bd, bs = self.bank(), self.bank(), self.bank()

                def qk(e, r4=r4, bs=bs, q=q, k=k):
                    for i in range(4):
                        r = r4 * 4 + i
                        ins = e.matmul(ps[:, bs, i * 128:(i + 1) * 128], k[:, r:L:16], q[:, r:L:16], start=True, stop=True)
                    return ins
                S.add("pe", qk, reads=(qb, kb), writes=(self.psb[bs],))
                tm, tmb, _ = self.tmpt.next()
                p, pb, _ = self.ptile.next()
                S.add("dve", lambda e, tm=tm, bs=bs, slope=slope: e.scalar_tensor_tensor(out=tm[:], in0=D0r, scalar=-float(slope), in1=ps[:, bs, :],
                                                                                       op0=ALU.mult, op1=ALU.add),
                      reads=(self.psb[bs], self.b_cst), writes=(tmb,))
                S.add("act", lambda e, p=p, tm=tm: e.activation(out=p[:], in_=tm[:], func=AF.Exp), reads=(tmb,), writes=(pb,))
                S.add("pool", lambda e, p=p: e.tensor_tensor(out=p[:], in0=p[:], in1=CMr, op=ALU.mult), reads=(pb, self.b_cst), writes=(pb,))

                def pv(e, r4=r4, p=p, bo=bo, bd=bd, v=v):
                    for i in range(4):
                        e.matmul(ps[:, bo, i * 128:(i + 1) * 128], v[:, r4 * 4 + i, :], p[:, i * 128:(i + 1) * 128], start=(i == 0), stop=False, skip_group_check=True)
                    for i in range(4):
                        ins = e.matmul(ps[:, bd, i * 128:(i + 1) * 128], ones, p[:, i * 128:(i + 1) * 128], start=(i == 0), stop=False, skip_group_check=True)
                    return ins
                S.add("pe", pv, reads=(pb, vb, self.b_cst), writes=(self.psb[bo], self.psb[bd]))
                oav = oa[:].rearrange("p (j r) -> p r j", r=16)[:, r4 * 4:(r4 + 1) * 4, :]
                dav = da[:].rearrange("p (j r) -> p r j", r=16)[:, r4 * 4:(r4 + 1) * 4, :]
                S.add("dve", lambda e, oav=oav, bo=bo: e.tensor_tensor(out=oav, in0=oav, in1=ps[:, bo, :].rearrange("p (i j) -> p i j", i=4), op=ALU.add),
                      reads=(self.psb[bo], oab), writes=(oab,))
                S.add("dve", lambda e, dav=dav, bd=bd: e.tensor_tensor(out=dav, in0=dav, in1=ps[:, bd, :].rearrange("p (i j) -> p i j", i=4), op=ALU.add),
                      reads=(self.psb[bd], dab), writes=(dab,))
            o, obb, ods = ob.next()
            S.add("dve", lambda e, da=da: e.reciprocal(out=da[:], in_=da[:]), reads=(dab,), writes=(dab,))
            S.add("dve", lambda e, o=o, oa=oa, da=da: e.tensor_tensor(out=o[:], in0=oa[:], in1=da[:], op=ALU.mult), reads=(oab, dab), writes=(obb,))
            self.dma(self.obT[h], o[:], (obb,), (), ods)
        self.barrier()
        S.sb_reset(mark)

    def mix_c(self):
        self.S.phase = "mixc"
        S = self.S
        ps = self.ps
        mark = S.sb_mark()
        self.attn_common()
        rot = {"q": Rot(S, 2, [128, L], BF16, "cq", dma=True), "k": Rot(S, 2, [128, L], BF16, "ck", dma=True),
               "v": Rot(S, 2, [128, NT, 128], BF16, "cv", dma=True)}
        ob = Rot(S, 2, [128, L], BF16, "ocb", dma=True)
        cf = S.sb([8, L], F32, "cf")
        cs = S.sb([8, L], F32, "cs")
        onesr = S.sb([8, L], F32, "onesr")
        b_cf = S.buf("cf")
        ds = S.dsem()
        self.dma(cf[:], self.cfT, (), (b_cf,), ds)
        S.add("act", lambda e: e.activation(out=cf[:], in_=cf[:], func=AF.Exp, scale=-1.0), reads=(b_cf,), writes=(b_cf,))
        S.add("act", lambda e: e.activation(out=cf[:], in_=cf[:], func=AF.Ln, bias=1.0), reads=(b_cf,), writes=(b_cf,))
        S.add("dve", lambda e: e.memset(onesr[:], 1.0), reads=(), writes=(b_cf,))
        S.add("dve", lambda e: e.tensor_tensor_scan(out=cs[:], data0=onesr[:], data1=cf[:], initial=0.0, op0=ALU.mult, op1=ALU.add),
              reads=(b_cf,), writes=(b_cf,))
        b_csd = S.buf("csd")
        self.dma(self.csd, cs[:], (b_cf,), (b_csd,), ds)
        csT = S.sb([128, NT, 8], F32, "csT")
        b_csT = S.buf("csT")
        for tt in range(NT):
            self.tr_small(csT[:, tt, :], cs[:, tt * 128:(tt + 1) * 128], 8, (b_cf,), b_csT)
        cbc = Rot(S, 2, [128, L], F32, "cbc", dma=True)
        for h in range(8):
            (q, qb), (k, kb), (v, vb) = self.load_qkv(40 + h, 48 + h, 2048 + h * 128, rot)
            cb_t, cb_b, cb_ds = cbc.next()
            self.dma(cb_t[:], self.csd[h:h + 1, :].to_broadcast([128, L]), (b_csd,), (cb_b,), cb_ds)
            o, obb, ods = ob.next()

            def bias_fn(u, kt, qg, c0, nq, r0, h=h, cb_t=cb_t, cb_b=cb_b):
                u["bias"] = ("fox", csT[:, kt, h:h + 1], cb_t[:, qg * 512 + c0: qg * 512 + c0 + nq], (b_csT, cb_b))
            for qg in range(4):
                units = self.causal_units(qg, k, q, v, bias_fn, (qb, kb), (vb,))
                self.attn_job(units, self.evac_norm(o, obb, qg * 512), self.ptile, self.tmpt)
            self.dma(self.obT[12 + h], o[:], (obb,), (), ods)
        self.barrier()
        S.sb_reset(mark)

    def mix_d(self):
        self.S.phase = "mixd"
        S = self.S
        ps = self.ps
        mark = S.sb_mark()
        self.attn_common()
        ident = self.cst[:, CO_ID:CO_ID + 128]
        NLE = self.cst[:, CO_NLE:CO_NLE + 128]
        qs = S.sb([128, 8, L], BF16, "dq")
        kk = S.sb([128, L], BF16, "dk")
        vv = S.sb([128, NT, 128], BF16, "dv")
        iq = S.sb([128, 4, L], BF16, "diq")
        ik = S.sb([128, L], BF16, "dik")
        iw = S.sb([128, NT, 8], F32, "diw")
        b_in = S.buf("din")
        ds = S.dsem()
        for h in range(8):
            self.dma(qs[:, h, :], self.pfm[56 + h], (), (b_in,), ds)
        self.dma(kk[:], self.pfm[64], (), (b_in,), ds)
        self.dma(vv[:], self.ptm[:, 3072:3200].rearrange("(kt p) d -> p kt d", p=128), (), (b_in,), ds)
        for b in range(4):
            self.dma(iq[:, b, :], self.pfm[65 + b], (), (b_in,), ds)
        self.dma(ik[:], self.pfm[69], (), (b_in,), ds)
        self.dma(iw[:], self.iwtm.rearrange("(tt p) h -> p tt h", p=128), (), (b_in,), ds)
        score = Rot(S, 2, [128, L], F32, "score")
        work = S.sb([128, L], F32, "work")
        b_work = S.buf("work")
        m8 = S.sb([128, 8], F32, "m8")
        msk = Rot(S, 2, [128, L], F32, "msk")
        relu = Rot(S, 3, [128, 512], F32, "relu")
        maskT = Rot(S, 2, [128, NT, 512], BF16, "maskT")
        ob = Rot(S, 8, [128, 512], BF16, "odb", dma=True)
        for qg in range(4):
            mT, mTb, _ = maskT.next()
            for qi in range(4):
                qt = qg * 4 + qi
                nk = (qt + 1) * 128
                sc, scb, _ = score.next()
                for kg in range((nk + 511) // 512):
                    n = min(512, nk - kg * 512)
                    for h in range(8):
                        bk = self.bank()
                        pr = 64 * (h % 2)
                        S.add("pe", lambda e, bk=bk, h=h, pr=pr, qt=qt, kg=kg, n=n: e.matmul(
                            ps[:, bk, :n], iq[pr:pr + 64, h // 2, qt * 128:(qt + 1) * 128], ik[pr:pr + 64, kg * 512:kg * 512 + n], start=True, stop=True),
                            reads=(b_in,), writes=(self.psb[bk],))
                        r, rb, _ = relu.next()
                        S.add("act", lambda e, r=r, bk=bk, n=n: e.activation(out=r[:, :n], in_=ps[:, bk, :n], func=AF.Relu), reads=(self.psb[bk],), writes=(rb,))
                        sl = slice(kg * 512, kg * 512 + n)
                        if h == 0:
                            S.add("dve", lambda e, sc=sc, r=r, n=n, sl=sl, qt=qt: e.tensor_scalar(out=sc[:, sl], in0=r[:, :n], scalar1=iw[:, qt, 0:1], scalar2=None, op0=ALU.mult),
                                  reads=(rb, b_in), writes=(scb,))
                        else:
                            S.add("dve", lambda e, sc=sc, r=r, n=n, sl=sl, qt=qt, h=h: e.scalar_tensor_tensor(out=sc[:, sl], in0=r[:, :n], scalar=iw[:, qt, h:h + 1], in1=sc[:, sl],
                                                                                                        op0=ALU.mult, op1=ALU.add),
                                  reads=(rb, b_in, scb), writes=(scb,))
                dsl = slice(qt * 128, (qt + 1) * 128)
                S.add("dve", lambda e, sc=sc, dsl=dsl: e.tensor_tensor(out=sc[:, dsl], in0=sc[:, dsl], in1=NLE, op=ALU.add), reads=(scb, self.b_cst), writes=(scb,))
                mk, mkb, _ = msk.next()
                if qt < 2:
                    S.add("dve", lambda e, mk=mk, sc=sc, nk=nk: e.tensor_scalar(out=mk[:, :nk], in0=sc[:, :nk], scalar1=-1e29, scalar2=None, op0=ALU.is_ge),
                          reads=(scb,), writes=(mkb,))
                else:
                    S.add("dve", lambda e, sc=sc, nk=nk: e.tensor_copy(out=work[:, :nk], in_=sc[:, :nk]), reads=(scb,), writes=(b_work,))
                    for rnd in range(32):
                        S.add("dve", lambda e, nk=nk: e.max(out=m8[:], in_=work[:, :nk]), reads=(b_work,), writes=(b_work,))
                        if rnd < 31:
                            S.add("dve", lambda e, nk=nk: e.match_replace(out=work[:, :nk], in_to_replace=m8[:], in_values=work[:, :nk], imm_value=-3e38),
                                  reads=(b_work,), writes=(b_work,))
                    S.add("dve", lambda e, mk=mk, sc=sc, nk=nk: e.tensor_scalar(out=mk[:, :nk], in0=sc[:, :nk], scalar1=m8[:, 7:8], scalar2=None, op0=ALU.is_ge),
                          reads=(scb, b_work), writes=(mkb,))
                for k4 in range((qt + 4) // 4):
                    nkt = min(4, qt + 1 - k4 * 4)
                    bk = self.bank()

                    def tr(e, mk=mk, bk=bk, k4=k4, nkt=nkt):
                        for j in range(nkt):
                            kt = k4 * 4 + j
                            ins = e.transpose(ps[:, bk, j * 128:(j + 1) * 128], mk[:, kt * 128:(kt + 1) * 128], ident)
                        return ins
                    S.add("pe", tr, reads=(mkb, self.b_cst), writes=(self.psb[bk],))
                    S.add("act", lambda e, mT=mT, bk=bk, k4=k4, nkt=nkt, qi=qi: e.activation(
                        out=mT[:, k4 * 4:k4 * 4 + nkt, qi * 128:(qi + 1) * 128], in_=ps[:, bk, :nkt * 128].rearrange("p (j q) -> p j q", j=nkt), func=AF.Copy),
                        reads=(self.psb[bk],), writes=(mTb,))
            for h in range(8):
                slope = SLOPES[D_SLOPE_IDX[h]]
                o, obb, ods = ob.next()
                units = self.causal_units(qg, kk, qs[:, h, :], vv, self.alibi_bias(slope), (b_in,), (b_in,),
                                          full_fn=lambda kt, c0, nq, mT=mT, mTb=mTb: (mT[:, kt, c0:c0 + nq], mTb))
                self.attn_job(units, self.evac_norm(o, obb, 0), self.ptile, self.tmpt)
                self.dma(self.obT[20 + h, :, qg * 512:(qg + 1) * 512], o[:], (obb,), (), ods)
        self.barrier()
        S.sb_reset(mark)

    def mix_b(self, l):
        self.S.phase = "mixb"
        S = self.S
        ps = self.ps
        mark = S.sb_mark()
        self.attn_common()
        ident = self.cst[:, CO_ID:CO_ID + 128]
        kcT = S.sb([128, 2, 128], BF16, "kcT")
        vc = S.sb([128, 2, 128], BF16, "vc")
        b_kc, b_vc = S.buf("kc"), S.buf("vc")
        m2 = S.sb_mark()
        w1s = Rot(S, 2, [128, 32, 512], BF16, "w1s", sw=True)
        w2s = Rot(S, 2, [128, 4, 128], BF16, "w2s", sw=True)
        posT = Rot(S, 2, [128, 32], F32, "posT")
        posraw = Rot(S, 2, [32, 128], F32, "posraw", dma=True)
        src_t = Rot(S, 2, [128, L], BF16, "cmpsrc", dma=True)
        kvp = Rot(S, 2, [128, 32, 128], BF16, "kvp")
        hid = Rot(S, 2, [128, 4, 128], BF16, "hid")
        gt = Rot(S, 4, [128, 128], F32, "gelu")
        for j in range(2):
            w1, w1b, w1d = w1s.next()
            for q4 in range(4):
                self.dma(w1[:, q4 * 8:(q4 + 1) * 8, :], self.w["cmp_w1"][l * 2 + j, q4 * 1024:(q4 + 1) * 1024, :].rearrange("(p d) h -> d p h", d=128),
                         (), (w1b,), w1d, q="pool")
            w2, w2b, w2d = w2s.next()
            self.dma(w2[:], self.w["cmp_w2"][l * 2 + j].rearrange("(hc p) d -> p hc d", p=128), (), (w2b,), w2d, q="pool")
            pT, pTb, pTd = posT.next()
            pr_t, pr_b, pr_d = posraw.next()
            self.dma(pr_t[:], self.w["cmp_pos"][l * 2 + j], (), (pr_b,), pr_d)
            self.tr_small(pT[:], pr_t[:], 32, (pr_b,), pTb)
            for g in range(2):
                sr, srb, srd = src_t.next()
                self.dma(sr[:], self.pfm[32 + j * 2 + g], (), (srb,), srd)
                kv, kvb, _ = kvp.next()
                for p in range(32):
                    S.add("dve" if p % 2 == 0 else "pool",
                          lambda e, kv=kv, sr=sr, pT=pT, p=p: e.tensor_scalar(out=kv[:, p, 0:127], in0=sr[:, p:p + 16 * 126 + 1:16], scalar1=pT[:, p:p + 1], scalar2=None, op0=ALU.add),
                          reads=(srb, pTb), writes=(kvb,))
                hd, hdb, _ = hid.next()
                for hc in range(4):
                    bk = self.bank()

                    def mm(e, w1=w1, kv=kv, bk=bk, hc=hc):
                        for p in range(32):
                            ins = e.matmul(ps[:, bk, :127], w1[:, p, hc * 128:(hc + 1) * 128], kv[:, p, 0:127], start=(p == 0), stop=(p == 31))
                        return ins
                    S.add("pe", mm, reads=(w1b, kvb), writes=(self.psb[bk],))
                    u, ub, _ = gt.next()
                    S.add("act", lambda e, u=u, bk=bk: e.activation(out=u[:, :127], in_=ps[:, bk, :127], func=AF.Square), reads=(self.psb[bk],), writes=(ub,))
                    S.add("dve", lambda e, u=u: e.tensor_scalar(out=u[:, :127], in0=u[:, :127], scalar1=0.044715, scalar2=1.0, op0=ALU.mult, op1=ALU.add), reads=(ub,), writes=(ub,))
                    S.add("dve", lambda e, u=u, bk=bk: e.tensor_tensor(out=u[:, :127], in0=u[:, :127], in1=ps[:, bk, :127], op=ALU.mult), reads=(ub, self.psb[bk]), writes=(ub,))
                    S.add("act", lambda e, u=u: e.activation(out=u[:, :127], in_=u[:, :127], func=AF.Sigmoid, scale=1.5957691216057308), reads=(ub,), writes=(ub,))
                    S.add("dve", lambda e, u=u, hd=hd, hc=hc, bk=bk: e.tensor_tensor(out=hd[:, hc, :127], in0=u[:, :127], in1=ps[:, bk, :127], op=ALU.mult),
                          reads=(ub, self.psb[bk]), writes=(hdb,))
                bk = self.bank()
                if j == 0:
                    def mm2(e, w2=w2, hd=hd, bk=bk):
                        for hc in range(4):
                            ins = e.matmul(ps[:, bk, :127], w2[:, hc, :], hd[:, hc, :127], start=(hc == 0), stop=(hc == 3))
                        return ins
                    S.add("pe", mm2, reads=(w2b, hdb), writes=(self.psb[bk],))
                    S.add("dve", lambda e, g=g, bk=bk: e.tensor_copy(out=kcT[:, g, :127], in_=ps[:, bk, :127]), reads=(self.psb[bk],), writes=(b_kc,))
                else:
                    def mm2(e, w2=w2, hd=hd, bk=bk):
                        for hc in range(4):
                            ins = e.matmul(ps[:127, bk, :128], hd[:, hc, :127], w2[:, hc, :], start=(hc == 0), stop=(hc == 3))
                        return ins
                    S.add("pe", mm2, reads=(w2b, hdb), writes=(self.psb[bk],))
                    S.add("dve", lambda e, g=g, bk=bk: e.tensor_copy(out=vc[:127, g, :], in_=ps[:127, bk, :128]), reads=(self.psb[bk],), writes=(b_vc,))
        self.barrier()
        S.all_bufs.extend([b_kc, b_vc])
        S.sb_reset(m2)
        qall = S.sb([128, 8, L], BF16, "bq")
        b_q = S.buf("bq")
        dq = S.dsem()
        for hb in range(8):
            self.dma(qall[:, hb, :], self.pfm[24 + hb], (), (b_q,), dq)
        selc = S.sb([128, 16, 96], F32, "selc")
        b_selc = S.buf("selc")
        self.dma(selc[:], self.selc_d.rearrange("p (a b) -> p a b", b=96), (), (b_selc,), dq)
        selE = S.sb([32, L], BF16, "selE")
        b_selE = S.buf("selE")
        self.dma(selE[:], self.sel_d, (), (b_selE,), S.dsem(True), q="pool")
        ocmp = S.sb([128, 8, L], BF16, "ocmp")
        b_oc = [S.buf("oc") for _ in range(8)]
        bmT = S.sb([32, 2, L], BF16, "bmT")
        b_bm = [S.buf("bm0"), S.buf("bm1")]
        dcl = Rot(S, 2, [128, 128], F32, "dcl")
        ngm = Rot(S, 2, [128, 128], F32, "ngm")
        et = Rot(S, 8, [128, 128], F32, "et")
        t1 = Rot(S, 8, [128, 128], F32, "t1")
        den = Rot(S, 8, [128, 2], F32, "den")
        imp = Rot(S, 2, [128, 132], F32, "imp")
        psl = Rot(S, 2, [128, 64], F32, "psl")
        m8 = Rot(S, 2, [128, 8], F32, "m8b")
        pT = Rot(S, 8, [128, 128], BF16, "pT")
        DC = self.cst[:, CO_DC:CO_DC + 128]
        for qt in range(NT):
            dc, dcb, _ = dcl.next()
            ng, ngb, _ = ngm.next()
            S.add("dve", lambda e, dc=dc, qt=qt: e.tensor_scalar(out=dc[:], in0=DC, scalar1=float(128 * qt), scalar2=0.0, op0=ALU.add, op1=ALU.max),
                  reads=(self.b_cst,), writes=(dcb,))
            S.add("dve", lambda e, ng=ng, qt=qt: e.tensor_scalar(out=ng[:], in0=DC, scalar1=float(-128 * qt), scalar2=NEGBIG, op0=ALU.is_lt, op1=ALU.mult),
                  reads=(self.b_cst,), writes=(ngb,))
            qsl = slice(qt * 128, (qt + 1) * 128)
            for g in range(2):
                im, imb, _ = imp.next()
                S.add("pool", lambda e, im=im: e.memset(im[:], 0.0), reads=(), writes=(imb,))
                bo = self.bank()
                bks, ets, ebs = [], [], []
                for r in range(4):
                    hb = g * 4 + r
                    bk = self.bank()
                    while bk == bo or bk in bks:
                        bk = self.bank()
                    bks.append(bk)
                    S.add("pe", lambda e, bk=bk, hb=hb, g=g, qsl=qsl: e.matmul(ps[:, bk, :127], qall[:, hb, qsl], kcT[:, g, :127], start=True, stop=True),
                          reads=(b_q, b_kc), writes=(self.psb[bk],))
                for r in range(4):
                    hb = g * 4 + r
                    slope = SLOPES[B_SLOPE_IDX[hb]]
                    bk = bks[r]
                    tt, ttb, _ = t1.next()
                    S.add("dve", lambda e, tt=tt, dc=dc, ng=ng, slope=slope: e.scalar_tensor_tensor(out=tt[:, :127], in0=dc[:, :127], scalar=-float(slope), in1=ng[:, :127],
                                                                                                 op0=ALU.mult, op1=ALU.add),
                          reads=(dcb, ngb), writes=(ttb,))
                    S.add("dve", lambda e, tt=tt, bk=bk: e.tensor_tensor(out=tt[:, :127], in0=tt[:, :127], in1=ps[:, bk, :127], op=ALU.add),
                          reads=(ttb, self.psb[bk]), writes=(ttb,))
                    e_t, eb, _ = et.next()
                    dn, dnb, _ = den.next()
                    S.add("pool", lambda e, e_t=e_t: e.memset(e_t[:, 127:128], 0.0), reads=(), writes=(eb,))
                    S.add("pool", lambda e, dn=dn: e.memset(dn[:], 0.0), reads=(), writes=(dnb,))
                    S.add("act", lambda e, e_t=e_t, tt=tt, dn=dn: e.activation(out=e_t[:, :127], in_=tt[:, :127], func=AF.Exp, accum_out=dn[:, 0:1]),
                          reads=(ttb,), writes=(eb, dnb))
                    S.add("dve", lambda e, dn=dn: e.tensor_scalar(out=dn[:, 1:2], in0=dn[:, 0:1], scalar1=1e-30, scalar2=None, op0=ALU.max), reads=(dnb,), writes=(dnb,))
                    S.add("dve", lambda e, dn=dn: e.reciprocal(out=dn[:, 1:2], in_=dn[:, 1:2]), reads=(dnb,), writes=(dnb,))
                    S.add("dve", lambda e, e_t=e_t, dn=dn: e.tensor_scalar(out=e_t[:], in0=e_t[:], scalar1=dn[:, 1:2], scalar2=None, op0=ALU.mult), reads=(eb, dnb), writes=(eb,))
                    S.add("pool", lambda e, im=im, e_t=e_t: e.tensor_tensor(out=im[:, 1:128], in0=im[:, 1:128], in1=e_t[:, 0:127], op=ALU.add), reads=(eb, imb), writes=(imb,))
                    ets.append(e_t)
                    ebs.append(eb)
                for r in range(4):
                    bt = bks[r]
                    S.add("pe", lambda e, bt=bt, e_t=ets[r]: e.transpose(ps[:, bt, :128], e_t[:], ident), reads=(ebs[r], self.b_cst), writes=(self.psb[bt],))
                for r in range(4):
                    bt = bks[r]
                    pt_, ptb, _ = pT.next()
                    S.add("act", lambda e, pt_=pt_, bt=bt: e.activation(out=pt_[:], in_=ps[:, bt, :128], func=AF.Copy), reads=(self.psb[bt],), writes=(ptb,))
                    S.add("pe", lambda e, bo=bo, r=r, g=g, pt_=pt_: e.matmul(ps[:, bo, r * 128:(r + 1) * 128], vc[:127, g, :], pt_[:127, :], start=(r == 0), stop=False, skip_group_check=True),
                          reads=(ptb, b_vc), writes=(self.psb[bo],))
                for r in range(4):
                    hb = g * 4 + r
                    S.add("act", lambda e, hb=hb, r=r, bo=bo, qsl=qsl: e.activation(out=ocmp[:, hb, qsl], in_=ps[:, bo, r * 128:(r + 1) * 128], func=AF.Copy),
                          reads=(self.psb[bo],), writes=(b_oc[hb],))
                pl, plb, _ = psl.next()
                S.add("dve", lambda e, pl=pl, im=im: e.tensor_scalar(out=pl[:, 0:32], in0=im[:, 0:128:4], scalar1=1.0, scalar2=None, op0=ALU.mult), reads=(imb,), writes=(plb,))
                for o, wgt in ((1, 2.0), (2, 2.0), (3, 2.0), (4, 1.0)):
                    S.add("dve", lambda e, pl=pl, im=im, o=o, wgt=wgt: e.scalar_tensor_tensor(out=pl[:, 0:32], in0=im[:, o:o + 128:4], scalar=wgt, in1=pl[:, 0:32], op0=ALU.mult, op1=ALU.add),
                          reads=(imb, plb), writes=(plb,))
                S.add("dve", lambda e, pl=pl, qt=qt: e.tensor_tensor(out=pl[:, 0:32], in0=pl[:, 0:32], in1=selc[:, qt, 0:32], op=ALU.mult), reads=(plb, b_selc), writes=(plb,))
                S.add("dve", lambda e, pl=pl, qt=qt: e.tensor_tensor(out=pl[:, 0:32], in0=pl[:, 0:32], in1=selc[:, qt, 32:64], op=ALU.add), reads=(plb, b_selc), writes=(plb,))
                S.add("dve", lambda e, pl=pl, qt=qt: e.tensor_tensor(out=pl[:, 0:32], in0=pl[:, 0:32], in1=selc[:, qt, 64:96], op=ALU.max), reads=(plb, b_selc), writes=(plb,))
                m, mb, _ = m8.next()
                S.add("dve", lambda e, m=m, pl=pl: e.max(out=m[:], in_=pl[:, 0:32]), reads=(plb,), writes=(mb,))
                S.add("dve", lambda e, m=m, pl=pl: e.tensor_scalar(out=pl[:, 32:64], in0=pl[:, 0:32], scalar1=m[:, 7:8], scalar2=None, op0=ALU.is_ge), reads=(plb, mb), writes=(plb,))
                bt = self.bank()
                S.add("pe", lambda e, bt=bt, pl=pl: e.transpose(ps[:32, bt, :128], pl[:, 32:64], ident), reads=(plb, self.b_cst), writes=(self.psb[bt],))
                S.add("act", lambda e, bt=bt, g=g, qsl=qsl: e.activation(out=bmT[:, g, qsl], in_=ps[:32, bt, :128], func=AF.Copy), reads=(self.psb[bt],), writes=(b_bm[g],))
        rot = {"k": Rot(S, 2, [128, L], BF16, "bk", dma=True), "v": Rot(S, 2, [128, NT, 128], BF16, "bv", dma=True)}
        mTr = Rot(S, 2, [128, NT, 512], BF16, "bmaskT")
        osl = Rot(S, 2, [128, 512], F32, "osl")
        owi = Rot(S, 2, [128, 512], F32, "owi")
        gbc = Rot(S, 2, [128, 3, 512], F32, "gbc", dma=True)
        ob = Rot(S, 3, [128, 512], BF16, "obb", dma=True)
        for g in range(2):
            ks, ksb, ksd = rot["k"].next()
            vs, vsb, vsd = rot["v"].next()
            kw, kwb, kwd = rot["k"].next()
            vw, vwb, vwd = rot["v"].next()
            self.dma(ks[:], self.pfm[36 + g], (), (ksb,), ksd)
            self.dma(kw[:], self.pfm[38 + g], (), (kwb,), kwd)
            self.dma(vs[:], self.ptm[:, 1536 + g * 128:1536 + (g + 1) * 128].rearrange("(kt p) d -> p kt d", p=128), (), (vsb,), vsd)
            self.dma(vw[:], self.ptm[:, 1792 + g * 128:1792 + (g + 1) * 128].rearrange("(kt p) d -> p kt d", p=128), (), (vwb,), vwd)
            for qg in range(4):
                mT, mTb, _ = mTr.next()
                for kt in range(4 * qg + 4):
                    bk = self.bank()
                    S.add("pe", lambda e, bk=bk, kt=kt, g=g, qg=qg: e.matmul(ps[:, bk, :], selE[:, kt * 128:(kt + 1) * 128], bmT[:, g, qg * 512:(qg + 1) * 512], start=True, stop=True),
                          reads=(b_selE, b_bm[g]), writes=(self.psb[bk],))
                    if kt % 2 == 0:
                        S.add("act", lambda e, mT=mT, bk=bk, kt=kt: e.activation(out=mT[:, kt, :], in_=ps[:, bk, :], func=AF.Copy), reads=(self.psb[bk],), writes=(mTb,))
                    else:
                        S.add("dve", lambda e, mT=mT, bk=bk, kt=kt: e.tensor_copy(out=mT[:, kt, :], in_=ps[:, bk, :]), reads=(self.psb[bk],), writes=(mTb,))
                for r in range(4):
                    hb = g * 4 + r
                    slope = SLOPES[B_SLOPE_IDX[hb]]
                    q = qall[:, hb, :]
                    o_s, osb, _ = osl.next()
                    o_w, owb, _ = owi.next()
                    units = self.causal_units(qg, ks, q, vs, self.alibi_bias(slope), (b_q, ksb), (vsb,),
                                              full_fn=lambda kt, c0, nq, mT=mT, mTb=mTb: (mT[:, kt, c0:c0 + nq], mTb))
                    self.attn_job(units, self.evac_norm(o_s, osb, 0), self.ptile, self.tmpt)
                    units = self.causal_units(qg, kw, q, vw, self.alibi_bias(slope), (b_q, kwb), (vwb,), n_prev=4)
                    self.attn_job(units, self.evac_norm(o_w, owb, 0), self.ptile, self.tmpt)
                    gb_t, gbb, gbd = gbc.next()
                    for br in range(3):
                        self.dma(gb_t[:, br, :], self.bgT[br * 8 + hb:br * 8 + hb + 1, qg * 512:(qg + 1) * 512].to_broadcast([128, 512]), (), (gbb,), gbd)
                    o, obb_, ods = ob.next()
                    sl = slice(qg * 512, (qg + 1) * 512)
                    S.add("dve", lambda e, o_s=o_s, gb_t=gb_t: e.tensor_tensor(out=o_s[:], in0=o_s[:], in1=gb_t[:, 1, :], op=ALU.mult), reads=(osb, gbb), writes=(osb,))
                    S.add("pool", lambda e, o_w=o_w, gb_t=gb_t: e.tensor_tensor(out=o_w[:], in0=o_w[:], in1=gb_t[:, 2, :], op=ALU.mult), reads=(owb, gbb), writes=(owb,))
                    S.add("dve", lambda e, o_s=o_s, o_w=o_w: e.tensor_tensor(out=o_s[:], in0=o_s[:], in1=o_w[:], op=ALU.add), reads=(osb, owb), writes=(osb,))
                    S.add("pool", lambda e, o_w=o_w, gb_t=gb_t, hb=hb, sl=sl: e.tensor_tensor(out=o_w[:], in0=ocmp[:, hb, sl], in1=gb_t[:, 0, :], op=ALU.mult),
                          reads=(b_oc[hb], gbb, owb), writes=(owb,))
                    S.add("dve", lambda e, o=o, o_s=o_s, o_w=o_w: e.tensor_tensor(out=o[:], in0=o_s[:], in1=o_w[:], op=ALU.add), reads=(osb, owb), writes=(obb_,))
                    self.dma(self.obT[4 + hb, :, sl], o[:], (obb_,), (), ods)
        self.barrier()
        S.sb_reset(mark)

    def ph_merge(self, l):
        self.S.phase = "merge"
        S = self.S
        ps = self.ps
        mark = S.sb_mark()
        wb = self.w["w_branch"][l]
        S.sb_reset(self.mark_noxb)
        osb_t = S.sb([128, 28, L], BF16, "obres")
        b_ob = [S.buf("ob") for _ in range(28)]
        dso = S.dsem()
        for b in range(28):
            self.dma(osb_t[:, b, :], self.obT[b], (), (b_ob[b],), dso)
        wsl = Rot(S, 2, [128, 28, 128], BF16, "wbr", sw=True)
        gt = Rot(S, 2, [128, 4, L], BF16, "gin", dma=True)
        acc = Rot(S, 2, [128, 512], F32, "macc")
        tmp = Rot(S, 2, [128, 512], F32, "mtmp")
        mo = Rot(S, 2, [128, L], BF16, "mout", dma=True)
        br_chunks = ((0, 4), (4, 8), (12, 8), (20, 8))
        for m in range(KC):
            tl, bf, ds = wsl.next()
            self.dma(tl[:], wb[:, m * 128:(m + 1) * 128].rearrange("(kc p) n -> p kc n", p=128), (), (bf,), ds, q="pool")
            g_t, gb, gds = gt.next()
            self.dma(g_t[:], self.gatesT.rearrange("(br f) t -> f br t", br=4)[m * 128:(m + 1) * 128], (), (gb,), gds)
            o, ob, ods = mo.next()
            for t in range(4):
                sl = slice(t * 512, (t + 1) * 512)
                a, ab, _ = acc.next()
                for bi, (c0, nc_) in enumerate(br_chunks):
                    bk = self.bank()

                    def mm(e, tl=tl, bk=bk, c0=c0, nc_=nc_, sl=sl):
                        for k in range(nc_):
                            ins = e.matmul(ps[:, bk, :], tl[:, c0 + k, :], osb_t[:, c0 + k, sl], start=(k == 0), stop=(k == nc_ - 1))
                        return ins
                    S.add("pe", mm, reads=[bf] + b_ob[c0:c0 + nc_], writes=(self.psb[bk],))
                    if bi == 0:
                        S.add("dve", lambda e, a=a, bk=bk, g_t=g_t, sl=sl: e.tensor_tensor(out=a[:], in0=ps[:, bk, :], in1=g_t[:, 0, sl], op=ALU.mult),
                              reads=(self.psb[bk], gb), writes=(ab,))
                    else:
                        tm, tmb, _ = tmp.next()
                        S.add("dve", lambda e, tm=tm, bk=bk, g_t=g_t, sl=sl, bi=bi: e.tensor_tensor(out=tm[:], in0=ps[:, bk, :], in1=g_t[:, bi, sl], op=ALU.mult),
                              reads=(self.psb[bk], gb), writes=(tmb,))
                        if bi < 3:
                            S.add("pool", lambda e, a=a, tm=tm: e.tensor_tensor(out=a[:], in0=a[:], in1=tm[:], op=ALU.add), reads=(ab, tmb), writes=(ab,))
                        else:
                            S.add("pool", lambda e, a=a, tm=tm, o=o, sl=sl: e.tensor_tensor(out=o[:, sl], in0=a[:], in1=tm[:], op=ALU.add), reads=(ab, tmb), writes=(ob,))
            self.dma(self.mergedT[m * 128:(m + 1) * 128, :], o[:], (ob,), (), ods)
        self.barrier()
        S.sb_reset(mark)


def host_selc():
    c = np.zeros((128, 16, 96), np.float32)
    p = np.arange(128)[:, None]
    j = np.arange(32)[None, :]
    for qt in range(16):
        cur = 2 * qt + (p >= 64)
        forced = (j == 0) | (j == cur) | (j == cur - 1)
        valid = (j <= cur)
        c[:, qt, 0:32] = valid
        c[:, qt, 32:64] = (valid - 1.0) * 1e9
        c[:, qt, 64:96] = np.where(forced, 1e9, -2e9)
    return c.reshape(128, 16 * 96)


N_CORES_USED = 8


def make_inputs(prog, inputs, xs):
    m = {}
    for nm in prog.declared:
        if nm == "x":
            m[nm] = np.ascontiguousarray(xs, dtype=np.float32)
        elif nm == "consts":
            m[nm] = host_consts()
        elif nm == "sel":
            m[nm] = host_sel()
        elif nm == "selc":
            m[nm] = host_selc()
        elif nm in ("ln_g", "ln_b"):
            m[nm] = np.ascontiguousarray(inputs[nm], dtype=np.float32).reshape(DEPTH * 3, D)
        elif nm in ("cmp_w1", "cmp_w2", "cmp_pos"):
            a = np.asarray(inputs[nm], dtype=np.float32)
            m[nm] = np.ascontiguousarray(a.reshape((DEPTH * 2,) + a.shape[2:]))
        else:
            m[nm] = np.ascontiguousarray(inputs[nm], dtype=np.float32)
    return m


def kernel(**inputs):
    R = N_CORES_USED
    spc = 8 // R
    prog = Prog(spc)
    x = np.asarray(inputs["x"], dtype=np.float32)
    in_maps = [make_inputs(prog, inputs, x[c * spc:(c + 1) * spc]) for c in range(R)]
    res = run_bass_kernel_spmd(prog.nc, in_maps, core_ids=list(range(R)))
    return np.concatenate([np.asarray(res.results[c]["out"]) for c in range(R)], axis=0).astype(np.float32)
```

```python
import math
from contextlib import ExitStack

import numpy as np
import concourse.bass as bass
import concourse.mybir as mybir
from concourse.bass_utils import run_bass_kernel_spmd

F32 = mybir.dt.float32
BF16 = mybir.dt.bfloat16
AF = mybir.ActivationFunctionType
ALU = mybir.AluOpType
AX = mybir.AxisListType

SB_BASE = 16640
SB_END = 229376

D = 4096
L = 2048
DFF = 8192
DEPTH = 2
KC = D // 128
NT = L // 128
ALPHA = (2 * DEPTH) ** 0.25
SCALE = 128 ** -0.5
EPS = 1e-5
D_IN = 28520
N_ALIBI = 28
SLOPES = [2.0 ** (-8.0 * i / N_ALIBI) for i in range(1, N_ALIBI + 1)]
A_SLOPE_IDX = (0, 1, 2, 3, 12, 13, 14, 15, 24, 25, 26, 27)
B_SLOPE_IDX = (4, 5, 6, 7, 8, 9, 10, 11)
D_SLOPE_IDX = (16, 17, 18, 19, 20, 21, 22, 23)
NEGBIG = -30000.0

C_AQ, C_AK, C_AV = 0, 1536, 3072
C_BQ, C_BKV, C_BG = 4608, 5632, 7168
C_CQ, C_CK, C_CV, C_CF = 7192, 8216, 9240, 10264
C_DQ, C_DK, C_DV, C_DIQ, C_DIK, C_DIW = 10272, 11296, 11424, 11552, 12064, 12128
C_G = 12136


class Buf:
    __slots__ = ("name", "w", "rd")

    def __init__(self, name=""):
        self.name = name
        self.w = None
        self.rd = []


class Op:
    __slots__ = ("eng", "fn", "deps", "dma", "dsem", "signal", "tok", "seq", "phase")
    _n = 0

    def __init__(self, eng, fn, dma, dsem):
        Op._n += 1
        self.seq = Op._n
        self.eng = eng
        self.fn = fn
        self.deps = []
        self.dma = dma
        self.dsem = dsem
        self.signal = dma
        self.tok = None
        self.phase = None


ENGS = ("pe", "act", "dve", "pool", "sp")


class Sched:
    def __init__(self, nc):
        self.nc = nc
        self.ops = {e: [] for e in ENGS}
        self.sb_off = SB_BASE
        self.n_dsem = 0
        self.cur_dsem = 0
        self.n_sw = 0
        self.cur_sw = 0
        self.phase = None
        self.profile_scopes = False
        self.all_bufs = []
        self.nalloc = 0

    def sb(self, shape, dtype, name=None):
        nbytes = 2 if dtype == BF16 else 4
        per_part = nbytes
        for s in shape[1:]:
            per_part *= s
        off = (self.sb_off + 63) // 64 * 64
        assert off + per_part <= SB_END, f"SBUF overflow {name} {off + per_part - SB_END}"
        self.sb_off = off + per_part
        self.nalloc += 1
        return self.nc.alloc_sbuf_tensor_at(f"{name or 't'}{self.nalloc}", list(shape), dtype, offset=off)

    def sb_mark(self):
        return (self.sb_off, self.cur_dsem, self.cur_sw)

    def sb_reset(self, mark):
        self.sb_off, self.cur_dsem, self.cur_sw = mark

    def buf(self, name=""):
        b = Buf(name)
        self.all_bufs.append(b)
        return b

    def dsem(self, sw=False):
        if sw:
            self.cur_sw += 1
            self.n_sw = max(self.n_sw, self.cur_sw)
            return ("s", self.cur_sw - 1)
        self.cur_dsem += 1
        self.n_dsem = max(self.n_dsem, self.cur_dsem)
        return ("h", self.cur_dsem - 1)

    def add(self, eng, fn, reads=(), writes=(), dma=False, dsem=None):
        if dma:
            assert (dsem[0] == "s") == (eng == "pool"), (eng, dsem)
        op = Op(eng, fn, dma, dsem)
        op.phase = self.phase
        deps = op.deps
        for b in reads:
            w = b.w
            if w is not None:
                if dma or w.eng != eng or w.dma or eng != "pe":
                    deps.append(w)
            b.rd.append(op)
        for b in writes:
            w = b.w
            if w is not None and (dma or w.dma or w.eng != eng):
                deps.append(w)
            for r in b.rd:
                if r is not op and (dma or r.dma or r.eng != eng):
                    deps.append(r)
            b.w = op
            b.rd = []
        for d in deps:
            d.signal = True
        self.ops[eng].append(op)
        return op

    def barrier(self):
        pend = set()
        for b in self.all_bufs:
            if b.w is not None:
                pend.add(b.w)
            for r in b.rd:
                pend.add(r)
        pend = list(pend)
        for e in ENGS:
            op = Op(e, None, False, None)
            for d in pend:
                if d.eng != e or d.dma:
                    op.deps.append(d)
                    d.signal = True
            self.ops[e].append(op)
        for b in self.all_bufs:
            b.w = None
            b.rd = []
        if len(self.all_bufs) > 20000:
            self.all_bufs = self.all_bufs[-5000:]

    def emit(self):
        nc = self.nc
        with ExitStack() as ctx:
            esem = {e: ctx.enter_context(nc.semaphore(f"s_{e}")) for e in ENGS}
            dsems = {("h", i): ctx.enter_context(nc.semaphore(f"d_{i}")) for i in range(self.n_dsem)}
            dsems.update({("s", i): ctx.enter_context(nc.semaphore(f"w_{i}")) for i in range(self.n_sw)})
            for e in ENGS:
                cnt = 0
                for op in self.ops[e]:
                    if op.dma or not op.signal or op.fn is None:
                        continue
                    cnt += 1
                    op.tok = (esem[e], cnt)
            dcnt = {k: 0 for k in dsems}
            alld = [op for e in ENGS for op in self.ops[e] if op.dma]
            alld.sort(key=lambda o: o.seq)
            import bisect
            dhist = {k: ([], []) for k in dsems}
            for op in alld:
                dcnt[op.dsem] += 16
                op.tok = (dsems[op.dsem], dcnt[op.dsem])
                dhist[op.dsem][0].append(op.seq)
                dhist[op.dsem][1].append(dcnt[op.dsem])
            block = ctx.enter_context(nc.Block())

            def run(e, eng):
                known = {}
                cur_ph, cur_id = None, None
                for op in self.ops[e]:
                    if self.profile_scopes and e == "pe" and op.fn is not None and getattr(op, "phase", None) != cur_ph:
                        if cur_ph is not None:
                            nc.leave_named_scope(cur_ph, cur_id, False)
                        cur_ph = op.phase
                        cur_id = nc.enter_named_scope(cur_ph, False)[0] if cur_ph is not None else None
                    need = {}
                    for d in op.deps:
                        if d.tok is None:
                            continue
                        s, v = d.tok
                        if d.dma:
                            seqs, vals = dhist[d.dsem]
                            v = vals[bisect.bisect_left(seqs, op.seq) - 1]
                        k = id(s)
                        if known.get(k, 0) < v and (k not in need or need[k][1] < v):
                            need[k] = (s, v)
                    for k, (s, v) in need.items():
                        eng.wait_ge(s, v)
                        known[k] = v
                    if op.fn is None:
                        continue
                    ins = op.fn(eng)
                    if op.dma:
                        ins.then_inc(op.tok[0], 16)
                    elif op.signal:
                        ins.then_inc(op.tok[0], 1)
                if cur_ph is not None:
                    nc.leave_named_scope(cur_ph, cur_id, False)

            @block.tensor
            def _(eng):
                run("pe", eng)

            @block.scalar
            def _(eng):
                run("act", eng)

            @block.vector
            def _(eng):
                run("dve", eng)

            @block.gpsimd
            def _(eng):
                run("pool", eng)

            @block.sync
            def _(eng):
                run("sp", eng)


class Rot:
    def __init__(self, S, n, shape, dtype, name, dma=False, sw=False):
        self.slots = []
        for i in range(n):
            t = S.sb(shape, dtype, name)
            self.slots.append((t, S.buf(name), S.dsem(sw) if (dma or sw) else None))
        self.i = 0

    def next(self):
        s = self.slots[self.i % len(self.slots)]
        self.i += 1
        return s


CO_ID = 0
CO_D0 = 128
CO_CM = 640
CO_LE = 768
CO_LT = 896
CO_ONE = 1024
CO_DC = 1152
CO_D0R = 1280
CO_CMR = 1792
CO_NLE = 2304
CO_D0P = 2432
CO_N = 2944


def host_consts():
    c = np.zeros((128, CO_N), np.float32)
    s = np.arange(128)[:, None]
    c[:, CO_ID:CO_ID + 128] = np.eye(128)
    c[:, CO_D0:CO_D0 + 512] = np.arange(512)[None, :] - s
    t = np.arange(128)[None, :]
    c[:, CO_CM:CO_CM + 128] = (t >= s)
    c[:, CO_LE:CO_LE + 128] = (t <= s)
    c[:, CO_LT:CO_LT + 128] = (t < s)
    c[:, CO_ONE:CO_ONE + 128] = 1.0
    c[:, CO_DC:CO_DC + 128] = s - 16 * t - 31
    c[:, CO_D0P:CO_D0P + 512] = np.maximum(np.arange(512)[None, :] - s, 0)
    for i in range(4):
        c[:, CO_D0R + i * 128:CO_D0R + (i + 1) * 128] = np.maximum(t - s, 0)
        c[:, CO_CMR + i * 128:CO_CMR + (i + 1) * 128] = (t >= s)
    c[:, CO_NLE:CO_NLE + 128] = ((t <= s) - 1.0) * 1e30
    return c


def host_sel():
    e = np.zeros((32, L), np.float32)
    for j in range(32):
        e[j, j * 64:(j + 1) * 64] = 1.0
    return e


class Prog:
    def __init__(self, spc, dbg=None, stages=None):
        self.spc = spc
        self.dbg = dbg or ()
        self.stages = stages
        nc = bass.Bass("TRN2", target_bir_lowering=False)
        self.nc = nc
        S = Sched(nc)
        self.S = S

        self.declared = []

        def din(name, shape):
            self.declared.append(name)
            return nc.dram_tensor(name, list(shape), F32, kind="ExternalInput").ap()

        self.x = din("x", [spc, L, D])
        self.ln_g = din("ln_g", [DEPTH * 3, D])
        self.ln_b = din("ln_b", [DEPTH * 3, D])
        wshapes = dict((("ffn1_w_gate", [DEPTH, D, DFF]), ("ffn1_w_up", [DEPTH, D, DFF]), ("ffn1_w_down", [DEPTH, DFF, D]),
                        ("w_in", [DEPTH, D, D_IN]), ("b_forget", [DEPTH, 8]), ("b_gate", [DEPTH, 4 * D]),
                        ("cmp_w1", [DEPTH * 2, 4096, 512]), ("cmp_w2", [DEPTH * 2, 512, 128]), ("cmp_pos", [DEPTH * 2, 32, 128]),
                        ("w_branch", [DEPTH, 3584, D]), ("w_out", [DEPTH, D, D]),
                        ("ffn2_w_gate", [DEPTH, D, DFF]), ("ffn2_w_up", [DEPTH, D, DFF]), ("ffn2_w_down", [DEPTH, DFF, D])))

        class LazyW(dict):
            def __missing__(s2, nm):
                s2[nm] = din(nm, wshapes[nm])
                return s2[nm]
        self.w = LazyW()
        self.consts_d = din("consts", [128, CO_N])
        self.sel_d = din("sel", [32, L])
        self.selc_d = din("selc", [128, 16 * 96])
        self.out = nc.dram_tensor("out", [spc, L, D], F32, kind="ExternalOutput").ap()

        def scr(name, shape, dt):
            kind = "ExternalOutput" if name in self.dbg else "Internal"
            return nc.dram_tensor(name, list(shape), dt, kind=kind).ap()

        self.xaT = scr("xaT", [D, L], F32)
        self.zT = scr("zT", [D, L], F32)
        self.hT = scr("hT", [DFF, L], BF16)
        self.pfm = scr("pfm", [70, 128, L], BF16)
        self.bgT = scr("bgT", [24, L], F32)
        self.cfT = scr("cfT", [8, L], F32)
        self.iwtm = scr("iwtm", [L, 8], F32)
        self.ptm = scr("ptm", [L, 3200], BF16)
        self.gatesT = scr("gatesT", [4 * D, L], BF16)
        self.obT = scr("obT", [28, 128, L], BF16)
        self.mergedT = scr("mergedT", [D, L], BF16)
        self.xbT_d = nc.dram_tensor("xbT_d", [D, L], BF16, kind=("ExternalOutput" if "xb" in self.dbg else "Internal")).ap()
        self.csd = scr("csd", [8, L], F32)
        self.bmd = scr("bmd", [2, 32, L], BF16)

        self.ps = nc.alloc_psum_tensor("ps", [128, 8, 512], F32)
        self.psb = [S.buf(f"ps{i}") for i in range(8)]
        self.psi = 0
        import os
        S.profile_scopes = bool(os.environ.get("SCOPES"))
        self.build()
        S.barrier()
        S.emit()

    def bank(self):
        i = self.psi % 8
        self.psi += 1
        return i

    def dma(self, out, in_, reads, writes, dsem, q="sp"):
        self.S.add(q, lambda e: e.dma_start(out=out, in_=in_), reads=reads, writes=writes, dma=True, dsem=dsem)

    def tr_small(self, dst, src, n, reads, wbuf):
        S = self.S
        ps = self.ps
        bk = self.bank()
        ident = self.cst[:n, CO_ID:CO_ID + n]
        S.add("pe", lambda e: e.transpose(ps[:, bk, :n], src, ident), reads=list(reads) + [self.b_cst], writes=(self.psb[bk],))
        S.add("act", lambda e: e.activation(out=dst, in_=ps[:, bk, :n], func=AF.Copy), reads=(self.psb[bk],), writes=(wbuf,))

    def load_colvec(self, dst, src1d, n, wbuf, tmp, tmpb, ds):
        self.dma(tmp[:n, :], src1d.rearrange("(c p) -> c p", p=128), (), (tmpb,), ds)
        self.tr_small(dst, tmp[:n, :], n, (tmpb,), wbuf)

    def reset_bufs(self):
        S = self.S
        for b in self.persist_bufs + self.psb:
            if b not in S.all_bufs[:64]:
                S.all_bufs.insert(0, b)

    def barrier(self):
        self.S.barrier()
        self.reset_bufs()

    def build(self):
        S = self.S
        self.cst = S.sb([128, CO_N], F32, "cst")
        self.cstb = S.sb([128, CO_N], BF16, "cstb")
        self.mark_noxb = S.sb_mark()
        self.xb = S.sb([128, KC, L], BF16, "xb")
        self.b_cst = S.buf("cst")
        self.b_xb = [S.buf(f"xb{c}") for c in range(KC)]
        self.persist_bufs = [self.b_cst] + self.b_xb
        ds = S.dsem()
        self.dma(self.cst[:], self.consts_d, (), (self.b_cst,), ds)
        S.add("dve", lambda e: e.tensor_copy(out=self.cstb[:], in_=self.cst[:]), reads=(self.b_cst,), writes=(self.b_cst,))
        self.base_mark = S.sb_mark()
        st = self.stages or {}
        layers = st.get("layers", list(range(DEPTH)))
        parts = st.get("parts", ("ffn1", "mixer", "ffn2"))
        for s in range(self.spc):
            if not st.get("skip_input"):
                self.ph_input(s, st.get("in_scale", ALPHA))
            for l in layers:
                if "ffn1" in parts:
                    self.ffn(l, 1)
                if "mixer" in parts:
                    self.mixer(l)
                    if "stop" in st:
                        break
                if "ffn2" in parts:
                    self.ffn(l, 2, final=(l == DEPTH - 1))
            if "xb" in self.dbg:
                dsx = S.dsem()
                for c in range(KC):
                    self.dma(self.xbT_d[c * 128:(c + 1) * 128, :], self.xb[:, c, :], (self.b_xb[c],), (), dsx)
            if not st:
                self.ph_output(s)

    def ph_input(self, s, in_scale=ALPHA):
        self.S.phase = "input"
        S = self.S
        nc = self.nc
        mark = S.sb_mark()
        xt = Rot(S, 8, [128, 1024], F32, "xin", dma=True)
        st_a = Rot(S, 3, [128, 512], F32, "xast", dma=True)
        ident = self.cst[:, CO_ID:CO_ID + 128]
        ps = self.ps
        for tg in range(4):
            for fq in range(4):
                tiles = []
                for j in range(4):
                    tl, bf, ds = xt.next()
                    tt = tg * 4 + j
                    self.dma(tl[:], self.x[s, tt * 128:(tt + 1) * 128, fq * 1024:(fq + 1) * 1024], (), (bf,), ds)
                    tiles.append((tl, bf))
                for ci in range(8):
                    c = fq * 8 + ci
                    bk = self.bank()

                    def tr(e, tiles=tiles, ci=ci, bk=bk):
                        for j, (tl, bf) in enumerate(tiles):
                            ins = e.transpose(ps[:, bk, j * 128:(j + 1) * 128], tl[:, ci * 128:(ci + 1) * 128], ident)
                        return ins
                    S.add("pe", tr, reads=[b for _, b in tiles] + [self.b_cst], writes=(self.psb[bk],))
                    sa, sab, sads = st_a.next()
                    S.add("act", lambda e, sa=sa, bk=bk: e.activation(out=sa[:], in_=ps[:, bk, :], func=AF.Copy, scale=float(in_scale)),
                          reads=(self.psb[bk],), writes=(sab,))
                    self.dma(self.xaT[c * 128:(c + 1) * 128, tg * 512:(tg + 1) * 512], sa[:], (sab,), (), sads, q="act")
                    S.add("dve", lambda e, c=c, tg=tg, sa=sa: e.tensor_scalar(out=self.xb[:, c, tg * 512:(tg + 1) * 512], in0=sa[:], scalar1=1.0 / float(in_scale),
                                                                         scalar2=None, op0=ALU.mult),
                          reads=(sab,), writes=(self.b_xb[c],))
        self.barrier()
        S.sb_reset(mark)

    def ph_output(self, s):
        self.S.phase = "output"
        S = self.S
        mark = S.sb_mark()
        zi = Rot(S, 6, [128, L], F32, "oin", dma=True)
        so = Rot(S, 3, [128, 512], F32, "oout", dma=True)
        ident = self.cst[:, CO_ID:CO_ID + 128]
        ps = self.ps
        for cg in range(8):
            tiles = []
            for j in range(4):
                c = cg * 4 + j
                tl, bf, ds = zi.next()
                self.dma(tl[:], self.zT[c * 128:(c + 1) * 128, :], (), (bf,), ds)
                tiles.append((tl, bf))
            for tt in range(NT):
                bk = self.bank()

                def tr(e, tiles=tiles, tt=tt, bk=bk):
                    for j, (tl, bf) in enumerate(tiles):
                        ins = e.transpose(ps[:, bk, j * 128:(j + 1) * 128], tl[:, tt * 128:(tt + 1) * 128], ident)
                    return ins
                S.add("pe", tr, reads=[b for _, b in tiles] + [self.b_cst], writes=(self.psb[bk],))
                so_t, sob, sods = so.next()
                if tt % 2 == 0:
                    S.add("dve", lambda e, so_t=so_t, bk=bk: e.tensor_copy(out=so_t[:], in_=ps[:, bk, :]), reads=(self.psb[bk],), writes=(sob,))
                else:
                    S.add("act", lambda e, so_t=so_t, bk=bk: e.activation(out=so_t[:], in_=ps[:, bk, :], func=AF.Copy), reads=(self.psb[bk],), writes=(sob,))
                self.dma(self.out[s, tt * 128:(tt + 1) * 128, cg * 512:(cg + 1) * 512], so_t[:], (sob,), (), sods)
        self.barrier()
        S.sb_reset(mark)

    def ffn(self, l, which, final=False, seq=0):
        wg = self.w[f"ffn{which}_w_gate"][l]
        wu = self.w[f"ffn{which}_w_up"][l]
        wd = self.w[f"ffn{which}_w_down"][l]
        fp = (self.stages or {}).get("ffn_parts", ("up", "down", "ln"))
        if "up" in fp:
            self.ph_up(wg, wu)
        if "down" in fp:
            self.ph_down(self.hT, wd, DFF // 128, 0.5)
        if "ln" in fp:
            self.ph_ln(l * 3 + (0 if which == 1 else 2), final)

    def ph_up(self, wg, wu):
        self.S.phase = "up"
        S = self.S
        ps = self.ps
        mark = S.sb_mark()
        CG = 128
        wsl = {"g": Rot(S, 3, [128, KC, CG], BF16, "wg", sw=True), "u": Rot(S, 3, [128, KC, CG], BF16, "wu", sw=True)}
        sg = Rot(S, 2, [128, 512], F32, "sg")
        hb = Rot(S, 3, [128, 512], BF16, "hb", dma=True)
        xb = self.xb
        import os
        for cg in range(int(os.environ.get('UPN', DFF // CG))):
            cur = {}
            for nm, w in (("g", wg), ("u", wu)):
                tl, bf, ds = wsl[nm].next()
                self.dma(tl[:], w[:, cg * CG:(cg + 1) * CG].rearrange("(kc p) n -> p kc n", p=128), (), (bf,), ds, q="pool")
                cur[nm] = (tl, bf)
            for mi in range(CG // 128):
                m = cg * (CG // 128) + mi
                for th in range(2):
                    banks = [self.bank() for _ in range(4)]
                    for j, nm in enumerate(("g", "u")):
                        tl, bf = cur[nm]
                        for tt in range(2):
                            bk = banks[j * 2 + tt]
                            t = th * 2 + tt

                            def mm(e, tl=tl, bk=bk, t=t, mi=mi):
                                for k in range(KC):
                                    ins = e.matmul(ps[:, bk, :], tl[:, k, mi * 128:(mi + 1) * 128], xb[:, k, t * 512:(t + 1) * 512],
                                                   start=(k == 0), stop=(k == KC - 1))
                                return ins
                            S.add("pe", mm, reads=[bf] + self.b_xb, writes=(self.psb[bk],))
                    for tt in range(2):
                        t = th * 2 + tt
                        sgt, sgb, _ = sg.next()
                        hbt, hbb, hds = hb.next()
                        bg, bu = banks[tt], banks[2 + tt]
                        S.add("act", lambda e, sgt=sgt, bg=bg: e.activation(out=sgt[:], in_=ps[:, bg, :], func=AF.Silu),
                              reads=(self.psb[bg],), writes=(sgb,))
                        S.add("dve", lambda e, hbt=hbt, sgt=sgt, bu=bu: e.tensor_tensor(out=hbt[:], in0=sgt[:], in1=ps[:, bu, :], op=ALU.mult),
                              reads=(sgb, self.psb[bu]), writes=(hbb,))
                        self.dma(self.hT[m * 128:(m + 1) * 128, t * 512:(t + 1) * 512], hbt[:], (hbb,), (), hds)
        self.barrier()
        S.sb_reset(mark)

    def ph_down(self, srcT, wd, kc, mul):
        self.S.phase = "down"
        S = self.S
        ps = self.ps
        mark = S.sb_mark()
        S.sb_reset(self.mark_noxb)
        TH = 1024 if kc > 32 else 2048
        npass = L // TH
        ntt = TH // 512
        hs = S.sb([128, kc, TH], BF16, "hs")
        hsb = [S.buf("hs") for _ in range(kc)]
        hds = S.dsem()
        wsl = Rot(S, 2, [128, kc, 128], BF16, "wd", sw=True)
        xat = Rot(S, 3, [128, 512], F32, "xat", dma=True)
        zst = Rot(S, 3, [128, 512], F32, "zst", dma=True)
        for p in range(npass):
            for k in range(kc):
                self.dma(hs[:, k, :], srcT[k * 128:(k + 1) * 128, p * TH:(p + 1) * TH], (), (hsb[k],), hds)
            for m in range(D // 128):
                tl, bf, ds = wsl.next()
                nsplit = 4
                ks = kc // nsplit
                for q in range(nsplit):
                    self.dma(tl[:, q * ks:(q + 1) * ks, :],
                             wd[q * ks * 128:(q + 1) * ks * 128, m * 128:(m + 1) * 128].rearrange("(kc p) n -> p kc n", p=128),
                             (), (bf,), ds, q="pool")
                for tt in range(ntt):
                    t0 = p * TH + tt * 512
                    bk = self.bank()

                    def mm(e, tl=tl, bk=bk, tt=tt):
                        for k in range(kc):
                            ins = e.matmul(ps[:, bk, :], tl[:, k, :], hs[:, k, tt * 512:(tt + 1) * 512], start=(k == 0), stop=(k == kc - 1))
                        return ins
                    S.add("pe", mm, reads=[bf] + hsb, writes=(self.psb[bk],))
                    xa, xab, xads = xat.next()
                    self.dma(xa[:], self.xaT[m * 128:(m + 1) * 128, t0:t0 + 512], (), (xab,), xads)
                    z, zb, zds = zst.next()
                    S.add("dve", lambda e, z=z, xa=xa, bk=bk: e.scalar_tensor_tensor(out=z[:], in0=ps[:, bk, :], scalar=float(mul), in1=xa[:],
                                                                                  op0=ALU.mult, op1=ALU.add),
                          reads=(self.psb[bk], xab), writes=(zb,))
                    self.dma(self.zT[m * 128:(m + 1) * 128, t0:t0 + 512], z[:], (zb,), (), zds)
        self.barrier()
        S.sb_reset(mark)

    def ph_ln(self, idx, final):
        self.S.phase = "ln"
        S = self.S
        ps = self.ps
        mark = S.sb_mark()
        H = L // 2
        zin = Rot(S, 2, [128, H], F32, "zin", dma=True)
        acc1 = S.sb([128, H], F32, "acc1")
        acc2 = S.sb([128, H], F32, "acc2")
        b_a1, b_a2 = S.buf("a1"), S.buf("a2")
        sq = Rot(S, 2, [128, H], F32, "sq")
        gb = S.sb([128, 4, KC], F32, "gb")
        b_gb = S.buf("gb")
        gds = S.dsem()
        cvt = S.sb([32, 2, 128], F32, "cvtmp")
        b_cvt = S.buf("cvt")
        self.load_colvec(gb[:, 0, :], self.ln_g[idx], KC, b_gb, cvt[:, 0, :], b_cvt, gds)
        self.load_colvec(gb[:, 1, :], self.ln_b[idx], KC, b_gb, cvt[:, 1, :], b_cvt, gds)
        S.add("dve", lambda e: e.tensor_scalar(out=gb[:, 2:4, :], in0=gb[:, 0:2, :], scalar1=float(ALPHA), scalar2=None, op0=ALU.mult),
              reads=(b_gb,), writes=(b_gb,))
        ones = self.cst[:, CO_ONE:CO_ONE + 128]
        mu = S.sb([128, H], F32, "mu")
        rs = S.sb([128, H], F32, "rs")
        b_mu, b_rs = S.buf("mu"), S.buf("rs")
        yt = Rot(S, 2, [128, H], F32, "yt")
        xo = Rot(S, 2, [128, H], F32, "xo", dma=True)
        for hf in range(2):
            hs_ = slice(hf * H, (hf + 1) * H)
            for c in range(KC):
                z, zb, zds = zin.next()
                self.dma(z[:], self.zT[c * 128:(c + 1) * 128, hs_], (), (zb,), zds)
                s_t, s_b, _ = sq.next()
                S.add("act", lambda e, s_t=s_t, z=z: e.activation(out=s_t[:], in_=z[:], func=AF.Square), reads=(zb,), writes=(s_b,))
                if c == 0:
                    S.add("pool", lambda e, z=z: e.tensor_copy(out=acc1[:], in_=z[:]), reads=(zb,), writes=(b_a1,))
                    S.add("dve", lambda e, s_t=s_t: e.tensor_copy(out=acc2[:], in_=s_t[:]), reads=(s_b,), writes=(b_a2,))
                else:
                    S.add("pool", lambda e, z=z: e.tensor_tensor(out=acc1[:], in0=acc1[:], in1=z[:], op=ALU.add), reads=(zb, b_a1), writes=(b_a1,))
                    S.add("dve", lambda e, s_t=s_t: e.tensor_tensor(out=acc2[:], in0=acc2[:], in1=s_t[:], op=ALU.add), reads=(s_b, b_a2), writes=(b_a2,))
            for tt in range(H // 512):
                b1, b2 = self.bank(), self.bank()
                sl = slice(tt * 512, (tt + 1) * 512)
                S.add("pe", lambda e, b1=b1, sl=sl: e.matmul(ps[:, b1, :], ones, acc1[:, sl], start=True, stop=True),
                      reads=(b_a1, self.b_cst), writes=(self.psb[b1],))
                S.add("pe", lambda e, b2=b2, sl=sl: e.matmul(ps[:, b2, :], ones, acc2[:, sl], start=True, stop=True),
                      reads=(b_a2, self.b_cst), writes=(self.psb[b2],))
                S.add("act", lambda e, b1=b1, sl=sl: e.activation(out=mu[:, sl], in_=ps[:, b1, :], func=AF.Copy, scale=1.0 / D),
                      reads=(self.psb[b1],), writes=(b_mu,))
                S.add("dve", lambda e, sl=sl: e.tensor_tensor(out=rs[:, sl], in0=mu[:, sl], in1=mu[:, sl], op=ALU.mult),
                      reads=(b_mu,), writes=(b_rs,))
                S.add("dve", lambda e, b2=b2, sl=sl: e.scalar_tensor_tensor(out=rs[:, sl], in0=ps[:, b2, :], scalar=1.0 / D, in1=rs[:, sl],
                                                                         op0=ALU.mult, op1=ALU.subtract),
                      reads=(self.psb[b2], b_rs), writes=(b_rs,))
                S.add("dve", lambda e, sl=sl: e.tensor_scalar(out=rs[:, sl], in0=rs[:, sl], scalar1=float(EPS), scalar2=None, op0=ALU.add),
                      reads=(b_rs,), writes=(b_rs,))
                S.add("act", lambda e, sl=sl: e.activation(out=rs[:, sl], in_=rs[:, sl], func=AF.Sqrt), reads=(b_rs,), writes=(b_rs,))
                S.add("dve", lambda e, sl=sl: e.reciprocal(out=rs[:, sl], in_=rs[:, sl]), reads=(b_rs,), writes=(b_rs,))
            for c in range(KC):
                z, zb, zds = zin.next()
                self.dma(z[:], self.zT[c * 128:(c + 1) * 128, hs_], (), (zb,), zds)
                y, yb, _ = yt.next()
                S.add("pool", lambda e, y=y, z=z: e.tensor_tensor(out=y[:], in0=z[:], in1=mu[:], op=ALU.subtract), reads=(zb, b_mu), writes=(yb,))
                S.add("dve", lambda e, y=y: e.tensor_tensor(out=y[:], in0=y[:], in1=rs[:], op=ALU.mult), reads=(yb, b_rs), writes=(yb,))
                S.add("act", lambda e, y=y, c=c, hs_=hs_: e.activation(out=self.xb[:, c, hs_], in_=y[:], func=AF.Identity, bias=gb[:, 1, c:c + 1], scale=gb[:, 0, c:c + 1]),
                      reads=(yb, b_gb), writes=(self.b_xb[c],))
                o, ob, ods = xo.next()
                if final:
                    S.add("act", lambda e, y=y, o=o, c=c: e.activation(out=o[:], in_=y[:], func=AF.Identity, bias=gb[:, 1, c:c + 1], scale=gb[:, 0, c:c + 1]),
                          reads=(yb, b_gb), writes=(ob,))
                    self.dma(self.zT[c * 128:(c + 1) * 128, hs_], o[:], (ob, zb), (), ods, q="act")
                else:
                    S.add("act", lambda e, y=y, o=o, c=c: e.activation(out=o[:], in_=y[:], func=AF.Identity, bias=gb[:, 3, c:c + 1], scale=gb[:, 2, c:c + 1]),
                          reads=(yb, b_gb), writes=(ob,))
                    self.dma(self.xaT[c * 128:(c + 1) * 128, hs_], o[:], (ob,), (), ods, q="act")
        self.barrier()
        S.sb_reset(mark)

    def mixer(self, l):
        self.ph_proj(l)
        st = self.stages or {}
        if st.get("stop") == ("proj", l):
            return
        self.S.sb_reset(self.mark_noxb)
        import os
        mx = os.environ.get("MIX", "acdb")
        if "a" in mx:
            self.mix_a()
        if "c" in mx:
            self.mix_c()
        if "d" in mx:
            self.mix_d()
        if "b" in mx:
            self.mix_b(l)
        self.S.sb_reset(self.base_mark)
        if st.get("stop") == ("attn", l):
            return
        self.ph_merge(l)
        self.ph_down(self.mergedT, self.w["w_out"][l], KC, 1.0)
        self.ph_ln(l * 3 + 1, False)

    def ph_proj(self, l):
        self.S.phase = "proj"
        S = self.S
        ps = self.ps
        xb = self.xb
        w = self.w["w_in"][l]
        mark = S.sb_mark()
        wsl = Rot(S, 2, [128, KC, 256], BF16, "wfm", sw=True)
        stg = Rot(S, 3, [128, L], BF16, "pst", dma=True)
        bgate = S.sb([128, 128], F32, "bgate")
        b_bg = S.buf("bgate")
        ds0 = S.dsem()
        cvt = S.sb([128, 128], F32, "cvtmp")
        b_cvt = S.buf("cvt")
        self.load_colvec(bgate[:], self.w["b_gate"][l], 128, b_bg, cvt, b_cvt, ds0)
        bfor = S.sb([8, 1], F32, "bfor")
        self.dma(bfor[:], self.w["b_forget"][l].rearrange("(p o) -> p o", o=1), (b_bg,), (b_bg,), ds0)
        sm = S.sb([128, L], F32, "smallst")
        b_sm = S.buf("sm")
        dsm = S.dsem()

        def fm_block(tl, bf, bi, ncol, kind, dst, arg=None):
            st_t, st_b, st_ds = (None, None, None)
            if kind in ("q", "k", "gate"):
                st_t, st_b, st_ds = stg.next()
            for t in range(4):
                bk = self.bank()

                def mm(e, tl=tl, bk=bk, t=t, bi=bi, ncol=ncol):
                    for k in range(KC):
                        ins = e.matmul(ps[:ncol, bk, :], tl[:, k, bi * 128:bi * 128 + ncol], xb[:, k, t * 512:(t + 1) * 512],
                                       start=(k == 0), stop=(k == KC - 1))
                    return ins
                S.add("pe", mm, reads=[bf] + self.b_xb, writes=(self.psb[bk],))
                sl = slice(t * 512, (t + 1) * 512)
                if kind == "q":
                    S.add("act", lambda e, st_t=st_t, bk=bk, sl=sl: e.activation(out=st_t[:, sl], in_=ps[:, bk, :], func=AF.Copy, scale=SCALE),
                          reads=(self.psb[bk],), writes=(st_b,))
                elif kind == "k":
                    S.add("dve", lambda e, st_t=st_t, bk=bk, sl=sl: e.tensor_copy(out=st_t[:, sl], in_=ps[:, bk, :]),
                          reads=(self.psb[bk],), writes=(st_b,))
                elif kind == "gate":
                    S.add("act", lambda e, st_t=st_t, bk=bk, sl=sl, arg=arg: e.activation(out=st_t[:, sl], in_=ps[:, bk, :], func=AF.Sigmoid,
                                                                                       bias=bgate[:, arg:arg + 1]),
                          reads=(self.psb[bk], b_bg), writes=(st_b,))
                elif kind == "bg":
                    S.add("act", lambda e, bk=bk, sl=sl: e.activation(out=sm[:24, sl], in_=ps[:24, bk, :], func=AF.Sigmoid),
                          reads=(self.psb[bk],), writes=(b_sm,))
                elif kind == "cf":
                    S.add("act", lambda e, bk=bk, sl=sl: e.activation(out=sm[:8, sl], in_=ps[:8, bk, :], func=AF.Identity, bias=bfor[:, 0:1]),
                          reads=(self.psb[bk], b_bg), writes=(b_sm,))
            if kind in ("q", "k", "gate"):
                self.dma(dst, st_t[:], (st_b,), (), st_ds)
            elif kind == "bg":
                self.dma(self.bgT, sm[:24, :], (b_sm,), (), dsm)
            elif kind == "cf":
                self.dma(self.cfT, sm[:8, :], (b_sm,), (), dsm)

        def load(col0, ncols, dup=False):
            tl, bf, ds = wsl.next()
            if dup:
                for h in range(2):
                    self.dma(tl[:, :, h * 64:(h + 1) * 64], w[:, col0:col0 + 64].rearrange("(kc p) n -> p kc n", p=128), (), (bf,), ds, q="pool")
            else:
                self.dma(tl[:, :, :ncols], w[:, col0:col0 + ncols].rearrange("(kc p) n -> p kc n", p=128), (), (bf,), ds, q="pool")
            return tl, bf

        def fm_range(col0, nblk, blk0, kind):
            b = 0
            while b < nblk:
                n = min(2, nblk - b)
                tl, bf = load(col0 + b * 128, n * 128)
                for i in range(n):
                    fm_block(tl, bf, i, 128, kind, self.pfm[blk0 + b + i])
                b += n

        fm_range(C_AQ, 12, 0, "q")
        fm_range(C_AK, 12, 12, "k")
        fm_range(C_BQ, 8, 24, "q")
        fm_range(C_BKV + 0, 2, 32, "k")
        fm_range(C_BKV + 256, 2, 34, "k")
        fm_range(C_BKV + 512, 2, 36, "k")
        fm_range(C_BKV + 1024, 2, 38, "k")
        fm_range(C_CQ, 8, 40, "q")
        fm_range(C_CK, 8, 48, "k")
        fm_range(C_DQ, 8, 56, "q")
        fm_range(C_DK, 1, 64, "k")
        fm_range(C_DIQ, 4, 65, "k")
        tl, bf = load(C_DIK, 64, dup=True)
        fm_block(tl, bf, 0, 128, "k", self.pfm[69])
        tl, bf = load(C_BG, 24)
        fm_block(tl, bf, 0, 24, "bg", None)
        tl, bf = load(C_CF, 8)
        fm_block(tl, bf, 0, 8, "cf", None)
        for gb in range(64):
            tl, bf = load(C_G + gb * 256, 256)
            for i in range(2):
                blk = gb * 2 + i
                fm_block(tl, bf, i, 128, "gate", self.gatesT[blk * 128:(blk + 1) * 128, :], arg=blk)
        self.barrier()
        S.sb_reset(mark)
        wtl = Rot(S, 2, [128, KC, 256], BF16, "wtm", sw=True)
        tst = Rot(S, 3, [128, 512], BF16, "tst", dma=True)
        ist = Rot(S, 2, [128, 8], F32, "ist", dma=True)
        tmjobs = [(C_AV + i * 256, 256, i * 256) for i in range(6)] + [(C_BKV + 6 * 128, 256, 1536), (C_BKV + 10 * 128, 256, 1792)] + \
                 [(C_CV + i * 256, 256, 2048 + i * 256) for i in range(4)] + [(C_DV, 128, 3072), (C_DIW, 8, -1)]
        for (col0, ncols, dcol) in tmjobs:
            tl, bf, ds = wtl.next()
            self.dma(tl[:, :, :ncols], w[:, col0:col0 + ncols].rearrange("(kc p) n -> p kc n", p=128), (), (bf,), ds, q="pool")
            for tt in range(NT):
                bk = self.bank()

                def mm(e, tl=tl, bk=bk, tt=tt, ncols=ncols):
                    for k in range(KC):
                        ins = e.matmul(ps[:, bk, :ncols], xb[:, k, tt * 128:(tt + 1) * 128], tl[:, k, :ncols], start=(k == 0), stop=(k == KC - 1))
                    return ins
                S.add("pe", mm, reads=[bf] + self.b_xb, writes=(self.psb[bk],))
                if dcol >= 0:
                    o, ob, ods = tst.next()
                    if tt % 2 == 0:
                        S.add("dve", lambda e, o=o, bk=bk, ncols=ncols: e.tensor_copy(out=o[:, :ncols], in_=ps[:, bk, :ncols]), reads=(self.psb[bk],), writes=(ob,))
                    else:
                        S.add("act", lambda e, o=o, bk=bk, ncols=ncols: e.activation(out=o[:, :ncols], in_=ps[:, bk, :ncols], func=AF.Copy), reads=(self.psb[bk],), writes=(ob,))
                    self.dma(self.ptm[tt * 128:(tt + 1) * 128, dcol:dcol + ncols], o[:, :ncols], (ob,), (), ods)
                else:
                    o, ob, ods = ist.next()
                    S.add("dve", lambda e, o=o, bk=bk: e.tensor_copy(out=o[:], in_=ps[:, bk, :8]), reads=(self.psb[bk],), writes=(ob,))
                    self.dma(self.iwtm[tt * 128:(tt + 1) * 128, :], o[:], (ob,), (), ods)
        self.barrier()
        S.sb_reset(mark)

    def attn_job(self, units, evac, ptile, tmpt):
        S = self.S
        ps = self.ps
        bo, bd = self.bank(), self.bank()
        ones = self.cstb[:, CO_ONE:CO_ONE + 128]
        LA = 3
        pend = []

        def emit_pv(item, first):
            u, p, pb, nq, c0 = item

            def pv(e, u=u, p=p, nq=nq, c0=c0, first=first):
                e.matmul(ps[:, bo, c0:c0 + nq], u["v"], p[:, :nq], start=first, stop=False, skip_group_check=True)
                return e.matmul(ps[:, bd, c0:c0 + nq], ones, p[:, :nq], start=first, stop=False, skip_group_check=True)
            S.add("pe", pv, reads=[pb, self.b_cst] + list(u["vreads"]), writes=(self.psb[bo], self.psb[bd]))

        npv = 0
        for u in units:
            nq, c0 = u["nq"], u["c0"]
            bs = self.bank()
            while bs in (bo, bd):
                bs = self.bank()
            rd = list(u["reads"])
            S.add("pe", lambda e, u=u, bs=bs, nq=nq: e.matmul(ps[:, bs, :nq], u["kT"], u["qT"], start=True, stop=True),
                  reads=rd, writes=(self.psb[bs],))
            p, pb, _ = ptile.next()
            bias = u.get("bias")
            cb = float(u.get("cb", 0.0))
            if bias is None:
                S.add("act", lambda e, p=p, bs=bs, nq=nq, cb=cb: e.activation(out=p[:, :nq], in_=ps[:, bs, :nq], func=AF.Exp, bias=cb),
                      reads=(self.psb[bs],), writes=(pb,))
            else:
                tm, tmb, _ = tmpt.next()
                if bias[0] == "alibi":
                    slope, d0 = bias[1], bias[2]
                    S.add("dve", lambda e, tm=tm, bs=bs, nq=nq, slope=slope, d0=d0: e.scalar_tensor_tensor(
                        out=tm[:, :nq], in0=d0, scalar=-float(slope), in1=ps[:, bs, :nq], op0=ALU.mult, op1=ALU.add),
                        reads=(self.psb[bs], self.b_cst), writes=(tmb,))
                else:
                    csT, csbc, brd = bias[1], bias[2], bias[3]
                    S.add("dve", lambda e, tm=tm, bs=bs, nq=nq, csT=csT, csbc=csbc: e.scalar_tensor_tensor(
                        out=tm[:, :nq], in0=ps[:, bs, :nq], scalar=csT, in1=csbc, op0=ALU.add, op1=ALU.subtract),
                        reads=[self.psb[bs]] + list(brd), writes=(tmb,))
                S.add("act", lambda e, p=p, tm=tm, nq=nq, cb=cb: e.activation(out=p[:, :nq], in_=tm[:, :nq], func=AF.Exp, bias=cb),
                      reads=(tmb,), writes=(pb,))
            for (mo, mw, map_) in u.get("masks", ()):
                S.add("pool", lambda e, p=p, mo=mo, mw=mw, map_=map_: e.tensor_tensor(out=p[:, mo:mo + mw], in0=p[:, mo:mo + mw], in1=map_, op=ALU.mult),
                      reads=(pb, self.b_cst), writes=(pb,))
            full = u.get("full")
            if full is not None:
                fap, fb = full
                S.add("pool", lambda e, p=p, nq=nq, fap=fap: e.tensor_tensor(out=p[:, :nq], in0=p[:, :nq], in1=fap, op=ALU.mult),
                      reads=(pb, fb), writes=(pb,))
            pend.append((u, p, pb, nq, c0))
            if len(pend) > LA:
                emit_pv(pend.pop(0), npv == 0)
                npv += 1
        while pend:
            emit_pv(pend.pop(0), npv == 0)
            npv += 1
        evac(bo, bd)

    def evac_norm(self, dst, dstb, c0=0, n=512):
        S = self.S
        ps = self.ps

        def f(bo, bd):
            r, rb, _ = self.rden.next()
            S.add("dve", lambda e, r=r, bd=bd: e.reciprocal(out=r[:, :n], in_=ps[:, bd, :n]), reads=(self.psb[bd],), writes=(rb,))
            S.add("dve", lambda e, r=r, bo=bo: e.tensor_tensor(out=dst[:, c0:c0 + n], in0=ps[:, bo, :n], in1=r[:, :n], op=ALU.mult),
                  reads=(self.psb[bo], rb), writes=(dstb,))
        return f

    def causal_units(self, qg, kT, qT, vt, bias_fn, reads, vreads, n_prev=None, full_fn=None):
        CM = self.cstb[:, CO_CM:CO_CM + 128]
        LT = self.cstb[:, CO_LT:CO_LT + 128]
        LE = self.cstb[:, CO_LE:CO_LE + 128]
        units = []
        k_lo = 0 if n_prev is None else max(0, 4 * qg - n_prev)
        for kt in range(k_lo, 4 * qg + 4):
            q_lo = max(kt, 4 * qg)
            q_hi = 4 * qg + 3 if n_prev is None else min(kt + n_prev, 4 * qg + 3)
            c0 = (q_lo - 4 * qg) * 128
            nq = (q_hi - q_lo + 1) * 128
            r0 = q_lo - kt
            masks = []
            if q_lo == kt:
                masks.append((0, 128, CM))
            if n_prev is not None and q_hi == kt + n_prev:
                masks.append((nq - 128, 128, LE if n_prev == 1 else LT))
            u = dict(kT=kT[:, kt * 128:(kt + 1) * 128], qT=qT[:, qg * 512 + c0: qg * 512 + c0 + nq], v=vt[:, kt, :],
                     c0=c0, nq=nq, masks=masks, reads=reads, vreads=vreads)
            bias_fn(u, kt, qg, c0, nq, r0)
            if full_fn is not None:
                u["full"] = full_fn(kt, c0, nq)
            units.append(u)
        return units

    def alibi_bias(self, slope):
        d0t = self.cst[:, CO_D0:CO_D0 + 512]
        d0p = self.cst[:, CO_D0P:CO_D0P + 512]

        def f(u, kt, qg, c0, nq, r0):
            u["bias"] = ("alibi", slope, (d0p if r0 == 0 else d0t)[:, :nq])
            u["cb"] = -slope * 128.0 * r0
        return f

    def load_qkv(self, qblk, kblk, vcol, rot):
        q, qb, qds = rot["q"].next()
        k, kb, kds = rot["k"].next()
        v, vb, vds = rot["v"].next()
        self.dma(q[:], self.pfm[qblk], (), (qb,), qds)
        self.dma(k[:], self.pfm[kblk], (), (kb,), kds)
        self.dma(v[:], self.ptm[:, vcol:vcol + 128].rearrange("(kt p) d -> p kt d", p=128), (), (vb,), vds)
        return (q, qb), (k, kb), (v, vb)

    def attn_common(self):
        S = self.S
        self.ptile = Rot(S, 6, [128, 512], BF16, "pt")
        self.tmpt = Rot(S, 4, [128, 512], F32, "tmp")
        self.rden = Rot(S, 2, [128, 512], F32, "rden")

    def mix_a(self):
        self.S.phase = "mixa"
        S = self.S
        ps = self.ps
        mark = S.sb_mark()
        self.attn_common()
        rot = {"q": Rot(S, 3, [128, L], BF16, "aq", dma=True), "k": Rot(S, 3, [128, L], BF16, "ak", dma=True),
               "v": Rot(S, 3, [128, NT, 128], BF16, "av", dma=True)}
        OA = Rot(S, 2, [128, L], F32, "OA")
        DA = Rot(S, 2, [128, L], F32, "DA")
        ob = Rot(S, 2, [128, L], BF16, "oab", dma=True)
        CMr = self.cstb[:, CO_CMR:CO_CMR + 512]
        D0r = self.cst[:, CO_D0R:CO_D0R + 512]
        ones = self.cstb[:, CO_ONE:CO_ONE + 128]
        for h in range(4):
            oa, oab, _ = OA.next()
            da, dab, _ = DA.next()
            slope = SLOPES[A_SLOPE_IDX[h]]
            (q, qb), (k, kb), (v, vb) = self.load_qkv(h, 12 + h, h * 128, rot)
            for qg in range(4):
                units = self.causal_units(qg, k, q, v, self.alibi_bias(slope), (qb, kb), (vb,), n_prev=1)

                def ev(bo, bd, qg=qg, oa=oa, da=da, oab=oab, dab=dab):
                    sl = slice(qg * 512, (qg + 1) * 512)
                    S.add("dve", lambda e: e.tensor_copy(out=oa[:, sl], in_=ps[:, bo, :]), reads=(self.psb[bo],), writes=(oab,))
                    S.add("act", lambda e: e.activation(out=da[:, sl], in_=ps[:, bd, :], func=AF.Copy), reads=(self.psb[bd],), writes=(dab,))
                self.attn_job(units, ev, self.ptile, self.tmpt)
            hd = 4 + h
            slope = SLOPES[A_SLOPE_IDX[hd]] * 4.0
            q, qb, qds = rot["q"].next()
            k, kb, kds = rot["k"].next()
            v, vb, vds = rot["v"].next()
            self.dma(q[:], self.pfm[hd], (), (qb,), qds)
            self.dma(k[:], self.pfm[12 + hd], (), (kb,), kds)
            for r in range(4):
                self.dma(v[:, r * 4:(r + 1) * 4, :],
                         self.ptm[:, hd * 128:(hd + 1) * 128].rearrange("(kt j r) d -> r j kt d", r=4, j=128)[r], (), (vb,), vds)
            for r in range(4):
                qs = q[:, r:L:4]
                ks = k[:, r:L:4]
                units = self.causal_units(0, ks, qs, v[:, r * 4:(r + 1) * 4, :], self.alibi_bias(slope), (qb, kb), (vb,), n_prev=1)

                def ev(bo, bd, r=r, oa=oa, da=da, oab=oab, dab=dab):
                    S.add("dve", lambda e: e.tensor_tensor(out=oa[:, r:L:4], in0=oa[:, r:L:4], in1=ps[:, bo, :], op=ALU.add),
                          reads=(self.psb[bo], oab), writes=(oab,))
                    S.add("dve", lambda e: e.tensor_tensor(out=da[:, r:L:4], in0=da[:, r:L:4], in1=ps[:, bd, :], op=ALU.add),
                          reads=(self.psb[bd], dab), writes=(dab,))
                self.attn_job(units, ev, self.ptile, self.tmpt)
            hd = 8 + h
            slope = SLOPES[A_SLOPE_IDX[hd]] * 16.0
            q, qb, qds = rot["q"].next()
            k, kb, kds = rot["k"].next()
            v, vb, vds = rot["v"].next()
            self.dma(q[:], self.pfm[hd], (), (qb,), qds)
            self.dma(k[:], self.pfm[12 + hd], (), (kb,), kds)
            for r4 in range(4):
                self.dma(v[:, r4 * 4:(r4 + 1) * 4, :],
                         self.ptm[:, hd * 128:(hd + 1) * 128].rearrange("(j r) d -> j r d", r=16)[:, r4 * 4:(r4 + 1) * 4, :], (), (vb,), vds)
            for r4 in range(4):
                bo, bd, bs = self.bank(), self.bank(), self.bank()

                def qk(e, r4=r4, bs=bs, q=q, k=k):
                    for i in range(4):
                        r = r4 * 4 + i
                        ins = e.matmul(ps[:, bs, i * 128:(i + 1) * 128], k[:, r:L:16], q[:, r:L:16], start=True, stop=True)
                    return ins
                S.add("pe", qk, reads=(qb, kb), writes=(self.psb[bs],))
                tm, tmb, _ = self.tmpt.next()
                p, pb, _ = self.ptile.next()
                S.add("dve", lambda e, tm=tm, bs=bs, slope=slope: e.scalar_tensor_tensor(out=tm[:], in0=D0r, scalar=-float(slope), in1=ps[:, bs, :],
                                                                                       op0=ALU.mult, op1=ALU.add),
                      reads=(self.psb[bs], self.b_cst), writes=(tmb,))
                S.add("act", lambda e, p=p, tm=tm: e.activation(out=p[:], in_=tm[:], func=AF.Exp), reads=(tmb,), writes=(pb,))
                S.add("pool", lambda e, p=p: e.tensor_tensor(out=p[:], in0=p[:], in1=CMr, op=ALU.mult), reads=(pb, self.b_cst), writes=(pb,))

                def pv(e, r4=r4, p=p, bo=bo, bd=bd, v=v):
                    for i in range(4):
                        e.matmul(ps[:, bo, i * 128:(i + 1) * 128], v[:, r4 * 4 + i, :], p[:, i * 128:(i + 1) * 128], start=(i == 0), stop=False, skip_group_check=True)
                    for i in range(4):
                        ins = e.matmul(ps[:, bd, i * 128:(i + 1) * 128], ones, p[:, i * 128:(i + 1) * 128], start=(i == 0), stop=False, skip_group_check=True)
                    return ins
                S.add("pe", pv, reads=(pb, vb, self.b_cst), writes=(self.psb[bo], self.psb[bd]))
                oav = oa[:].rearrange("p (j r) -> p r j", r=16)[:, r4 * 4:(r4 + 1) * 4, :]
                dav = da[:].rearrange("p (j r) -> p r j", r=16)[:, r4 * 4:(r4 + 1) * 4, :]
                S.add("dve", lambda e, oav=oav, bo=bo: e.tensor_tensor(out=oav, in0=oav, in1=ps[:, bo, :].rearrange("p (i j) -> p i j", i=4), op=ALU.add),
                      reads=(self.psb[bo], oab), writes=(oab,))
                S.add("dve", lambda e, dav=dav, bd=bd: e.tensor_tensor(out=dav, in0=dav, in1=ps[:, bd, :].rearrange("p (i j) -> p i j", i=4), op=ALU.add),
                      reads=(self.psb[bd], dab), writes=(dab,))
            o, obb, ods = ob.next()
            S.add("dve", lambda e, da=da: e.reciprocal(out=da[:], in_=da[:]), reads=(dab,), writes=(dab,))
            S.add("dve", lambda e, o=o, oa=oa, da=da: e.tensor_tensor(out=o[:], in0=oa[:], in1=da[:], op=ALU.mult), reads=(oab, dab), writes=(obb,))
            self.dma(self.obT[h], o[:], (obb,), (), ods)
        self.barrier()
        S.sb_reset(mark)

    def mix_c(self):
        self.S.phase = "mixc"
        S = self.S
        ps = self.ps
        mark = S.sb_mark()
        self.attn_common()
        rot = {"q": Rot(S, 2, [128, L], BF16, "cq", dma=True), "k": Rot(S, 2, [128, L], BF16, "ck", dma=True),
               "v": Rot(S, 2, [128, NT, 128], BF16, "cv", dma=True)}
        ob = Rot(S, 2, [128, L], BF16, "ocb", dma=True)
        cf = S.sb([8, L], F32, "cf")
        cs = S.sb([8, L], F32, "cs")
        onesr = S.sb([8, L], F32, "onesr")
        b_cf = S.buf("cf")
        ds = S.dsem()
        self.dma(cf[:], self.cfT, (), (b_cf,), ds)
        S.add("act", lambda e: e.activation(out=cf[:], in_=cf[:], func=AF.Exp, scale=-1.0), reads=(b_cf,), writes=(b_cf,))
        S.add("act", lambda e: e.activation(out=cf[:], in_=cf[:], func=AF.Ln, bias=1.0), reads=(b_cf,), writes=(b_cf,))
        S.add("dve", lambda e: e.memset(onesr[:], 1.0), reads=(), writes=(b_cf,))
        S.add("dve", lambda e: e.tensor_tensor_scan(out=cs[:], data0=onesr[:], data1=cf[:], initial=0.0, op0=ALU.mult, op1=ALU.add),
              reads=(b_cf,), writes=(b_cf,))
        b_csd = S.buf("csd")
        self.dma(self.csd, cs[:], (b_cf,), (b_csd,), ds)
        csT = S.sb([128, NT, 8], F32, "csT")
        b_csT = S.buf("csT")
        for tt in range(NT):
            self.tr_small(csT[:, tt, :], cs[:, tt * 128:(tt + 1) * 128], 8, (b_cf,), b_csT)
        cbc = Rot(S, 2, [128, L], F32, "cbc", dma=True)
        for h in range(8):
            (q, qb), (k, kb), (v, vb) = self.load_qkv(40 + h, 48 + h, 2048 + h * 128, rot)
            cb_t, cb_b, cb_ds = cbc.next()
            self.dma(cb_t[:], self.csd[h:h + 1, :].to_broadcast([128, L]), (b_csd,), (cb_b,), cb_ds)
            o, obb, ods = ob.next()

            def bias_fn(u, kt, qg, c0, nq, r0, h=h, cb_t=cb_t, cb_b=cb_b):
                u["bias"] = ("fox", csT[:, kt, h:h + 1], cb_t[:, qg * 512 + c0: qg * 512 + c0 + nq], (b_csT, cb_b))
            for qg in range(4):
                units = self.causal_units(qg, k, q, v, bias_fn, (qb, kb), (vb,))
                self.attn_job(units, self.evac_norm(o, obb, qg * 512), self.ptile, self.tmpt)
            self.dma(self.obT[12 + h], o[:], (obb,), (), ods)
        self.barrier()
        S.sb_reset(mark)

    def mix_d(self):
        self.S.phase = "mixd"
        S = self.S
        ps = self.ps
        mark = S.sb_mark()
        self.attn_common()
        ident = self.cst[:, CO_ID:CO_ID + 128]
        NLE = self.cst[:, CO_NLE:CO_NLE + 128]
        qs = S.sb([128, 8, L], BF16, "dq")
        kk = S.sb([128, L], BF16, "dk")
        vv = S.sb([128, NT, 128], BF16, "dv")
        iq = S.sb([128, 4, L], BF16, "diq")
        ik = S.sb([128, L], BF16, "dik")
        iw = S.sb([128, NT, 8], F32, "diw")
        b_in = S.buf("din")
        ds = S.dsem()
        for h in range(8):
            self.dma(qs[:, h, :], self.pfm[56 + h], (), (b_in,), ds)
        self.dma(kk[:], self.pfm[64], (), (b_in,), ds)
        self.dma(vv[:], self.ptm[:, 3072:3200].rearrange("(kt p) d -> p kt d", p=128), (), (b_in,), ds)
        for b in range(4):
            self.dma(iq[:, b, :], self.pfm[65 + b], (), (b_in,), ds)
        self.dma(ik[:], self.pfm[69], (), (b_in,), ds)
        self.dma(iw[:], self.iwtm.rearrange("(tt p) h -> p tt h", p=128), (), (b_in,), ds)
        score = Rot(S, 2, [128, L], F32, "score")
        work = S.sb([128, L], F32, "work")
        b_work = S.buf("work")
        m8 = S.sb([128, 8], F32, "m8")
        msk = Rot(S, 2, [128, L], F32, "msk")
        relu = Rot(S, 3, [128, 512], F32, "relu")
        maskT = Rot(S, 2, [128, NT, 512], BF16, "maskT")
        ob = Rot(S, 8, [128, 512], BF16, "odb", dma=True)
        for qg in range(4):
            mT, mTb, _ = maskT.next()
            for qi in range(4):
                qt = qg * 4 + qi
                nk = (qt + 1) * 128
                sc, scb, _ = score.next()
                for kg in range((nk + 511) // 512):
                    n = min(512, nk - kg * 512)
                    for h in range(8):
                        bk = self.bank()
                        pr = 64 * (h % 2)
                        S.add("pe", lambda e, bk=bk, h=h, pr=pr, qt=qt, kg=kg, n=n: e.matmul(
                            ps[:, bk, :n], iq[pr:pr + 64, h // 2, qt * 128:(qt + 1) * 128], ik[pr:pr + 64, kg * 512:kg * 512 + n], start=True, stop=True),
                            reads=(b_in,), writes=(self.psb[bk],))
                        r, rb, _ = relu.next()
                        S.add("act", lambda e, r=r, bk=bk, n=n: e.activation(out=r[:, :n], in_=ps[:, bk, :n], func=AF.Relu), reads=(self.psb[bk],), writes=(rb,))
                        sl = slice(kg * 512, kg * 512 + n)
                        if h == 0:
                            S.add("dve", lambda e, sc=sc, r=r, n=n, sl=sl, qt=qt: e.tensor_scalar(out=sc[:, sl], in0=r[:, :n], scalar1=iw[:, qt, 0:1], scalar2=None, op0=ALU.mult),
                                  reads=(rb, b_in), writes=(scb,))
                        else:
                            S.add("dve", lambda e, sc=sc, r=r, n=n, sl=sl, qt=qt, h=h: e.scalar_tensor_tensor(out=sc[:, sl], in0=r[:, :n], scalar=iw[:, qt, h:h + 1], in1=sc[:, sl],
                                                                                                        op0=ALU.mult, op1=ALU.add),
                                  reads=(rb, b_in, scb), writes=(scb,))
                dsl = slice(qt * 128, (qt + 1) * 128)
                S.add("dve", lambda e, sc=sc, dsl=dsl: e.tensor_tensor(out=sc[:, dsl], in0=sc[:, dsl], in1=NLE, op=ALU.add), reads=(scb, self.b_cst), writes=(scb,))
                mk, mkb, _ = msk.next()
                if qt < 2:
                    S.add("dve", lambda e, mk=mk, sc=sc, nk=nk: e.tensor_scalar(out=mk[:, :nk], in0=sc[:, :nk], scalar1=-1e29, scalar2=None, op0=ALU.is_ge),
                          reads=(scb,), writes=(mkb,))
                else:
                    S.add("dve", lambda e, sc=sc, nk=nk: e.tensor_copy(out=work[:, :nk], in_=sc[:, :nk]), reads=(scb,), writes=(b_work,))
                    for rnd in range(32):
                        S.add("dve", lambda e, nk=nk: e.max(out=m8[:], in_=work[:, :nk]), reads=(b_work,), writes=(b_work,))
                        if rnd < 31:
                            S.add("dve", lambda e, nk=nk: e.match_replace(out=work[:, :nk], in_to_replace=m8[:], in_values=work[:, :nk], imm_value=-3e38),
                                  reads=(b_work,), writes=(b_work,))
                    S.add("dve", lambda e, mk=mk, sc=sc, nk=nk: e.tensor_scalar(out=mk[:, :nk], in0=sc[:, :nk], scalar1=m8[:, 7:8], scalar2=None, op0=ALU.is_ge),
                          reads=(scb, b_work), writes=(mkb,))
                for k4 in range((qt + 4) // 4):
                    nkt = min(4, qt + 1 - k4 * 4)
                    bk = self.bank()

                    def tr(e, mk=mk, bk=bk, k4=k4, nkt=nkt):
                        for j in range(nkt):
                            kt = k4 * 4 + j
                            ins = e.transpose(ps[:, bk, j * 128:(j + 1) * 128], mk[:, kt * 128:(kt + 1) * 128], ident)
                        return ins
                    S.add("pe", tr, reads=(mkb, self.b_cst), writes=(self.psb[bk],))
                    S.add("act", lambda e, mT=mT, bk=bk, k4=k4, nkt=nkt, qi=qi: e.activation(
                        out=mT[:, k4 * 4:k4 * 4 + nkt, qi * 128:(qi + 1) * 128], in_=ps[:, bk, :nkt * 128].rearrange("p (j q) -> p j q", j=nkt), func=AF.Copy),
                        reads=(self.psb[bk],), writes=(mTb,))
            for h in range(8):
                slope = SLOPES[D_SLOPE_IDX[h]]
                o, obb, ods = ob.next()
                units = self.causal_units(qg, kk, qs[:, h, :], vv, self.alibi_bias(slope), (b_in,), (b_in,),
                                          full_fn=lambda kt, c0, nq, mT=mT, mTb=mTb: (mT[:, kt, c0:c0 + nq], mTb))
                self.attn_job(units, self.evac_norm(o, obb, 0), self.ptile, self.tmpt)
                self.dma(self.obT[20 + h, :, qg * 512:(qg + 1) * 512], o[:], (obb,), (), ods)
        self.barrier()
        S.sb_reset(mark)

    def mix_b(self, l):
        self.S.phase = "mixb"
        S = self.S
        ps = self.ps
        mark = S.sb_mark()
        self.attn_common()
        ident = self.cst[:, CO_ID:CO_ID + 128]
        kcT = S.sb([128, 2, 128], BF16, "kcT")
        vc = S.sb([128, 2, 128], BF16, "vc")
        b_kc, b_vc = S.buf("kc"), S.buf("vc")
        m2 = S.sb_mark()
        w1s = Rot(S, 2, [128, 32, 512], BF16, "w1s", sw=True)
        w2s = Rot(S, 2, [128, 4, 128], BF16, "w2s", sw=True)
        posT = Rot(S, 2, [128, 32], F32, "posT")
        posraw = Rot(S, 2, [32, 128], F32, "posraw", dma=True)
        src_t = Rot(S, 2, [128, L], BF16, "cmpsrc", dma=True)
        kvp = Rot(S, 2, [128, 32, 128], BF16, "kvp")
        hid = Rot(S, 2, [128, 4, 128], BF16, "hid")
        gt = Rot(S, 4, [128, 128], F32, "gelu")
        for j in range(2):
            w1, w1b, w1d = w1s.next()
            for q4 in range(4):
                self.dma(w1[:, q4 * 8:(q4 + 1) * 8, :], self.w["cmp_w1"][l * 2 + j, q4 * 1024:(q4 + 1) * 1024, :].rearrange("(p d) h -> d p h", d=128),
                         (), (w1b,), w1d, q="pool")
            w2, w2b, w2d = w2s.next()
            self.dma(w2[:], self.w["cmp_w2"][l * 2 + j].rearrange("(hc p) d -> p hc d", p=128), (), (w2b,), w2d, q="pool")
            pT, pTb, pTd = posT.next()
            pr_t, pr_b, pr_d = posraw.next()
            self.dma(pr_t[:], self.w["cmp_pos"][l * 2 + j], (), (pr_b,), pr_d)
            self.tr_small(pT[:], pr_t[:], 32, (pr_b,), pTb)
            for g in range(2):
                sr, srb, srd = src_t.next()
                self.dma(sr[:], self.pfm[32 + j * 2 + g], (), (srb,), srd)
                kv, kvb, _ = kvp.next()
                for p in range(32):
                    S.add("dve" if p % 2 == 0 else "pool",
                          lambda e, kv=kv, sr=sr, pT=pT, p=p: e.tensor_scalar(out=kv[:, p, 0:127], in0=sr[:, p:p + 16 * 126 + 1:16], scalar1=pT[:, p:p + 1], scalar2=None, op0=ALU.add),
                          reads=(srb, pTb), writes=(kvb,))
                hd, hdb, _ = hid.next()
                for hc in range(4):
                    bk = self.bank()

                    def mm(e, w1=w1, kv=kv, bk=bk, hc=hc):
                        for p in range(32):
                            ins = e.matmul(ps[:, bk, :127], w1[:, p, hc * 128:(hc + 1) * 128], kv[:, p, 0:127], start=(p == 0), stop=(p == 31))
                        return ins
                    S.add("pe", mm, reads=(w1b, kvb), writes=(self.psb[bk],))
                    u, ub, _ = gt.next()
                    S.add("act", lambda e, u=u, bk=bk: e.activation(out=u[:, :127], in_=ps[:, bk, :127], func=AF.Square), reads=(self.psb[bk],), writes=(ub,))
                    S.add("dve", lambda e, u=u: e.tensor_scalar(out=u[:, :127], in0=u[:, :127], scalar1=0.044715, scalar2=1.0, op0=ALU.mult, op1=ALU.add), reads=(ub,), writes=(ub,))
                    S.add("dve", lambda e, u=u, bk=bk: e.tensor_tensor(out=u[:, :127], in0=u[:, :127], in1=ps[:, bk, :127], op=ALU.mult), reads=(ub, self.psb[bk]), writes=(ub,))
                    S.add("act", lambda e, u=u: e.activation(out=u[:, :127], in_=u[:, :127], func=AF.Sigmoid, scale=1.5957691216057308), reads=(ub,), writes=(ub,))
                    S.add("dve", lambda e, u=u, hd=hd, hc=hc, bk=bk: e.tensor_tensor(out=hd[:, hc, :127], in0=u[:, :127], in1=ps[:, bk, :127], op=ALU.mult),
                          reads=(ub, self.psb[bk]), writes=(hdb,))
                bk = self.bank()
                if j == 0:
                    def mm2(e, w2=w2, hd=hd, bk=bk):
                        for hc in range(4):
                            ins = e.matmul(ps[:, bk, :127], w2[:, hc, :], hd[:, hc, :127], start=(hc == 0), stop=(hc == 3))
                        return ins
                    S.add("pe", mm2, reads=(w2b, hdb), writes=(self.psb[bk],))
                    S.add("dve", lambda e, g=g, bk=bk: e.tensor_copy(out=kcT[:, g, :127], in_=ps[:, bk, :127]), reads=(self.psb[bk],), writes=(b_kc,))
                else:
                    def mm2(e, w2=w2, hd=hd, bk=bk):
                        for hc in range(4):
                            ins = e.matmul(ps[:127, bk, :128], hd[:, hc, :127], w2[:, hc, :], start=(hc == 0), stop=(hc == 3))
                        return ins
                    S.add("pe", mm2, reads=(w2b, hdb), writes=(self.psb[bk],))
                    S.add("dve", lambda e, g=g, bk=bk: e.tensor_copy(out=vc[:127, g, :], in_=ps[:127, bk, :128]), reads=(self.psb[bk],), writes=(b_vc,))
        self.barrier()
        S.all_bufs.extend([b_kc, b_vc])
        S.sb_reset(m2)
        qall = S.sb([128, 8, L], BF16, "bq")
        b_q = S.buf("bq")
        dq = S.dsem()
        for hb in range(8):
            self.dma(qall[:, hb, :], self.pfm[24 + hb], (), (b_q,), dq)
        selc = S.sb([128, 16, 96], F32, "selc")
        b_selc = S.buf("selc")
        self.dma(selc[:], self.selc_d.rearrange("p (a b) -> p a b", b=96), (), (b_selc,), dq)
        selE = S.sb([32, L], BF16, "selE")
        b_selE = S.buf("selE")
        self.dma(selE[:], self.sel_d, (), (b_selE,), S.dsem(True), q="pool")
        ocmp = S.sb([128, 8, L], BF16, "ocmp")
        b_oc = [S.buf("oc") for _ in range(8)]
        bmT = S.sb([32, 2, L], BF16, "bmT")
        b_bm = [S.buf("bm0"), S.buf("bm1")]
        dcl = Rot(S, 2, [128, 128], F32, "dcl")
        ngm = Rot(S, 2, [128, 128], F32, "ngm")
        et = Rot(S, 8, [128, 128], F32, "et")
        t1 = Rot(S, 8, [128, 128], F32, "t1")
        den = Rot(S, 8, [128, 2], F32, "den")
        imp = Rot(S, 2, [128, 132], F32, "imp")
        psl = Rot(S, 2, [128, 64], F32, "psl")
        m8 = Rot(S, 2, [128, 8], F32, "m8b")
        pT = Rot(S, 8, [128, 128], BF16, "pT")
        DC = self.cst[:, CO_DC:CO_DC + 128]
        for qt in range(NT):
            dc, dcb, _ = dcl.next()
            ng, ngb, _ = ngm.next()
            S.add("dve", lambda e, dc=dc, qt=qt: e.tensor_scalar(out=dc[:], in0=DC, scalar1=float(128 * qt), scalar2=0.0, op0=ALU.add, op1=ALU.max),
                  reads=(self.b_cst,), writes=(dcb,))
            S.add("dve", lambda e, ng=ng, qt=qt: e.tensor_scalar(out=ng[:], in0=DC, scalar1=float(-128 * qt), scalar2=NEGBIG, op0=ALU.is_lt, op1=ALU.mult),
                  reads=(self.b_cst,), writes=(ngb,))
            qsl = slice(qt * 128, (qt + 1) * 128)
            for g in range(2):
                im, imb, _ = imp.next()
                S.add("pool", lambda e, im=im: e.memset(im[:], 0.0), reads=(), writes=(imb,))
                bo = self.bank()
                bks, ets, ebs = [], [], []
                for r in range(4):
                    hb = g * 4 + r
                    bk = self.bank()
                    while bk == bo or bk in bks:
                        bk = self.bank()
                    bks.append(bk)
                    S.add("pe", lambda e, bk=bk, hb=hb, g=g, qsl=qsl: e.matmul(ps[:, bk, :127], qall[:, hb, qsl], kcT[:, g, :127], start=True, stop=True),
                          reads=(b_q, b_kc), writes=(self.psb[bk],))
                for r in range(4):
                    hb = g * 4 + r
                    slope = SLOPES[B_SLOPE_IDX[hb]]
                    bk = bks[r]
                    tt, ttb, _ = t1.next()
                    S.add("dve", lambda e, tt=tt, dc=dc, ng=ng, slope=slope: e.scalar_tensor_tensor(out=tt[:, :127], in0=dc[:, :127], scalar=-float(slope), in1=ng[:, :127],
                                                                                                 op0=ALU.mult, op1=ALU.add),
                          reads=(dcb, ngb), writes=(ttb,))
                    S.add("dve", lambda e, tt=tt, bk=bk: e.tensor_tensor(out=tt[:, :127], in0=tt[:, :127], in1=ps[:, bk, :127], op=ALU.add),
                          reads=(ttb, self.psb[bk]), writes=(ttb,))
                    e_t, eb, _ = et.next()
                    dn, dnb, _ = den.next()
                    S.add("pool", lambda e, e_t=e_t: e.memset(e_t[:, 127:128], 0.0), reads=(), writes=(eb,))
                    S.add("pool", lambda e, dn=dn: e.memset(dn[:], 0.0), reads=(), writes=(dnb,))
                    S.add("act", lambda e, e_t=e_t, tt=tt, dn=dn: e.activation(out=e_t[:, :127], in_=tt[:, :127], func=AF.Exp, accum_out=dn[:, 0:1]),
                          reads=(ttb,), writes=(eb, dnb))
                    S.add("dve", lambda e, dn=dn: e.tensor_scalar(out=dn[:, 1:2], in0=dn[:, 0:1], scalar1=1e-30, scalar2=None, op0=ALU.max), reads=(dnb,), writes=(dnb,))
                    S.add("dve", lambda e, dn=dn: e.reciprocal(out=dn[:, 1:2], in_=dn[:, 1:2]), reads=(dnb,), writes=(dnb,))
                    S.add("dve", lambda e, e_t=e_t, dn=dn: e.tensor_scalar(out=e_t[:], in0=e_t[:], scalar1=dn[:, 1:2], scalar2=None, op0=ALU.mult), reads=(eb, dnb), writes=(eb,))
                    S.add("pool", lambda e, im=im, e_t=e_t: e.tensor_tensor(out=im[:, 1:128], in0=im[:, 1:128], in1=e_t[:, 0:127], op=ALU.add), reads=(eb, imb), writes=(imb,))
                    ets.append(e_t)
                    ebs.append(eb)
                for r in range(4):
                    bt = bks[r]
                    S.add("pe", lambda e, bt=bt, e_t=ets[r]: e.transpose(ps[:, bt, :128], e_t[:], ident), reads=(ebs[r], self.b_cst), writes=(self.psb[bt],))
                for r in range(4):
                    bt = bks[r]
                    pt_, ptb, _ = pT.next()
                    S.add("act", lambda e, pt_=pt_, bt=bt: e.activation(out=pt_[:], in_=ps[:, bt, :128], func=AF.Copy), reads=(self.psb[bt],), writes=(ptb,))
                    S.add("pe", lambda e, bo=bo, r=r, g=g, pt_=pt_: e.matmul(ps[:, bo, r * 128:(r + 1) * 128], vc[:127, g, :], pt_[:127, :], start=(r == 0), stop=False, skip_group_check=True),
                          reads=(ptb, b_vc), writes=(self.psb[bo],))
                for r in range(4):
                    hb = g * 4 + r
                    S.add("act", lambda e, hb=hb, r=r, bo=bo, qsl=qsl: e.activation(out=ocmp[:, hb, qsl], in_=ps[:, bo, r * 128:(r + 1) * 128], func=AF.Copy),
                          reads=(self.psb[bo],), writes=(b_oc[hb],))
                pl, plb, _ = psl.next()
                S.add("dve", lambda e, pl=pl, im=im: e.tensor_scalar(out=pl[:, 0:32], in0=im[:, 0:128:4], scalar1=1.0, scalar2=None, op0=ALU.mult), reads=(imb,), writes=(plb,))
                for o, wgt in ((1, 2.0), (2, 2.0), (3, 2.0), (4, 1.0)):
                    S.add("dve", lambda e, pl=pl, im=im, o=o, wgt=wgt: e.scalar_tensor_tensor(out=pl[:, 0:32], in0=im[:, o:o + 128:4], scalar=wgt, in1=pl[:, 0:32], op0=ALU.mult, op1=ALU.add),
                          reads=(imb, plb), writes=(plb,))
                S.add("dve", lambda e, pl=pl, qt=qt: e.tensor_tensor(out=pl[:, 0:32], in0=pl[:, 0:32], in1=selc[:, qt, 0:32], op=ALU.mult), reads=(plb, b_selc), writes=(plb,))
                S.add("dve", lambda e, pl=pl, qt=qt: e.tensor_tensor(out=pl[:, 0:32], in0=pl[:, 0:32], in1=selc[:, qt, 32:64], op=ALU.add), reads=(plb, b_selc), writes=(plb,))
                S.add("dve", lambda e, pl=pl, qt=qt: e.tensor_tensor(out=pl[:, 0:32], in0=pl[:, 0:32], in1=selc[:, qt, 64:96], op=ALU.max), reads=(plb, b_selc), writes=(plb,))
                m, mb, _ = m8.next()
                S.add("dve", lambda e, m=m, pl=pl: e.max(out=m[:], in_=pl[:, 0:32]), reads=(plb,), writes=(mb,))
                S.add("dve", lambda e, m=m, pl=pl: e.tensor_scalar(out=pl[:, 32:64], in0=pl[:, 0:32], scalar1=m[:, 7:8], scalar2=None, op0=ALU.is_ge), reads=(plb, mb), writes=(plb,))
                bt = self.bank()
                S.add("pe", lambda e, bt=bt, pl=pl: e.transpose(ps[:32, bt, :128], pl[:, 32:64], ident), reads=(plb, self.b_cst), writes=(self.psb[bt],))
                S.add("act", lambda e, bt=bt, g=g, qsl=qsl: e.activation(out=bmT[:, g, qsl], in_=ps[:32, bt, :128], func=AF.Copy), reads=(self.psb[bt],), writes=(b_bm[g],))
        rot = {"k": Rot(S, 2, [128, L], BF16, "bk", dma=True), "v": Rot(S, 2, [128, NT, 128], BF16, "bv", dma=True)}
        mTr = Rot(S, 2, [128, NT, 512], BF16, "bmaskT")
        osl = Rot(S, 2, [128, 512], F32, "osl")
        owi = Rot(S, 2, [128, 512], F32, "owi")
        gbc = Rot(S, 2, [128, 3, 512], F32, "gbc", dma=True)
        ob = Rot(S, 3, [128, 512], BF16, "obb", dma=True)
        for g in range(2):
            ks, ksb, ksd = rot["k"].next()
            vs, vsb, vsd = rot["v"].next()
            kw, kwb, kwd = rot["k"].next()
            vw, vwb, vwd = rot["v"].next()
            self.dma(ks[:], self.pfm[36 + g], (), (ksb,), ksd)
            self.dma(kw[:], self.pfm[38 + g], (), (kwb,), kwd)
            self.dma(vs[:], self.ptm[:, 1536 + g * 128:1536 + (g + 1) * 128].rearrange("(kt p) d -> p kt d", p=128), (), (vsb,), vsd)
            self.dma(vw[:], self.ptm[:, 1792 + g * 128:1792 + (g + 1) * 128].rearrange("(kt p) d -> p kt d", p=128), (), (vwb,), vwd)
            for qg in range(4):
                mT, mTb, _ = mTr.next()
                for kt in range(4 * qg + 4):
                    bk = self.bank()
                    S.add("pe", lambda e, bk=bk, kt=kt, g=g, qg=qg: e.matmul(ps[:, bk, :], selE[:, kt * 128:(kt + 1) * 128], bmT[:, g, qg * 512:(qg + 1) * 512], start=True, stop=True),
                          reads=(b_selE, b_bm[g]), writes=(self.psb[bk],))
                    if kt % 2 == 0:
                        S.add("act", lambda e, mT=mT, bk=bk, kt=kt: e.activation(out=mT[:, kt, :], in_=ps[:, bk, :], func=AF.Copy), reads=(self.psb[bk],), writes=(mTb,))
                    else:
                        S.add("dve", lambda e, mT=mT, bk=bk, kt=kt: e.tensor_copy(out=mT[:, kt, :], in_=ps[:, bk, :]), reads=(self.psb[bk],), writes=(mTb,))
                for r in range(4):
                    hb = g * 4 + r
                    slope = SLOPES[B_SLOPE_IDX[hb]]
                    q = qall[:, hb, :]
                    o_s, osb, _ = osl.next()
                    o_w, owb, _ = owi.next()
                    units = self.causal_units(qg, ks, q, vs, self.alibi_bias(slope), (b_q, ksb), (vsb,),
                                              full_fn=lambda kt, c0, nq, mT=mT, mTb=mTb: (mT[:, kt, c0:c0 + nq], mTb))
                    self.attn_job(units, self.evac_norm(o_s, osb, 0), self.ptile, self.tmpt)
                    units = self.causal_units(qg, kw, q, vw, self.alibi_bias(slope), (b_q, kwb), (vwb,), n_prev=4)
                    self.attn_job(units, self.evac_norm(o_w, owb, 0), self.ptile, self.tmpt)
                    gb_t, gbb, gbd = gbc.next()
                    for br in range(3):
                        self.dma(gb_t[:, br, :], self.bgT[br * 8 + hb:br * 8 + hb + 1, qg * 512:(qg + 1) * 512].to_broadcast([128, 512]), (), (gbb,), gbd)
                    o, obb_, ods = ob.next()
                    sl = slice(qg * 512, (qg + 1) * 512)
                    S.add("dve", lambda e, o_s=o_s, gb_t=gb_t: e.tensor_tensor(out=o_s[:], in0=o_s[:], in1=gb_t[:, 1, :], op=ALU.mult), reads=(osb, gbb), writes=(osb,))
                    S.add("pool", lambda e, o_w=o_w, gb_t=gb_t: e.tensor_tensor(out=o_w[:], in0=o_w[:], in1=gb_t[:, 2, :], op=ALU.mult), reads=(owb, gbb), writes=(owb,))
                    S.add("dve", lambda e, o_s=o_s, o_w=o_w: e.tensor_tensor(out=o_s[:], in0=o_s[:], in1=o_w[:], op=ALU.add), reads=(osb, owb), writes=(osb,))
                    S.add("pool", lambda e, o_w=o_w, gb_t=gb_t, hb=hb, sl=sl: e.tensor_tensor(out=o_w[:], in0=ocmp[:, hb, sl], in1=gb_t[:, 0, :], op=ALU.mult),
                          reads=(b_oc[hb], gbb, owb), writes=(owb,))
                    S.add("dve", lambda e, o=o, o_s=o_s, o_w=o_w: e.tensor_tensor(out=o[:], in0=o_s[:], in1=o_w[:], op=ALU.add), reads=(osb, owb), writes=(obb_,))
                    self.dma(self.obT[4 + hb, :, sl], o[:], (obb_,), (), ods)
        self.barrier()
        S.sb_reset(mark)

    def ph_merge(self, l):
        self.S.phase = "merge"
        S = self.S
        ps = self.ps
        mark = S.sb_mark()
        wb = self.w["w_branch"][l]
        S.sb_reset(self.mark_noxb)
        osb_t = S.sb([128, 28, L], BF16, "obres")
        b_ob = [S.buf("ob") for _ in range(28)]
        dso = S.dsem()
        for b in range(28):
            self.dma(osb_t[:, b, :], self.obT[b], (), (b_ob[b],), dso)
        wsl = Rot(S, 2, [128, 28, 128], BF16, "wbr", sw=True)
        gt = Rot(S, 2, [128, 4, L], BF16, "gin", dma=True)
        acc = Rot(S, 2, [128, 512], F32, "macc")
        tmp = Rot(S, 2, [128, 512], F32, "mtmp")
        mo = Rot(S, 2, [128, L], BF16, "mout", dma=True)
        br_chunks = ((0, 4), (4, 8), (12, 8), (20, 8))
        for m in range(KC):
            tl, bf, ds = wsl.next()
            self.dma(tl[:], wb[:, m * 128:(m + 1) * 128].rearrange("(kc p) n -> p kc n", p=128), (), (bf,), ds, q="pool")
            g_t, gb, gds = gt.next()
            self.dma(g_t[:], self.gatesT.rearrange("(br f) t -> f br t", br=4)[m * 128:(m + 1) * 128], (), (gb,), gds)
            o, ob, ods = mo.next()
            for t in range(4):
                sl = slice(t * 512, (t + 1) * 512)
                a, ab, _ = acc.next()
                for bi, (c0, nc_) in enumerate(br_chunks):
                    bk = self.bank()

                    def mm(e, tl=tl, bk=bk, c0=c0, nc_=nc_, sl=sl):
                        for k in range(nc_):
                            ins = e.matmul(ps[:, bk, :], tl[:, c0 + k, :], osb_t[:, c0 + k, sl], start=(k == 0), stop=(k == nc_ - 1))
                        return ins
                    S.add("pe", mm, reads=[bf] + b_ob[c0:c0 + nc_], writes=(self.psb[bk],))
                    if bi == 0:
                        S.add("dve", lambda e, a=a, bk=bk, g_t=g_t, sl=sl: e.tensor_tensor(out=a[:], in0=ps[:, bk, :], in1=g_t[:, 0, sl], op=ALU.mult),
                              reads=(self.psb[bk], gb), writes=(ab,))
                    else:
                        tm, tmb, _ = tmp.next()
                        S.add("dve", lambda e, tm=tm, bk=bk, g_t=g_t, sl=sl, bi=bi: e.tensor_tensor(out=tm[:], in0=ps[:, bk, :], in1=g_t[:, bi, sl], op=ALU.mult),
                              reads=(self.psb[bk], gb), writes=(tmb,))
                        if bi < 3:
                            S.add("pool", lambda e, a=a, tm=tm: e.tensor_tensor(out=a[:], in0=a[:], in1=tm[:], op=ALU.add), reads=(ab, tmb), writes=(ab,))
                        else:
                            S.add("pool", lambda e, a=a, tm=tm, o=o, sl=sl: e.tensor_tensor(out=o[:, sl], in0=a[:], in1=tm[:], op=ALU.add), reads=(ab, tmb), writes=(ob,))
            self.dma(self.mergedT[m * 128:(m + 1) * 128, :], o[:], (ob,), (), ods)
        self.barrier()
        S.sb_reset(mark)


def host_selc():
    c = np.zeros((128, 16, 96), np.float32)
    p = np.arange(128)[:, None]
    j = np.arange(32)[None, :]
    for qt in range(16):
        cur = 2 * qt + (p >= 64)
        forced = (j == 0) | (j == cur) | (j == cur - 1)
        valid = (j <= cur)
        c[:, qt, 0:32] = valid
        c[:, qt, 32:64] = (valid - 1.0) * 1e9
        c[:, qt, 64:96] = np.where(forced, 1e9, -2e9)
    return c.reshape(128, 16 * 96)


N_CORES_USED = 8


def make_inputs(prog, inputs, xs):
    m = {}
    for nm in prog.declared:
        if nm == "x":
            m[nm] = np.ascontiguousarray(xs, dtype=np.float32)
        elif nm == "consts":
            m[nm] = host_consts()
        elif nm == "sel":
            m[nm] = host_sel()
        elif nm == "selc":
            m[nm] = host_selc()
        elif nm in ("ln_g", "ln_b"):
            m[nm] = np.ascontiguousarray(inputs[nm], dtype=np.float32).reshape(DEPTH * 3, D)
        elif nm in ("cmp_w1", "cmp_w2", "cmp_pos"):
            a = np.asarray(inputs[nm], dtype=np.float32)
            m[nm] = np.ascontiguousarray(a.reshape((DEPTH * 2,) + a.shape[2:]))
        else:
            m[nm] = np.ascontiguousarray(inputs[nm], dtype=np.float32)
    return m


def kernel(**inputs):
    R = N_CORES_USED
    spc = 8 // R
    prog = Prog(spc)
    x = np.asarray(inputs["x"], dtype=np.float32)
    in_maps = [make_inputs(prog, inputs, x[c * spc:(c + 1) * spc]) for c in range(R)]
    res = run_bass_kernel_spmd(prog.nc, in_maps, core_ids=list(range(R)))
    return np.concatenate([np.asarray(res.results[c]["out"]) for c in range(R)], axis=0).astype(np.float32)
```
